# Optimizing a Trainium2 kernel written in Bass

```python
import jax, jax.numpy as jnp
from jax import lax
import numpy as np

D_MODEL = 1024
BATCH = 8
SEQ = 8192
DEPTH = 2

GRID_W = 64
CTX_LEN = 256
N_MIXERS = 2
NORM_EPS = 1e-6
N_HEADS = 16
N_KV_HEADS = 4
HEAD_DIM = D_MODEL // N_HEADS
KV_GROUP = N_HEADS // N_KV_HEADS
Q_BLOCK = 128
ROPE_THETA = 10000.0
ROPE_PAIRS = HEAD_DIM // 4
RWKV_HEAD = 64
RWKV_HEADS = D_MODEL // RWKV_HEAD
DECAY_LORA = 64
ICLR_LORA = 64
GATE_LORA = 128
GN_EPS = 64e-5
D_FF = -(-8 * D_MODEL // (3 * 256)) * 256

kernel_name = "hybrid_gqa_rwkv7_prefix_dit"


def _rmsnorm(x, g):
    xf = x.astype(jnp.float32)
    y = xf * lax.rsqrt(jnp.mean(xf * xf, axis=-1, keepdims=True) + NORM_EPS)
    return (y * g.astype(jnp.float32)).astype(x.dtype)


def _axial_rope_tables(rows, dtype):
    L = rows * GRID_W
    row = jnp.repeat(jnp.arange(rows, dtype=jnp.float32), GRID_W, total_repeat_length=L)
    col = jnp.tile(jnp.arange(GRID_W, dtype=jnp.float32), rows)
    inv_freq = ROPE_THETA ** (-jnp.arange(ROPE_PAIRS, dtype=jnp.float32) / ROPE_PAIRS)
    ang = jnp.stack([row[:, None] * inv_freq, col[:, None] * inv_freq], axis=1)
    return jnp.cos(ang).astype(dtype), jnp.sin(ang).astype(dtype)


def _apply_axial_rope(x, cos, sin):
    xs = x.reshape(x.shape[:-1] + (2, 2, ROPE_PAIRS))
    x1, x2 = xs[..., 0, :], xs[..., 1, :]
    cb, sb = cos[None, :, None], sin[None, :, None]
    out = jnp.stack([x1 * cb - x2 * sb, x1 * sb + x2 * cb], axis=-2)
    return out.reshape(x.shape)


def _gqa_softmax(q, k, v):
    s = jnp.einsum('bqkgd,bskd->bkgqs', q, k).astype(jnp.float32) * (HEAD_DIM ** -0.5)
    p = jax.nn.softmax(s, axis=-1).astype(v.dtype)
    return jnp.einsum('bkgqs,bskd->bqkgd', p, v)


def _attention_mixer(h, hc, wqkv, q_gain, k_gain, wo, cos, sin, need_ctx):
    B, L, _ = h.shape
    C = hc.shape[1]
    nq = N_HEADS * HEAD_DIM
    nk = N_KV_HEADS * HEAD_DIM
    qkv = h @ wqkv
    q = qkv[..., :nq].reshape(B, L, N_HEADS, HEAD_DIM)
    k = qkv[..., nq:nq + nk].reshape(B, L, N_KV_HEADS, HEAD_DIM)
    v = qkv[..., nq + nk:].reshape(B, L, N_KV_HEADS, HEAD_DIM)
    q = _apply_axial_rope(_rmsnorm(q, q_gain), cos, sin)
    k = _apply_axial_rope(_rmsnorm(k, k_gain), cos, sin)
    qkv_c = hc @ (wqkv if need_ctx else wqkv[:, nq:])
    kv_c = qkv_c[..., -2 * nk:]
    k_c = _rmsnorm(kv_c[..., :nk].reshape(B, C, N_KV_HEADS, HEAD_DIM), k_gain)
    v_c = kv_c[..., nk:].reshape(B, C, N_KV_HEADS, HEAD_DIM)
    k_all = jnp.concatenate([k_c, k], axis=1)
    v_all = jnp.concatenate([v_c, v], axis=1)
    nb = L // Q_BLOCK
    q_blocks = q.reshape(B, nb, Q_BLOCK, N_KV_HEADS, KV_GROUP, HEAD_DIM).transpose(1, 0, 2, 3, 4, 5)
    o = lax.map(lambda qb: _gqa_softmax(qb, k_all, v_all), q_blocks)
    o = o.transpose(1, 0, 2, 3, 4, 5).reshape(B, L, nq)
    y = o @ wo
    yc = None
    if need_ctx:
        q_c = _rmsnorm(qkv_c[..., :nq].reshape(B, C, N_KV_HEADS, KV_GROUP, HEAD_DIM), q_gain)
        yc = _gqa_softmax(q_c, k_c, v_c).reshape(B, C, nq) @ wo
    return y, yc


def _centred_shift_delta(u):
    pad = jnp.pad(u, ((0, 0), (1, 1), (0, 0)))
    return 0.5 * (pad[:, :-2] + pad[:, 2:]) - u


def _rwkv_branches(u, mix, w_rkv, w0, w1, w2, a0, a1, a2, g1, g2, k_k, k_a, readout):
    B, T, _ = u.shape
    xx = _centred_shift_delta(u)

    def lerp(j):
        return u + xx * mix[j]

    sel = (0, 2, 3) if readout else (2, 3)
    proj = jnp.einsum('jbtd,jde->jbte', jnp.stack([lerp(s) for s in sel]), w_rkv[-len(sel):])
    k = proj[-2].astype(jnp.float32)

    def heads(z):
        return z.reshape(B, T, RWKV_HEADS, RWKV_HEAD).astype(jnp.float32)

    kk = heads(k * k_k)
    kk = kk * lax.rsqrt(jnp.maximum(jnp.sum(kk * kk, axis=-1, keepdims=True), 1e-24))
    xw, xa = lerp(1), lerp(4)
    dirs = []
    for d in range(2):
        w_log = -jax.nn.softplus(-(w0[d] + jnp.tanh(xw @ w1[d]) @ w2[d]).astype(jnp.float32)) - 0.5
        decay = jnp.exp(-jnp.exp(w_log))
        a = jax.nn.sigmoid((a0[d] + (xa @ a1[d]) @ a2[d]).astype(jnp.float32))
        k_d = k * (1.0 + (a - 1.0) * k_a)
        dirs.append((heads(decay), heads(k_d), heads(a)))
    r = heads(proj[0]) if readout else None
    g = (jax.nn.sigmoid(lerp(5) @ g1) @ g2) if readout else None
    return r, heads(proj[-1]), kk, dirs, g


def _wkv_scan(S0, decay, k, v, kk, a, r, reverse):
    xs = (decay, k, v, kk, a) + (() if r is None else (r,))

    def step(S, inp):
        w_t, k_t, v_t, kk_t, a_t = inp[:5]
        sa = jnp.einsum('bhvk,bhk->bhv', S, -kk_t)
        S = (S * w_t[:, :, None, :] + sa[..., None] * (kk_t * a_t)[:, :, None, :]
             + v_t[..., None] * k_t[:, :, None, :])
        y = jnp.einsum('bhvk,bhk->bhv', S, inp[5]) if len(inp) == 6 else None
        return S, y

    S, ys = lax.scan(step, S0, tuple(jnp.moveaxis(z, 1, 0) for z in xs), reverse=reverse)
    return S, (None if ys is None else jnp.moveaxis(ys, 0, 1))


def _rwkv_readout(wkv, r, v, k_dirs, g, r_k, ln_g, ln_b, wo, dtype):
    B, T, H, N = wkv.shape
    mu = jnp.mean(wkv, axis=-1, keepdims=True)
    var = jnp.mean(jnp.square(wkv - mu), axis=-1, keepdims=True)
    gn = ((wkv - mu) * lax.rsqrt(var + GN_EPS)).reshape(B, T, H * N) * ln_g + ln_b
    bonus = sum(jnp.sum(r * k_d * r_k, axis=-1, keepdims=True) for k_d in k_dirs) * v
    out = (gn + bonus.reshape(B, T, H * N)) * g
    return out.astype(dtype) @ wo


def _rwkv7_mixer(h, hc, mix, w_rkv, w0, w1, w2, a0, a1, a2, g1, g2, k_k, k_a, r_k,
                 ln_g, ln_b, wo, need_ctx):
    params = (mix, w_rkv, w0, w1, w2, a0, a1, a2, g1, g2, k_k, k_a)
    r, v, kk, dirs, g = _rwkv_branches(h, *params, readout=True)
    rc, vc, kkc, dirs_c, gc = _rwkv_branches(hc, *params, readout=need_ctx)
    B = h.shape[0]
    S0 = jnp.zeros((B, RWKV_HEADS, RWKV_HEAD, RWKV_HEAD), jnp.float32)
    wkv = 0.0
    wkv_c = 0.0
    for d, rev in enumerate((False, True)):
        decay_c, k_c, a_c = dirs_c[d]
        S_ctx, yc = _wkv_scan(S0, decay_c, k_c, vc, kkc, a_c, rc, rev)
        decay, k_d, a_d = dirs[d]
        _, y = _wkv_scan(S_ctx, decay, k_d, v, kk, a_d, r, rev)
        wkv = wkv + y
        if need_ctx:
            wkv_c = wkv_c + yc
    y = _rwkv_readout(wkv, r, v, [dd[1] for dd in dirs], g, r_k, ln_g, ln_b, wo, h.dtype)
    yc = None
    if need_ctx:
        yc = _rwkv_readout(wkv_c, rc, vc, [dd[1] for dd in dirs_c], gc, r_k, ln_g, ln_b, wo, hc.dtype)
    return y, yc


def _swiglu(h, wg, wu, wd):
    return (jax.nn.silu(h @ wg) * (h @ wu)) @ wd


def setup_inputs(seed: int = 0) -> dict:
    key = jax.random.key(seed)
    ks = iter(jax.random.split(key, 40))
    f32 = jnp.float32
    n_attn = len(range(0, DEPTH, N_MIXERS))
    n_rwkv = len(range(1, DEPTH, N_MIXERS))
    D = D_MODEL

    def nrm(shape, std):
        return jax.random.normal(next(ks), shape, f32) * std

    def unif(shape, lo, hi):
        return jax.random.uniform(next(ks), shape, f32, lo, hi)

    nqkv = (N_HEADS + 2 * N_KV_HEADS) * HEAD_DIM
    return {
        "x": nrm((BATCH, SEQ, D), 1.0),
        "c": nrm((BATCH, D), 1.0),
        "ctx": nrm((BATCH, CTX_LEN, D), 1.0),
        "c_ctx": nrm((D,), 1.0),
        "mod_w": nrm((DEPTH, D, 6 * D), 0.3 * D ** -0.5),
        "mod_b": nrm((DEPTH, 6 * D), 0.02),
        "norm1_g": 1.0 + nrm((DEPTH, D), 0.02),
        "norm2_g": 1.0 + nrm((DEPTH, D), 0.02),
        "ffn_wg": nrm((DEPTH, D, D_FF), D ** -0.5),
        "ffn_wu": nrm((DEPTH, D, D_FF), D ** -0.5),
        "ffn_wd": nrm((DEPTH, D_FF, D), D_FF ** -0.5),
        "attn_wqkv": nrm((n_attn, D, nqkv), D ** -0.5),
        "attn_q_gain": 1.0 + nrm((n_attn, HEAD_DIM), 0.02),
        "attn_k_gain": 1.0 + nrm((n_attn, HEAD_DIM), 0.02),
        "attn_wo": nrm((n_attn, N_HEADS * HEAD_DIM, D), (N_HEADS * HEAD_DIM) ** -0.5),
        "rwkv_mix": unif((n_rwkv, 6, D), 0.0, 1.0),
        "rwkv_wrkv": nrm((n_rwkv, 3, D, D), D ** -0.5),
        "rwkv_w0": unif((n_rwkv, 2, D), -6.0, -1.0),
        "rwkv_w1": nrm((n_rwkv, 2, D, DECAY_LORA), D ** -0.5),
        "rwkv_w2": nrm((n_rwkv, 2, DECAY_LORA, D), 0.5 * DECAY_LORA ** -0.5),
        "rwkv_a0": nrm((n_rwkv, 2, D), 0.1),
        "rwkv_a1": nrm((n_rwkv, 2, D, ICLR_LORA), D ** -0.5),
        "rwkv_a2": nrm((n_rwkv, 2, ICLR_LORA, D), 0.5 * ICLR_LORA ** -0.5),
        "rwkv_g1": nrm((n_rwkv, D, GATE_LORA), D ** -0.5),
        "rwkv_g2": nrm((n_rwkv, GATE_LORA, D), GATE_LORA ** -0.5),
        "rwkv_k_k": 0.85 + nrm((n_rwkv, D), 0.05),
        "rwkv_k_a": 1.0 + nrm((n_rwkv, D), 0.05),
        "rwkv_r_k": nrm((n_rwkv, RWKV_HEADS, RWKV_HEAD), 0.1),
        "rwkv_ln_g": 1.0 + nrm((n_rwkv, D), 0.02),
        "rwkv_ln_b": nrm((n_rwkv, D), 0.02),
        "rwkv_wo": nrm((n_rwkv, D, D), D ** -0.5),
        "final_g": 1.0 + nrm((D,), 0.02),
    }


def reference(x, c, ctx, c_ctx, mod_w, mod_b, norm1_g, norm2_g, ffn_wg, ffn_wu, ffn_wd,
              attn_wqkv, attn_q_gain, attn_k_gain, attn_wo,
              rwkv_mix, rwkv_wrkv, rwkv_w0, rwkv_w1, rwkv_w2, rwkv_a0, rwkv_a1, rwkv_a2,
              rwkv_g1, rwkv_g2, rwkv_k_k, rwkv_k_a, rwkv_r_k, rwkv_ln_g, rwkv_ln_b, rwkv_wo,
              final_g):
    B, L, D = x.shape
    rows = L // GRID_W
    cos, sin = _axial_rope_tables(rows, x.dtype)
    sc = jax.nn.silu(c)
    scc = jax.nn.silu(c_ctx)
    xc = ctx
    for i in range(DEPTH):
        last = i == DEPTH - 1
        j = i // N_MIXERS
        m = (sc @ mod_w[i] + mod_b[i]).reshape(B, 6, 1, D)
        mc = (scc @ mod_w[i] + mod_b[i]).reshape(6, D)
        h = _rmsnorm(x, norm1_g[i]) * (1.0 + m[:, 1]) + m[:, 0]
        hc = _rmsnorm(xc, norm1_g[i]) * (1.0 + mc[1]) + mc[0]
        if i % N_MIXERS == 0:
            y, yc = _attention_mixer(h, hc, attn_wqkv[j], attn_q_gain[j], attn_k_gain[j],
                                     attn_wo[j], cos, sin, not last)
        else:
            y, yc = _rwkv7_mixer(h, hc, rwkv_mix[j], rwkv_wrkv[j], rwkv_w0[j], rwkv_w1[j],
                                 rwkv_w2[j], rwkv_a0[j], rwkv_a1[j], rwkv_a2[j], rwkv_g1[j],
                                 rwkv_g2[j], rwkv_k_k[j], rwkv_k_a[j], rwkv_r_k[j],
                                 rwkv_ln_g[j], rwkv_ln_b[j], rwkv_wo[j], not last)
        x = x + m[:, 2] * y
        h2 = _rmsnorm(x, norm2_g[i]) * (1.0 + m[:, 4]) + m[:, 3]
        x = x + m[:, 5] * _swiglu(h2, ffn_wg[i], ffn_wu[i], ffn_wd[i])
        if not last:
            xc = xc + mc[2] * yc
            hc2 = _rmsnorm(xc, norm2_g[i]) * (1.0 + mc[4]) + mc[3]
            xc = xc + mc[5] * _swiglu(hc2, ffn_wg[i], ffn_wu[i], ffn_wd[i])
    return _rmsnorm(x, final_g)
```

```python
import contextlib
import os
import numpy as np
import concourse.bass as bass
import concourse.mybir as mybir
from concourse.bass_utils import run_bass_kernel_spmd

F32 = mybir.dt.float32
BF16 = mybir.dt.bfloat16
AF = mybir.ActivationFunctionType
ALU = mybir.AluOpType

D = 1024
NCH = 8
CL = 256
DFF = 2816
NF = 22
EPS = 1e-6
GN_EPS = 64e-5


class Buf:
    __slots__ = ("name", "w", "r", "dsem", "dcnt")

    def __init__(self, name):
        self.name = name
        self.w = None
        self.r = {}
        self.dsem = None
        self.dcnt = 0


class Eng:
    def __init__(self, key, eng, sem):
        self.key = key
        self.eng = eng
        self.sem = sem
        self.cnt = 0
        self.waited = {}


class K:
    def __init__(self, nc, stack):
        self.nc = nc
        self.stack = stack
        self.pe = Eng("pe", nc.tensor, self._sem("c_pe"))
        self.act = Eng("act", nc.scalar, self._sem("c_act"))
        self.dve = Eng("dve", nc.vector, self._sem("c_dve"))
        self.pool = Eng("pool", nc.gpsimd, self._sem("c_pool"))
        self.sp = Eng("sp", nc.sync, self._sem("c_sp"))
        self.n_inst = 0
        self.n_wait = 0
        self.dma_bufs = []

    def _sem(self, name):
        return self.stack.enter_context(self.nc.semaphore(name))

    def _need(self, e, sem, val):
        if val > e.waited.get(id(sem), 0):
            e.eng.wait_ge(sem, val)
            e.waited[id(sem)] = val
            self.n_wait += 1

    def _pre(self, e, reads, writes, is_dma=False):
        for b in reads:
            if b.w is not None:
                sem, val, key = b.w
                self._need(e, sem, val)
        for b in writes:
            if b.w is not None:
                sem, val, key = b.w
                if is_dma or key != e.key:
                    self._need(e, sem, val)
            for key, (sem, val) in b.r.items():
                if is_dma or key != e.key:
                    self._need(e, sem, val)

    def op(self, e, fn, reads=(), writes=(), inc=True):
        self._pre(e, reads, writes)
        ins = fn()
        self.n_inst += 1
        if inc:
            ins.then_inc(e.sem, 1)
            e.cnt += 1
            val = e.cnt
        else:
            val = e.cnt + 1
        for b in reads:
            b.r[e.key] = (e.sem, val)
        for b in writes:
            b.w = (e.sem, val, e.key)
            b.r = {}
        return ins

    def dma(self, q, out_ap, in_ap, sb, reads=(), writes=(), **kw):
        if sb.dsem is None:
            sb.dsem = self._sem("d_" + sb.name)
            self.dma_bufs.append(sb)
        self._pre(q, reads, writes, is_dma=True)
        ins = q.eng.dma_start(out=out_ap, in_=in_ap, **kw)
        ins.then_inc(sb.dsem, 16)
        sb.dcnt += 16
        self.n_inst += 1
        key = "dma_" + sb.name
        for b in reads:
            b.r[key] = (sb.dsem, sb.dcnt)
        for b in writes:
            b.w = (sb.dsem, sb.dcnt, key)
            b.r = {}
        return ins

    def finish(self):
        for sb in self.dma_bufs:
            self._need(self.sp, sb.dsem, sb.dcnt)

    def barrier(self):
        engs = [self.pe, self.act, self.dve, self.pool, self.sp]
        for e in engs:
            for o in engs:
                if o is not e and o.cnt > 0:
                    self._need(e, o.sem, o.cnt)
            for sb in self.dma_bufs:
                if sb.dcnt > 0:
                    self._need(e, sb.dsem, sb.dcnt)


class T:
    def __init__(self, k, stack, name, shape, dt, psum=False):
        nc = k.nc
        if psum:
            self.t = stack.enter_context(nc.psum_tensor("t_" + name, list(shape), dt))
        else:
            self.t = stack.enter_context(nc.sbuf_tensor("t_" + name, list(shape), dt))
        self.b = Buf(name)

    def __getitem__(self, key):
        return self.t[key]


def _perm_d():
    i = np.arange(64)
    return ((i % 32) // 16) * 32 + (i // 32) * 16 + (i % 16)


HEAD_OF = [[0, 1, 2, 3, 8, 9, 10, 11], [4, 5, 6, 7, 12, 13, 14, 15]]


def _fm(v):
    v = np.asarray(v, np.float32).reshape(-1, 128)
    return np.ascontiguousarray(v.T)


class VecPack:
    def __init__(self):
        self.cols = []
        self.off = {}
        self.n = 0

    def add(self, name, arr):
        arr = np.asarray(arr, np.float32)
        assert arr.shape[0] == 128
        self.off[name] = (self.n, arr.shape[1])
        self.cols.append(arr)
        self.n += arr.shape[1]

    def build(self):
        return np.ascontiguousarray(np.concatenate(self.cols, axis=1))


def vec_layout():
    names = [("c", 8), ("c_ctx", 8), ("mod_b0", 48), ("mod_b1", 48), ("n1g0", 8), ("n1g1", 8), ("n2g0", 8),
             ("n2g1", 8), ("final_g", 8), ("qg", 1), ("kg", 1), ("mix", 48), ("w0", 16), ("a0", 16), ("k_k", 8),
             ("k_a", 8), ("r_k", 8), ("ln_g", 8), ("ln_b", 8)]
    off = {}
    n = 0
    for nm, w in names:
        off[nm] = (n, w)
        n += w
    return off, n


def pack_vecs(inp, b):
    pd = _perm_d()
    vp = VecPack()
    vp.add("c", _fm(inp["c"][b]))
    vp.add("c_ctx", _fm(inp["c_ctx"]))
    vp.add("mod_b0", _fm(inp["mod_b"][0]))
    vp.add("mod_b1", _fm(inp["mod_b"][1]))
    vp.add("n1g0", _fm(inp["norm1_g"][0]))
    vp.add("n1g1", _fm(inp["norm1_g"][1]))
    vp.add("n2g0", _fm(inp["norm2_g"][0]))
    vp.add("n2g1", _fm(inp["norm2_g"][1]))
    vp.add("final_g", _fm(inp["final_g"]))
    vp.add("qg", np.tile(inp["attn_q_gain"][0][pd], 2).reshape(128, 1))
    vp.add("kg", np.tile(inp["attn_k_gain"][0][pd], 2).reshape(128, 1))
    vp.add("mix", np.concatenate([_fm(inp["rwkv_mix"][0][j]) for j in range(6)], axis=1))
    vp.add("w0", np.concatenate([_fm(inp["rwkv_w0"][0][d]) for d in range(2)], axis=1))
    vp.add("a0", np.concatenate([_fm(inp["rwkv_a0"][0][d]) for d in range(2)], axis=1))
    vp.add("k_k", _fm(inp["rwkv_k_k"][0]))
    vp.add("k_a", _fm(inp["rwkv_k_a"][0]))
    vp.add("r_k", _fm(inp["rwkv_r_k"][0].reshape(-1)))
    vp.add("ln_g", _fm(inp["rwkv_ln_g"][0]))
    vp.add("ln_b", _fm(inp["rwkv_ln_b"][0]))
    off, n = vec_layout()
    assert vp.off == off
    return vp.build()


def make_consts(L):
    c = {}
    c["ident"] = np.eye(128, dtype=np.float32)
    p = np.zeros((128, 128), np.float32)
    for m in range(128):
        p[m ^ 32, m] = 1.0
    c["perm32"] = p
    bo = np.zeros((128, 128), np.float32)
    bo[:64, :64] = 1.0
    bo[64:, 64:] = 1.0
    c["blockones"] = bo
    c["allones"] = np.ones((128, 128), np.float32)
    cm = np.concatenate([c["ident"], c["perm32"], c["blockones"], c["allones"]], axis=1)
    t = np.arange(L)
    row = (t // 64).astype(np.float32)
    col = (t % 64).astype(np.float32)
    inv_freq = (np.float32(10000.0) ** (-np.arange(16, dtype=np.float32) / np.float32(16))).astype(np.float32)
    ang = np.stack([row[:, None] * inv_freq, col[:, None] * inv_freq], axis=1)
    cos = np.cos(ang).astype(np.float32)
    sin = np.sin(ang).astype(np.float32)
    ct = np.zeros((128, L), np.float32)
    stb = np.zeros((128, L), np.float32)
    for r in range(128):
        i = r % 64
        a = (i % 32) // 16
        pp = i % 16
        ct[r] = cos[:, a, pp]
        stb[r] = -sin[:, a, pp] if i < 32 else sin[:, a, pp]
    return np.ascontiguousarray(cm), ct, stb


def make_masks():
    i = np.arange(64)
    us = (i[:, None] < i[None, :]).astype(np.float32)
    ui = (i[:, None] <= i[None, :]).astype(np.float32)
    m = np.concatenate([us, ui, us.T, ui.T], axis=1)
    return np.ascontiguousarray(np.concatenate([m, m], axis=0))


def host_weights(inp):
    pd = _perm_d()
    wqkv = np.asarray(inp["attn_wqkv"][0], np.float32)
    qcols = []
    for c in range(8):
        for s in range(2):
            h = HEAD_OF[s][c]
            qcols.append(h * 64 + pd)
    qcols = np.concatenate(qcols)
    kcols = np.concatenate([1024 + g * 64 + pd for g in range(4)])
    vcols = 1280 + np.arange(256)
    wqkv_p = np.ascontiguousarray(wqkv[:, np.concatenate([qcols, kcols, vcols])])
    orows = np.concatenate([HEAD_OF[s][c] * 64 + np.arange(64) for c in range(8) for s in range(2)])
    wo_p = np.ascontiguousarray(np.asarray(inp["attn_wo"][0], np.float32)[orows, :])
    return wqkv_p, wo_p


def build(L, stage=99):
    TT = CL + L
    NKT = TT // 128
    nc = bass.Bass("TRN2", target_bir_lowering=False)
    voff, NV = vec_layout()

    def din(name, shape, dt=F32):
        return nc.dram_tensor(name, list(shape), dt, kind="ExternalInput").ap()

    def dscr(name, shape, dt=F32):
        return nc.dram_tensor(name, list(shape), dt).ap()

    x_d = din("x", [L, D])
    ctx_d = din("ctx", [CL, D])
    vecs_d = din("vecs", [128, NV])
    consts_d = din("consts", [128, 512])
    ctab_d = din("ctab", [128, L])
    stab_d = din("stab", [128, L])
    modw_d = din("mod_w", [2, D, 6 * D])
    wqkv_d = din("wqkv", [D, 1536])
    wo0_d = din("wo0", [D, D])
    wg_d = din("ffn_wg", [2, D, DFF])
    wu_d = din("ffn_wu", [2, D, DFF])
    wd_d = din("ffn_wd", [2, DFF, D])
    wrkv_d = din("wrkv", [3, D, D])
    wo1_d = din("wo1", [D, D])
    w1c_d = din("w1c", [D, 128])
    a1c_d = din("a1c", [D, 128])
    g1_d = din("g1", [D, 128])
    w2c_d = din("w2c", [128, D])
    a2c_d = din("a2c", [128, D])
    g2_d = din("g2", [128, D])
    masks_d = din("masks", [128, 256])
    out_d = nc.dram_tensor("out", [L, D], F32, kind="ExternalOutput").ap()

    xT_d = dscr("xT_s", [NCH, 128, TT])
    qT_d = dscr("qT_s", [NCH, 128, TT], BF16)
    kT_d = dscr("kT_s", [2, 128, TT], BF16)
    v_d = dscr("v_s", [TT, 256], BF16)
    x1T_d = dscr("x1T_s", [NCH, 128, TT])
    x2T_d = dscr("x2T_s", [NCH, 128, TT])
    h1c_d = dscr("h1c_s", [NCH, 128, CL + 2])
    h1l_d = dscr("h1l_s", [NCH, 128, L + 2])
    yf_d = dscr("yf_s", [NCH, 128, L])
    x3T_d = dscr("x3T_s", [NCH, 128, TT])

    st = contextlib.ExitStack()
    with st:
        k = K(nc, st)

        uniq = {"n": 0}

        def tl(name, shape, dt, stack=None, psum=False):
            uniq["n"] += 1
            return T(k, stack or st, f"{name}_{uniq['n']}", shape, dt, psum=psum)

        vecs = tl("vecs", [128, NV], F32)
        k.dma(k.sp, vecs[:, :], vecs_d[:, :], vecs.b, writes=[vecs.b])
        cst = tl("cst", [128, 512], F32)
        k.dma(k.sp, cst[:, :], consts_d[:, :], cst.b, writes=[cst.b])
        cbf = tl("cbf", [128, 512], BF16)
        k.op(k.dve, lambda: nc.vector.tensor_copy(out=cbf[:, :], in_=cst[:, :]), reads=[cst.b], writes=[cbf.b])
        ident = cst[:, 0:128]
        perm_bf = cbf[:, 128:256]
        bones_bf = cbf[:, 256:384]
        ones_bf = cbf[:, 384:512]
        modv = tl("modv", [128, 2, 2, 6, 8], F32)
        qg8 = tl("qg8", [128, 2], F32)

        def V(name, i=0, n=1):
            o, w = voff[name]
            return vecs[:, o + i:o + i + n]

        k.op(k.dve, lambda: nc.vector.tensor_scalar(out=qg8[:, 0:1], in0=V("qg"), scalar1=0.125, scalar2=None,
                                                    op0=ALU.mult), reads=[vecs.b], writes=[qg8.b])
        k.op(k.dve, lambda: nc.vector.tensor_copy(out=qg8[:, 1:2], in_=V("kg")), reads=[vecs.b], writes=[qg8.b])

        psA = [tl(f"psA{i}", [128, 512], F32, psum=True) for i in range(3)]
        psB = [tl(f"psB{i}", [128, 512], F32, psum=True) for i in range(2)]
        psC = [tl(f"psC{i}", [128, 512], F32, psum=True) for i in range(3)]
        rr = {"A": 0, "B": 0, "C": 0}

        def nxt(which):
            lst = {"A": psA, "B": psB, "C": psC}[which]
            i = rr[which]
            rr[which] = (i + 1) % len(lst)
            return lst[i]

        with contextlib.ExitStack() as s0:
            sc = tl("sc", [128, 8, 2], F32, s0)
            k.op(k.act, lambda: nc.scalar.activation(out=sc[:, :, 0], in_=V("c", 0, 8), func=AF.Silu),
                 reads=[vecs.b], writes=[sc.b])
            k.op(k.act, lambda: nc.scalar.activation(out=sc[:, :, 1], in_=V("c_ctx", 0, 8), func=AF.Silu),
                 reads=[vecs.b], writes=[sc.b])
            mst = [tl(f"mst{i}", [128, 6 * D], F32, s0) for i in range(2)]
            macc = tl("macc", [128, 96], F32, s0)
            for l in range(2):
                for kc in range(8):
                    ms = mst[kc % 2]
                    k.dma(k.sp, ms[:, :], modw_d[l, kc * 128:(kc + 1) * 128, :], ms.b, writes=[ms.b])
                    pm = nxt("C")
                    for n in range(48):
                        k.op(k.pe, lambda: nc.tensor.matmul(pm[:, 2 * n:2 * n + 2], lhsT=ms[:, n * 128:(n + 1) * 128],
                                                            rhs=sc[:, kc, :], start=True, stop=True),
                             reads=[ms.b, sc.b], writes=[pm.b], inc=(n == 47))
                    if kc == 0:
                        k.op(k.dve, lambda: nc.vector.tensor_copy(out=macc[:, :], in_=pm[:, 0:96]),
                             reads=[pm.b], writes=[macc.b])
                    else:
                        k.op(k.dve, lambda: nc.vector.tensor_tensor(out=macc[:, :], in0=macc[:, :], in1=pm[:, 0:96],
                                                                    op=ALU.add),
                             reads=[pm.b, macc.b], writes=[macc.b])
                mb = V(f"mod_b{l}", 0, 48)
                for s in range(2):
                    mv = macc[:, :].rearrange("p (n s) -> p n s", s=2)[:, :, s]
                    tmp = tl(f"mtmp{l}{s}", [128, 48], F32, s0)
                    k.op(k.dve, lambda: nc.vector.tensor_tensor(out=tmp[:, :], in0=mv, in1=mb, op=ALU.add),
                         reads=[macc.b, vecs.b], writes=[tmp.b])
                    for (dst, src) in ((0, 0), (2, 2), (3, 3), (5, 5)):
                        k.op(k.dve, lambda: nc.vector.tensor_copy(out=modv[:, l, s, dst, :],
                                                                  in_=tmp[:, src * 8:(src + 1) * 8]),
                             reads=[tmp.b], writes=[modv.b])
                    for (dst, src, gname) in ((1, 1, f"n1g{l}"), (4, 4, f"n2g{l}")):
                        k.op(k.dve, lambda: nc.vector.scalar_tensor_tensor(
                            out=modv[:, l, s, dst, :], in0=tmp[:, src * 8:(src + 1) * 8], scalar=1.0,
                            in1=V(gname, 0, 8), op0=ALU.add, op1=ALU.mult),
                            reads=[tmp.b, vecs.b], writes=[modv.b])

        k.barrier()
        if os.environ.get("DBG", "") == "modv":
            k.dma(k.pool, out_d[0:128, 0:192], modv[:, :, :, :, :].rearrange("p a b c d -> p (a b c d)"), modv.b,
                  reads=[modv.b])
        def load_weight_bf16(dst, dst_view_fn, src_rows_fn, nrow_tiles, ncols, stg, engs):
            for i in range(nrow_tiles):
                sg = stg[i % len(stg)]
                k.dma(k.sp, sg[:, 0:ncols], src_rows_fn(i), sg.b, writes=[sg.b])
                e = engs[i % len(engs)]
                if e is k.act:
                    k.op(e, lambda: nc.scalar.copy(out=dst_view_fn(i), in_=sg[:, 0:ncols]), reads=[sg.b], writes=[dst.b])
                elif e is k.dve:
                    k.op(e, lambda: nc.vector.tensor_copy(out=dst_view_fn(i), in_=sg[:, 0:ncols]), reads=[sg.b],
                         writes=[dst.b])
                else:
                    k.op(e, lambda: nc.gpsimd.tensor_copy(out=dst_view_fn(i), in_=sg[:, 0:ncols]), reads=[sg.b],
                         writes=[dst.b])

        def rms_stats(xT, N, sq, rstd):
            for c in range(8):
                k.op(k.act, lambda: nc.scalar.activation(out=sq[:, c, 0:N], in_=xT[:, c, 0:N], func=AF.Square),
                     reads=[xT.b], writes=[sq.b])
            ps = nxt("C")
            for c in range(8):
                k.op(k.pe, lambda: nc.tensor.matmul(ps[:, 0:N], lhsT=ones_bf, rhs=sq[:, c, 0:N], start=(c == 0),
                                                    stop=(c == 7)),
                     reads=[sq.b, cbf.b], writes=[ps.b], inc=(c == 7))
            k.op(k.act, lambda: nc.scalar.activation(out=rstd[:, 0:N], in_=ps[:, 0:N], func=AF.Sqrt, scale=1.0 / D,
                                                     bias=EPS),
                 reads=[ps.b], writes=[rstd.b])
            k.op(k.dve, lambda: nc.vector.reciprocal(out=rstd[:, 0:N], in_=rstd[:, 0:N]), reads=[rstd.b],
                 writes=[rstd.b])

        def norm_mod(xT, N, l, s, which, hT, sq, rstd, tmp):
            rms_stats(xT, N, sq, rstd)
            jsh, jgm = (0, 1) if which == 0 else (3, 4)
            for c in range(8):
                tb = tmp[c % len(tmp)]
                k.op(k.dve, lambda: nc.vector.scalar_tensor_tensor(
                    out=tb[:, 0:N], in0=xT[:, c, 0:N], scalar=modv[:, l, s, jgm, c:c + 1], in1=rstd[:, 0:N],
                    op0=ALU.mult, op1=ALU.mult), reads=[xT.b, modv.b, rstd.b], writes=[tb.b])
                k.op(k.pool, lambda: nc.gpsimd.tensor_scalar(
                    out=hT[:, c, 0:N], in0=tb[:, 0:N], scalar1=modv[:, l, s, jsh, c:c + 1], scalar2=None,
                    op0=ALU.add), reads=[tb.b, modv.b], writes=[hT.b])

        def ffn_phase(l, src_d, dst_d, blocks, post_fn=None):
            with contextlib.ExitStack() as sp_:
                wg = tl("wg", [128, 8, DFF], BF16, sp_)
                wu = tl("wu", [128, 8, DFF], BF16, sp_)
                wd = tl("wd", [128, NF, D], BF16, sp_)
                with contextlib.ExitStack() as s2:
                    stg = [tl(f"f_stg{i}", [128, DFF], F32, s2) for i in range(2)]
                    engs = [k.dve, k.pool, k.act]
                    load_weight_bf16(wg, lambda i: wg[:, i, :], lambda i: wg_d[l, i * 128:(i + 1) * 128, :], 8, DFF,
                                     stg, engs)
                    load_weight_bf16(wu, lambda i: wu[:, i, :], lambda i: wu_d[l, i * 128:(i + 1) * 128, :], 8, DFF,
                                     stg, engs)
                    load_weight_bf16(wd, lambda i: wd[:, i, :], lambda i: wd_d[l, i * 128:(i + 1) * 128, :], NF, D,
                                     stg, engs)
                k.barrier()
                xT = tl("f_xT", [128, 8, 512], F32, sp_)
                hT = tl("f_hT", [128, 8, 512], BF16, sp_)
                aT = tl("f_aT", [128, NF, 512], BF16, sp_)
                rstd = tl("f_rstd", [128, 512], F32, sp_)
                tmp = [tl(f"f_tmp{i}", [128, 512], F32, sp_) for i in range(2)]
                sg = [tl(f"f_sg{i}", [128, 512], F32, sp_) for i in range(2)]
                for (t0, N, s) in blocks:
                    k.dma(k.sp, xT[:, :, 0:N], src_d[:, :, t0:t0 + N].rearrange("c p t -> p c t"), xT.b,
                          writes=[xT.b])
                    sq = aT
                    norm_mod(xT, N, l, s, 1, hT, T_alias(aT, "sq"), rstd, tmp)
                    for f in range(NF):
                        pg = nxt("A")
                        for c in range(8):
                            k.op(k.pe, lambda: nc.tensor.matmul(pg[:, 0:N], lhsT=wg[:, c, f * 128:(f + 1) * 128],
                                                                rhs=hT[:, c, 0:N], start=(c == 0), stop=(c == 7)),
                                 reads=[wg.b, hT.b], writes=[pg.b], inc=(c == 7))
                        pu = nxt("C")
                        for c in range(8):
                            k.op(k.pe, lambda: nc.tensor.matmul(pu[:, 0:N], lhsT=wu[:, c, f * 128:(f + 1) * 128],
                                                                rhs=hT[:, c, 0:N], start=(c == 0), stop=(c == 7)),
                                 reads=[wu.b, hT.b], writes=[pu.b], inc=(c == 7))
                        sgi = sg[f % 2]
                        k.op(k.act, lambda: nc.scalar.activation(out=sgi[:, 0:N], in_=pg[:, 0:N], func=AF.Silu),
                             reads=[pg.b], writes=[sgi.b])
                        k.op(k.dve, lambda: nc.vector.tensor_tensor(out=aT[:, f, 0:N], in0=sgi[:, 0:N],
                                                                    in1=pu[:, 0:N], op=ALU.mult),
                             reads=[sgi.b, pu.b], writes=[aT.b])
                    for n in range(8):
                        pd_ = nxt("B")
                        for f in range(NF):
                            k.op(k.pe, lambda: nc.tensor.matmul(pd_[:, 0:N], lhsT=wd[:, f, n * 128:(n + 1) * 128],
                                                                rhs=aT[:, f, 0:N], start=(f == 0), stop=(f == NF - 1)),
                                 reads=[wd.b, aT.b], writes=[pd_.b], inc=(f == NF - 1))
                        k.op(k.dve, lambda: nc.vector.scalar_tensor_tensor(
                            out=xT[:, n, 0:N], in0=pd_[:, 0:N], scalar=modv[:, l, s, 5, n:n + 1], in1=xT[:, n, 0:N],
                            op0=ALU.mult, op1=ALU.add), reads=[pd_.b, modv.b, xT.b], writes=[xT.b])
                    if dst_d is not None:
                        k.dma(k.pool, dst_d[:, :, t0:t0 + N].rearrange("c p t -> p c t"), xT[:, :, 0:N], xT.b,
                              reads=[xT.b])
                    if post_fn is not None:
                        post_fn(t0, N, s, xT, hT, T_alias(aT, "sq"), rstd, tmp, sg)

        def T_alias(t, name):
            return t


        def rwkv_phase():
            NB = 256
            C0 = float(np.exp(-0.5))
            with contextlib.ExitStack() as s1:
                Wr = tl("Wr", [128, 8, D], BF16, s1)
                Wk = tl("Wk", [128, 8, D], BF16, s1)
                Wv = tl("Wv", [128, 8, D], BF16, s1)
                wo1 = tl("wo1", [128, 8, D], BF16, s1)
                w1c = tl("w1c", [128, 8, 128], BF16, s1)
                a1c = tl("a1c", [128, 8, 128], BF16, s1)
                g1 = tl("g1", [128, 8, 128], BF16, s1)
                w2c = tl("w2c", [128, D], BF16, s1)
                a2c = tl("a2c", [128, D], BF16, s1)
                g2 = tl("g2", [128, D], BF16, s1)
                with contextlib.ExitStack() as s2:
                    stg = [tl(f"r_stg{i}", [128, D], F32, s2) for i in range(2)]
                    engs = [k.dve, k.pool, k.act]
                    for (wt, j) in ((Wr, 0), (Wk, 1), (Wv, 2)):
                        load_weight_bf16(wt, lambda i: wt[:, i, :], lambda i: wrkv_d[j, i * 128:(i + 1) * 128, :], 8, D,
                                         stg, engs)
                    load_weight_bf16(wo1, lambda i: wo1[:, i, :], lambda i: wo1_d[i * 128:(i + 1) * 128, :], 8, D, stg,
                                     engs)
                    for (wt, src) in ((w1c, w1c_d), (a1c, a1c_d), (g1, g1_d)):
                        load_weight_bf16(wt, lambda i: wt[:, i, :], lambda i: src[i * 128:(i + 1) * 128, :], 8, 128, stg,
                                         engs)
                    for (wt, src) in ((w2c, w2c_d), (a2c, a2c_d), (g2, g2_d)):
                        load_weight_bf16(wt, lambda i: wt[:, :], lambda i: src[:, :], 1, D, stg, engs)
                k.barrier()
                msk = tl("msk", [128, 256], F32, s1)
                k.dma(k.sp, msk[:, :], masks_d[:, :], msk.b, writes=[msk.b])
                rst = tl("rst", [128, NB], F32, s1)
                k.op(k.pool, lambda: nc.gpsimd.memset(rst[:, :], 1.0), writes=[rst.b])
                k.op(k.pool, lambda: nc.gpsimd.memset(rst[:, :].rearrange("p (c t) -> p c t", t=64)[:, :, 0:1], 0.0),
                     writes=[rst.b])
                omka = tl("omka", [128, 8], F32, s1)
                k.op(k.dve, lambda: nc.vector.tensor_scalar(out=omka[:, :], in0=V("k_a", 0, 8), scalar1=-1.0,
                                                            scalar2=1.0, op0=ALU.mult, op1=ALU.add),
                     reads=[vecs.b], writes=[omka.b])
                ident_bf = cbf[:, 0:128]

                U = tl("r_U", [128, 8, NB + 2], F32, s1)
                XX = tl("r_XX", [128, 8, NB], F32, s1)
                LB = tl("r_LB", [128, 8, NB], BF16, s1)
                Kt = tl("r_Kt", [128, 8, NB], F32, s1)
                Vt = tl("r_Vt", [128, 8, NB], F32, s1)
                Rt = tl("r_Rt", [128, 8, NB], F32, s1)
                OB = tl("r_OB", [128, 8, NB], BF16, s1)
                tw = tl("r_tw", [128, NB], BF16, s1)
                ta = tl("r_ta", [128, NB], BF16, s1)
                tg = tl("r_tg", [128, NB], BF16, s1)
                ST = tl("r_ST", [64, 16, 64], F32, s1)

                def f32t(nm, n=1):
                    return [tl(f"r_{nm}{i}", [128, NB], F32, s1) for i in range(n)]

                def dbl(lst):
                    return lst if len(lst) == 2 else [lst[0], lst[0]]

                sig = f32t("sig", 2); aa = f32t("aa", 2); ao = dbl(f32t("ao", 1)); kkr = dbl(f32t("kkr", 1))
                rn = dbl(f32t("rn", 1)); kkn = f32t("kkn", 2); kd = f32t("kd", 2); bet = f32t("bet", 2)
                cum = f32t("cum", 2); xe = dbl(f32t("xe", 1)); xi = dbl(f32t("xi", 1)); Ee = f32t("Ee", 2)
                ai = f32t("ai", 2); ri = f32t("ri", 2); t3 = dbl(f32t("t3", 1)); yf = f32t("yf", 2)
                gn = dbl(f32t("gn", 1)); x2b = f32t("x2b", 2)
                sqb = [tl(f"r_sqb{i}", [128, NB], BF16, s1) for i in range(2)]
                wtot = [tl(f"r_wtot{i}", [128, 4], F32, s1) for i in range(2)]
                wt0 = [tl(f"r_wt0{i}", [64, 4], F32, s1) for i in range(4)]
                AR = [tl(f"r_AR{i}", [128, 4, 128], BF16, s1) for i in range(2)]
                BK = [tl(f"r_BK{i}", [128, 4, 128], BF16, s1) for i in range(2)]
                TMa = [tl(f"r_TMa{i}", [64, 4, 128], BF16, s1) for i in range(2)]
                TMb = [tl(f"r_TMb{i}", [64, 4, 128], BF16, s1) for i in range(2)]
                TMk = [tl(f"r_TMk{i}", [64, 4, 128], BF16, s1) for i in range(2)]
                Vtm = [tl(f"r_Vtm{i}", [64, 4, 128], F32, s1) for i in range(2)]
                Xs = [tl(f"r_X{i}", [64, 4, 192], BF16, s1) for i in range(2)]
                Ws = [tl(f"r_W{i}", [64, 4, 128], BF16, s1) for i in range(2)]
                AKT = [tl(f"r_AKT{i}", [64, 4, 64], BF16, s1) for i in range(2)]
                KA = [tl(f"r_KA{i}", [64, 4, 128], F32, s1) for i in range(2)]
                M1 = [tl(f"r_M1{i}", [64, 4, 64], F32, s1) for i in range(2)]
                QT = [tl(f"r_QT{i}", [64, 4, 64], F32, s1) for i in range(2)]
                ZN = [tl(f"r_ZN{i}", [64, 4, 128], F32, s1) for i in range(2)]
                STP = [tl(f"r_STP{i}", [64, 64], F32, s1) for i in range(3)]
                RC0 = tl("r_RC0", [64, 4, 64], BF16, s1)
                cnt = {"u": 0}

                def bc4(ap2d):
                    return ap2d.unsqueeze(1).to_broadcast([ap2d.shape[0], 4, ap2d.shape[1]])

                def lerp(j, N):
                    for c in range(8):
                        e = k.dve
                        eng = nc.vector
                        k.op(e, lambda: eng.scalar_tensor_tensor(
                            out=LB[:, c, 0:N], in0=XX[:, c, 0:N], scalar=V("mix", j * 8 + c, 1), in1=U[:, c, 1:N + 1],
                            op0=ALU.mult, op1=ALU.add), reads=[XX.b, U.b, vecs.b], writes=[LB.b])

                def proj_full(Wt, dst, N):
                    for n in range(8):
                        ps = nxt("A")
                        for c in range(8):
                            k.op(k.pe, lambda: nc.tensor.matmul(ps[:, 0:N], lhsT=Wt[:, c, n * 128:(n + 1) * 128],
                                                                rhs=LB[:, c, 0:N], start=(c == 0), stop=(c == 7)),
                                 reads=[Wt.b, LB.b], writes=[ps.b], inc=(c == 7))
                        k.op(k.act, lambda: nc.scalar.copy(out=dst[:, n, 0:N], in_=ps[:, 0:N]), reads=[ps.b],
                             writes=[dst.b])

                def proj_lora(Wt, dst, N, func):
                    ps = nxt("A")
                    for c in range(8):
                        k.op(k.pe, lambda: nc.tensor.matmul(ps[:, 0:N], lhsT=Wt[:, c, :], rhs=LB[:, c, 0:N],
                                                            start=(c == 0), stop=(c == 7)),
                             reads=[Wt.b, LB.b], writes=[ps.b], inc=(c == 7))
                    k.op(k.act, lambda: nc.scalar.activation(out=dst[:, 0:N], in_=ps[:, 0:N], func=func),
                         reads=[ps.b], writes=[dst.b])

                RW_STOP = int(os.environ.get("RW_STOP", "99"))
                DBG = os.environ.get("DBG", "")

                def block(t0, N, s_, d, sweepB):
                    if RW_STOP <= 0:
                        return
                    readout = (s_ == 0)
                    src = h1c_d if s_ == 1 else h1l_d
                    r0 = t0 if s_ == 1 else t0 - CL
                    nch = N // 64
                    k.dma(k.sp, U[:, :, 0:N + 2], src[:, :, r0:r0 + N + 2].rearrange("c p t -> p c t"), U.b,
                          writes=[U.b])
                    k.op(k.dve, lambda: nc.vector.tensor_tensor(out=XX[:, :, 0:N], in0=U[:, :, 0:N], in1=U[:, :, 2:N + 2],
                                                                op=ALU.add), reads=[U.b], writes=[XX.b])
                    k.op(k.dve, lambda: nc.vector.scalar_tensor_tensor(out=XX[:, :, 0:N], in0=XX[:, :, 0:N], scalar=0.5,
                                                                       in1=U[:, :, 1:N + 1], op0=ALU.mult,
                                                                       op1=ALU.subtract), reads=[XX.b, U.b],
                         writes=[XX.b])
                    lerp(2, N); proj_full(Wk, Kt, N)
                    lerp(3, N); proj_full(Wv, Vt, N)
                    if readout:
                        lerp(0, N); proj_full(Wr, Rt, N)
                    else:
                        k.op(k.pool, lambda: nc.gpsimd.memset(Rt[:, :, :], 0.0), writes=[Rt.b])
                    lerp(1, N); proj_lora(w1c, tw, N, AF.Tanh)
                    lerp(4, N); proj_lora(a1c, ta, N, AF.Copy)
                    if sweepB and readout:
                        lerp(5, N); proj_lora(g1, tg, N, AF.Sigmoid)
                    if RW_STOP <= 1:
                        return
                    if d == 0:
                        m_lt, m_le, m_ltT = msk[:, 0:64], msk[:, 64:128], msk[:, 128:192]
                    else:
                        m_lt, m_le, m_ltT = msk[:, 128:192], msk[:, 192:256], msk[:, 0:64]
                    db = slice(d * 64, (d + 1) * 64)
                    ob_ = slice((1 - d) * 64, (2 - d) * 64)
                    for hp in range(8):
                        i2 = hp % 2
                        hs = slice(hp * 128, (hp + 1) * 128)
                        pz = nxt("A")
                        k.op(k.pe, lambda: nc.tensor.matmul(pz[:, 0:N], lhsT=w2c[db, hs], rhs=tw[db, 0:N], start=True,
                                                            stop=True), reads=[w2c.b, tw.b], writes=[pz.b])
                        k.op(k.act, lambda: nc.scalar.activation(out=sig[i2][:, 0:N], in_=pz[:, 0:N], func=AF.Sigmoid,
                                                                 bias=V("w0", d * 8 + hp, 1)), reads=[pz.b, vecs.b],
                             writes=[sig[i2].b])
                        pa = nxt("A")
                        k.op(k.pe, lambda: nc.tensor.matmul(pa[:, 0:N], lhsT=a2c[db, hs], rhs=ta[db, 0:N], start=True,
                                                            stop=True), reads=[a2c.b, ta.b], writes=[pa.b])
                        k.op(k.act, lambda: nc.scalar.activation(out=aa[i2][:, 0:N], in_=pa[:, 0:N], func=AF.Sigmoid,
                                                                 bias=V("a0", d * 8 + hp, 1)), reads=[pa.b, vecs.b],
                             writes=[aa[i2].b])
                        k.op(k.dve, lambda: nc.vector.tensor_scalar(out=kkr[i2][:, 0:N], in0=Kt[:, hp, 0:N],
                                                                    scalar1=V("k_k", hp, 1), scalar2=None,
                                                                    op0=ALU.mult), reads=[Kt.b, vecs.b],
                             writes=[kkr[i2].b])
                        k.op(k.act, lambda: nc.scalar.activation(out=sqb[i2][:, 0:N], in_=kkr[i2][:, 0:N],
                                                                 func=AF.Square), reads=[kkr[i2].b], writes=[sqb[i2].b])
                        pn = nxt("C")
                        k.op(k.pe, lambda: nc.tensor.matmul(pn[:, 0:N], lhsT=bones_bf, rhs=sqb[i2][:, 0:N], start=True,
                                                            stop=True), reads=[sqb[i2].b, cbf.b], writes=[pn.b])
                        k.op(k.dve, lambda: nc.vector.tensor_scalar(out=rn[i2][:, 0:N], in0=pn[:, 0:N], scalar1=1e-24,
                                                                    scalar2=None, op0=ALU.max), reads=[pn.b],
                             writes=[rn[i2].b])
                        k.op(k.act, lambda: nc.scalar.activation(out=rn[i2][:, 0:N], in_=rn[i2][:, 0:N], func=AF.Sqrt),
                             reads=[rn[i2].b], writes=[rn[i2].b])
                        k.op(k.dve, lambda: nc.vector.reciprocal(out=rn[i2][:, 0:N], in_=rn[i2][:, 0:N]),
                             reads=[rn[i2].b], writes=[rn[i2].b])
                        k.op(k.dve, lambda: nc.vector.tensor_tensor(out=kkn[i2][:, 0:N], in0=kkr[i2][:, 0:N],
                                                                    in1=rn[i2][:, 0:N], op=ALU.mult),
                             reads=[kkr[i2].b, rn[i2].b], writes=[kkn[i2].b])
                        k.op(k.pool, lambda: nc.gpsimd.tensor_scalar(out=t3[i2][:, 0:N], in0=aa[i2][:, 0:N],
                                                                     scalar1=V("k_a", hp, 1), scalar2=omka[:, hp:hp + 1],
                                                                     op0=ALU.mult, op1=ALU.add),
                             reads=[aa[i2].b, vecs.b, omka.b], writes=[t3[i2].b])
                        k.op(k.pool, lambda: nc.gpsimd.tensor_tensor(out=kd[i2][:, 0:N], in0=t3[i2][:, 0:N],
                                                                     in1=Kt[:, hp, 0:N], op=ALU.mult),
                             reads=[t3[i2].b, Kt.b], writes=[kd[i2].b])
                        k.op(k.pool, lambda: nc.gpsimd.tensor_tensor(out=bet[i2][:, 0:N], in0=kkn[i2][:, 0:N],
                                                                     in1=aa[i2][:, 0:N], op=ALU.mult),
                             reads=[kkn[i2].b, aa[i2].b], writes=[bet[i2].b])
                        k.op(k.dve, lambda: nc.vector.tensor_tensor_scan(out=cum[i2][:, 0:N], data0=rst[:, 0:N],
                                                                         data1=sig[i2][:, 0:N], initial=0.0,
                                                                         op0=ALU.mult, op1=ALU.add),
                             reads=[rst.b, sig[i2].b], writes=[cum[i2].b])
                        c3 = cum[i2][:, 0:N].rearrange("p (c t) -> p c t", t=64)
                        if d == 0:
                            k.op(k.dve, lambda: nc.vector.tensor_tensor(
                                out=xe[i2][:, 0:N].rearrange("p (c t) -> p c t", t=64),
                                in0=c3[:, :, 63:64].to_broadcast([128, nch, 64]), in1=c3, op=ALU.subtract),
                                reads=[cum[i2].b], writes=[xe[i2].b])
                            k.op(k.dve, lambda: nc.vector.tensor_tensor(out=xi[i2][:, 0:N], in0=xe[i2][:, 0:N],
                                                                        in1=sig[i2][:, 0:N], op=ALU.add),
                                 reads=[xe[i2].b, sig[i2].b], writes=[xi[i2].b])
                            xe_, xi_ = xe[i2], xi[i2]
                        else:
                            k.op(k.dve, lambda: nc.vector.tensor_tensor(out=xe[i2][:, 0:N], in0=cum[i2][:, 0:N],
                                                                        in1=sig[i2][:, 0:N], op=ALU.subtract),
                                 reads=[cum[i2].b, sig[i2].b], writes=[xe[i2].b])
                            xe_, xi_ = xe[i2], cum[i2]
                        k.op(k.act, lambda: nc.scalar.activation(out=Ee[i2][:, 0:N], in_=xe_[:, 0:N], func=AF.Exp,
                                                                 scale=-C0), reads=[xe_.b], writes=[Ee[i2].b])
                        k.op(k.act, lambda: nc.scalar.activation(out=ai[i2][:, 0:N], in_=xi_[:, 0:N], func=AF.Exp,
                                                                 scale=C0), reads=[xi_.b], writes=[ai[i2].b])
                        k.op(k.act, lambda: nc.scalar.activation(out=ri[i2][:, 0:N], in_=xe_[:, 0:N], func=AF.Exp,
                                                                 scale=C0), reads=[xe_.b], writes=[ri[i2].b])
                        k.op(k.act, lambda: nc.scalar.activation(out=wtot[i2][:, 0:nch], in_=c3[:, :, 63], func=AF.Exp,
                                                                 scale=-C0), reads=[cum[i2].b], writes=[wtot[i2].b])
                        ARv, BKv = AR[i2], BK[i2]

                        def v3(t_):
                            return t_[:, 0:N].rearrange("p (c t) -> p c t", t=64)

                        k.op(k.dve, lambda: nc.vector.scalar_tensor_tensor(out=ARv[:, 0:nch, 0:64], in0=v3(kkn[i2]),
                                                                           scalar=-1.0, in1=v3(ai[i2]), op0=ALU.mult,
                                                                           op1=ALU.mult), reads=[kkn[i2].b, ai[i2].b],
                             writes=[ARv.b])
                        k.op(k.pool, lambda: nc.gpsimd.tensor_tensor(
                            out=ARv[:, 0:nch, 64:128], in0=Rt[:, hp, 0:N].rearrange("p (c t) -> p c t", t=64),
                            in1=v3(ri[i2]), op=ALU.mult), reads=[Rt.b, ri[i2].b], writes=[ARv.b])
                        k.op(k.dve, lambda: nc.vector.tensor_tensor(out=BKv[:, 0:nch, 0:64], in0=v3(bet[i2]),
                                                                    in1=v3(Ee[i2]), op=ALU.mult),
                             reads=[bet[i2].b, Ee[i2].b], writes=[BKv.b])
                        k.op(k.pool, lambda: nc.gpsimd.tensor_tensor(out=BKv[:, 0:nch, 64:128], in0=v3(kd[i2]),
                                                                     in1=v3(Ee[i2]), op=ALU.mult),
                             reads=[kd[i2].b, Ee[i2].b], writes=[BKv.b])
                        if DBG.startswith("t_") and readout and not sweepB:
                            srcs = {"t_u": (U[:, hp, 1:N + 1], U.b), "t_xx": (XX[:, hp, 0:N], XX.b),
                                    "t_lb": (Rt[:, hp, 0:N], Rt.b),
                                    "t_r": (Rt[:, hp, 0:N], Rt.b), "t_v": (Vt[:, hp, 0:N], Vt.b),
                                    "t_k": (Kt[:, hp, 0:N], Kt.b), "t_kkn": (kkn[i2][:, 0:N], kkn[i2].b),
                                    "t_kd": (kd[i2][:, 0:N], kd[i2].b), "t_aa": (aa[i2][:, 0:N], aa[i2].b),
                                    "t_sig": (sig[i2][:, 0:N], sig[i2].b), "t_cum": (cum[i2][:, 0:N], cum[i2].b),
                                    "t_E": (Ee[i2][:, 0:N], Ee[i2].b), "t_ai": (ai[i2][:, 0:N], ai[i2].b)}
                            sap, sbuf_ = srcs[DBG]
                            k.dma(k.pool, x3T_d[hp, :, t0:t0 + N], sap, sbuf_, reads=[sbuf_])
                        if RW_STOP <= 2:
                            continue
                        pta = nxt("C"); ptb = nxt("C"); ptk = nxt("C"); ptv = nxt("B")
                        ptab = pta[0:64, :].bitcast(BF16).rearrange("p (c t) -> p c t", t=256)
                        ptbb = ptb[0:64, :].bitcast(BF16).rearrange("p (c t) -> p c t", t=256)
                        ptkb = ptk[0:64, :].bitcast(BF16).rearrange("p (c t) -> p c t", t=256)
                        for ci in range(nch):
                            last = (ci == nch - 1)
                            k.op(k.pe, lambda: nc.tensor.transpose(ptab[:, ci, 0:128], ARv[:, ci, 0:64], ident_bf),
                                 reads=[ARv.b, cbf.b], writes=[pta.b], inc=last)
                            k.op(k.pe, lambda: nc.tensor.transpose(ptbb[:, ci, 0:128], BKv[:, ci, 0:64], ident_bf),
                                 reads=[BKv.b, cbf.b], writes=[ptb.b], inc=last)
                            k.op(k.pe, lambda: nc.tensor.transpose(ptkb[:, ci, 0:128], BKv[:, ci, 64:128], ident_bf),
                                 reads=[BKv.b, cbf.b], writes=[ptk.b], inc=last)
                            k.op(k.pe, lambda: nc.tensor.transpose(ptv[0:64, ci * 128:(ci + 1) * 128],
                                                                   Vt[:, hp, ci * 64:(ci + 1) * 64], ident),
                                 reads=[Vt.b, cst.b], writes=[ptv.b], inc=last)
                        k.op(k.act, lambda: nc.scalar.copy(out=TMa[i2][:, 0:nch, :], in_=ptab[:, 0:nch, 0:128]),
                             reads=[pta.b], writes=[TMa[i2].b])
                        k.op(k.act, lambda: nc.scalar.copy(out=TMb[i2][:, 0:nch, :], in_=ptbb[:, 0:nch, 0:128]),
                             reads=[ptb.b], writes=[TMb[i2].b])
                        k.op(k.dve, lambda: nc.vector.tensor_copy(out=TMk[i2][:, 0:nch, :], in_=ptkb[:, 0:nch, 0:128]),
                             reads=[ptk.b], writes=[TMk[i2].b])
                        k.op(k.act, lambda: nc.scalar.copy(
                            out=Vtm[i2][:, 0:nch, :], in_=ptv[0:64, 0:nch * 128].rearrange("p (c t) -> p c t", t=128)),
                            reads=[ptv.b], writes=[Vtm[i2].b])
                        if RW_STOP <= 3:
                            continue
                        py = nxt("B") if readout else None
                        for h in range(2):
                            hb = slice(h * 64, (h + 1) * 64)
                            u2 = cnt["u"] % 2
                            cnt["u"] += 1
                            X, W0, W1 = Xs[u2], Ws[0], Ws[1]
                            p2 = nxt("A"); p1 = nxt("A")
                            p2v = p2[:, :].rearrange("p (c t) -> p c t", t=128)
                            p1v = p1[0:64, :].rearrange("p (c t) -> p c t", t=128)
                            for ci in range(nch):
                                k.op(k.pe, lambda: nc.tensor.matmul(p2v[:, ci, :], lhsT=BKv[hb, ci, :], rhs=ARv[hb, ci, :],
                                                                    start=True, stop=True), reads=[BKv.b, ARv.b],
                                     writes=[p2.b], inc=(ci == nch - 1))
                            for ci in range(nch):
                                k.op(k.pe, lambda: nc.tensor.matmul(p1v[:, ci, :], lhsT=ARv[hb, ci, 0:64],
                                                                    rhs=BKv[hb, ci, :], start=True, stop=True),
                                     reads=[BKv.b, ARv.b], writes=[p1.b], inc=(ci == nch - 1))
                            nb_ = [64, nch, 64]
                            k.op(k.dve, lambda: nc.vector.tensor_tensor(
                                out=X[:, 0:nch, 64:128], in0=p2v[0:64, 0:nch, 0:64],
                                in1=m_lt[0:64, :].unsqueeze(1).to_broadcast(nb_), op=ALU.mult),
                                reads=[p2.b, msk.b], writes=[X.b])
                            k.op(k.dve, lambda: nc.vector.tensor_tensor(
                                out=W0[:, 0:nch, 64:128], in0=p2v[0:64, 0:nch, 64:128],
                                in1=m_le[0:64, :].unsqueeze(1).to_broadcast(nb_), op=ALU.mult),
                                reads=[p2.b, msk.b], writes=[W0.b])
                            k.op(k.dve, lambda: nc.vector.tensor_tensor(
                                out=KA[u2][:, 0:nch, 64:128], in0=p2v[64:128, 0:nch, 64:128],
                                in1=m_le[64:128, :].unsqueeze(1).to_broadcast(nb_), op=ALU.mult),
                                reads=[p2.b, msk.b], writes=[KA[u2].b])
                            k.op(k.dve, lambda: nc.vector.tensor_tensor(
                                out=X[:, 0:nch, 0:64], in0=p1v[:, 0:nch, 0:64],
                                in1=m_ltT[0:64, :].unsqueeze(1).to_broadcast(nb_), op=ALU.mult),
                                reads=[p1.b, msk.b], writes=[X.b])
                            k.op(k.pool, lambda: nc.gpsimd.tensor_copy(out=X[:, 0:nch, 128:192], in_=X[:, 0:nch, 0:64]),
                                 reads=[X.b], writes=[X.b])
                            k.op(k.dve, lambda: nc.vector.tensor_tensor(
                                out=AKT[u2][:, 0:nch, :], in0=p1v[:, 0:nch, 64:128],
                                in1=m_ltT[0:64, :].unsqueeze(1).to_broadcast(nb_), op=ALU.mult),
                                reads=[p1.b, msk.b], writes=[AKT[u2].b])
                            k.op(k.pool, lambda: nc.gpsimd.tensor_copy(out=W0[:, 0:nch, 0:64], in_=TMb[i2][:, 0:nch, hb]),
                                 reads=[TMb[i2].b], writes=[W0.b])
                            k.op(k.pool, lambda: nc.gpsimd.tensor_copy(out=KA[u2][:, 0:nch, 0:64],
                                                                       in_=TMk[i2][:, 0:nch, hb]),
                                 reads=[TMk[i2].b], writes=[KA[u2].b])
                            if RW_STOP <= 4:
                                continue
                            Wc, Wn = W0, W1
                            Xc, Xn = X, Xs[1 - u2]
                            for j in range(6):
                                pw = nxt("A")
                                pwv = pw[0:64, :].rearrange("p (c t) -> p c t", t=128)
                                for ci in range(nch):
                                    k.op(k.pe, lambda: nc.tensor.matmul(pwv[:, ci, :], lhsT=ident_bf[0:64, 0:64],
                                                                        rhs=Wc[:, ci, :], start=True, stop=False),
                                         reads=[Wc.b, cbf.b], writes=[pw.b], inc=False)
                                    k.op(k.pe, lambda: nc.tensor.matmul(pwv[:, ci, :], lhsT=Xc[:, ci, 0:64],
                                                                        rhs=Wc[:, ci, :], start=False, stop=True),
                                         reads=[Wc.b, Xc.b], writes=[pw.b], inc=(ci == nch - 1))
                                k.op(k.act, lambda: nc.scalar.copy(out=Wn[:, 0:nch, :], in_=pwv[:, 0:nch, :]),
                                     reads=[pw.b], writes=[Wn.b])
                                Wc, Wn = Wn, Wc
                                if j < 5:
                                    pq_ = nxt("C")
                                    pqv = pq_[:, :].rearrange("p (c t) -> p c t", t=128)
                                    for ci in range(nch):
                                        k.op(k.pe, lambda: nc.tensor.matmul(pqv[:, ci, :], lhsT=Xc[:, ci, 0:128],
                                                                            rhs=Xc[:, ci, 64:192], start=True, stop=True),
                                             reads=[Xc.b], writes=[pq_.b], inc=(ci == nch - 1))
                                    k.op(k.act, lambda: nc.scalar.copy(out=Xn[:, 0:nch, 64:128],
                                                                       in_=pqv[0:64, 0:nch, 0:64]), reads=[pq_.b],
                                         writes=[Xn.b])
                                    k.op(k.dve, lambda: nc.vector.tensor_copy(out=Xn[:, 0:nch, 0:64],
                                                                              in_=pqv[64:128, 0:nch, 64:128]),
                                         reads=[pq_.b], writes=[Xn.b])
                                    k.op(k.pool, lambda: nc.gpsimd.tensor_copy(out=Xn[:, 0:nch, 128:192],
                                                                               in_=Xn[:, 0:nch, 0:64]),
                                         reads=[Xn.b], writes=[Xn.b])
                                    Xc, Xn = Xn, Xc
                            if RW_STOP <= 5:
                                continue
                            pfa = nxt("A"); pfb = nxt("A")
                            pfav = pfa[0:64, :].rearrange("p (c t) -> p c t", t=128)
                            pfbv = pfb[0:64, :].rearrange("p (c t) -> p c t", t=128)
                            for ci in range(nch):
                                k.op(k.pe, lambda: nc.tensor.matmul(pfav[:, ci, :], lhsT=TMa[i2][:, ci, hb],
                                                                    rhs=Wc[:, ci, :], start=True, stop=True),
                                     reads=[TMa[i2].b, Wc.b], writes=[pfa.b], inc=(ci == nch - 1))
                            for ci in range(nch):
                                k.op(k.pe, lambda: nc.tensor.matmul(pfbv[:, ci, :], lhsT=AKT[u2][:, ci, :],
                                                                    rhs=Wc[:, ci, :], start=True, stop=True),
                                     reads=[AKT[u2].b, Wc.b], writes=[pfb.b], inc=(ci == nch - 1))
                            k.op(k.dve, lambda: nc.vector.tensor_tensor(
                                out=M1[u2][:, 0:nch, :], in0=pfav[:, 0:nch, 0:64],
                                in1=ident[0:64, 0:64].unsqueeze(1).to_broadcast(nb_), op=ALU.add),
                                reads=[pfa.b, cst.b], writes=[M1[u2].b])
                            if h == 0:
                                rcs, rcb = ARv[0:64, 0:nch, 64:128], ARv.b
                            else:
                                k.op(k.dve, lambda: nc.vector.tensor_copy(out=RC0[:, 0:nch, :],
                                                                          in_=ARv[64:128, 0:nch, 64:128]),
                                     reads=[ARv.b], writes=[RC0.b])
                                rcs, rcb = RC0[:, 0:nch, :], RC0.b
                            k.op(k.dve, lambda: nc.vector.tensor_tensor(out=QT[u2][:, 0:nch, :], in0=pfav[:, 0:nch, 64:128],
                                                                        in1=rcs, op=ALU.add),
                                 reads=[pfa.b, rcb], writes=[QT[u2].b])
                            k.op(k.dve, lambda: nc.vector.tensor_tensor(out=ZN[u2][:, 0:nch, :], in0=pfbv[:, 0:nch, :],
                                                                        in1=KA[u2][:, 0:nch, :], op=ALU.add),
                                 reads=[pfb.b, KA[u2].b], writes=[ZN[u2].b])
                            wti = wt0[cnt["u"] % 4]
                            k.op(k.dve, lambda: nc.vector.tensor_copy(out=wti[:, 0:nch], in_=wtot[i2][hb, 0:nch]),
                                 reads=[wtot[i2].b], writes=[wti.b])
                            if RW_STOP <= 6:
                                continue
                            hh = hp * 2 + h
                            order = range(nch) if d == 0 else range(nch - 1, -1, -1)
                            for oi, ci in enumerate(order):
                                stp = STP[(cnt["u"] + oi) % 3]
                                k.op(k.dve, lambda: nc.vector.tensor_scalar(out=stp[:, :], in0=ST[:, hh, :],
                                                                            scalar1=wti[:, ci:ci + 1], scalar2=None,
                                                                            op0=ALU.mult), reads=[ST.b, wti.b],
                                     writes=[stp.b])
                                pst = nxt("C")
                                k.op(k.pe, lambda: nc.tensor.matmul(pst[0:64, 0:64], lhsT=M1[u2][:, ci, :], rhs=stp[:, :],
                                                                    start=True, stop=False), reads=[M1[u2].b, stp.b],
                                     writes=[pst.b], inc=False)
                                k.op(k.pe, lambda: nc.tensor.matmul(pst[0:64, 0:64], lhsT=ZN[u2][:, ci, 0:64],
                                                                    rhs=Vtm[i2][:, ci, hb], start=False, stop=True),
                                     reads=[ZN[u2].b, Vtm[i2].b], writes=[pst.b])
                                if readout:
                                    k.op(k.pe, lambda: nc.tensor.matmul(py[hb, ci * 64:(ci + 1) * 64], lhsT=stp[:, :],
                                                                        rhs=QT[u2][:, ci, :], start=True, stop=False),
                                         reads=[stp.b, QT[u2].b], writes=[py.b], inc=False)
                                    k.op(k.pe, lambda: nc.tensor.matmul(py[hb, ci * 64:(ci + 1) * 64],
                                                                        lhsT=Vtm[i2][:, ci, hb],
                                                                        rhs=ZN[u2][:, ci, 64:128], start=False, stop=True),
                                         reads=[Vtm[i2].b, ZN[u2].b], writes=[py.b])
                                k.op(k.act, lambda: nc.scalar.copy(out=ST[:, hh, :], in_=pst[0:64, 0:64]), reads=[pst.b],
                                     writes=[ST.b])
                        if not readout or RW_STOP <= 7:
                            continue
                        tr0 = t0 - CL
                        if not sweepB:
                            k.op(k.act, lambda: nc.scalar.copy(out=yf[i2][:, 0:N], in_=py[:, 0:N]), reads=[py.b],
                                 writes=[yf[i2].b])
                            k.dma(k.pool, yf_d[hp, :, tr0:tr0 + N], yf[i2][:, 0:N], yf[i2].b, reads=[yf[i2].b])
                            continue
                        k.dma(k.sp, yf[i2][:, 0:N], yf_d[hp, :, tr0:tr0 + N], yf[i2].b, writes=[yf[i2].b])
                        wk_ = gn[i2]
                        k.op(k.dve, lambda: nc.vector.tensor_tensor(out=wk_[:, 0:N], in0=py[:, 0:N], in1=yf[i2][:, 0:N],
                                                                    op=ALU.add), reads=[py.b, yf[i2].b], writes=[wk_.b])
                        if DBG == "wkv":
                            k.dma(k.pool, x3T_d[hp, :, t0:t0 + N], wk_[:, 0:N], wk_.b, reads=[wk_.b])
                        if DBG == "yf":
                            k.dma(k.pool, x3T_d[hp, :, t0:t0 + N], yf[i2][:, 0:N], yf[i2].b, reads=[yf[i2].b])
                        k.op(k.act, lambda: nc.scalar.copy(out=sqb[i2][:, 0:N], in_=wk_[:, 0:N]), reads=[wk_.b],
                             writes=[sqb[i2].b])
                        pm_ = nxt("C")
                        k.op(k.pe, lambda: nc.tensor.matmul(pm_[:, 0:N], lhsT=bones_bf, rhs=sqb[i2][:, 0:N], start=True,
                                                            stop=True), reads=[sqb[i2].b, cbf.b], writes=[pm_.b])
                        k.op(k.dve, lambda: nc.vector.scalar_tensor_tensor(out=wk_[:, 0:N], in0=pm_[:, 0:N],
                                                                           scalar=-1.0 / 64, in1=wk_[:, 0:N],
                                                                           op0=ALU.mult, op1=ALU.add),
                             reads=[pm_.b, wk_.b], writes=[wk_.b])
                        k.op(k.act, lambda: nc.scalar.activation(out=sqb[i2][:, 0:N], in_=wk_[:, 0:N], func=AF.Square),
                             reads=[wk_.b], writes=[sqb[i2].b])
                        pv_ = nxt("C")
                        k.op(k.pe, lambda: nc.tensor.matmul(pv_[:, 0:N], lhsT=bones_bf, rhs=sqb[i2][:, 0:N], start=True,
                                                            stop=True), reads=[sqb[i2].b, cbf.b], writes=[pv_.b])
                        k.op(k.act, lambda: nc.scalar.activation(out=rn[i2][:, 0:N], in_=pv_[:, 0:N], func=AF.Sqrt,
                                                                 scale=1.0 / 64, bias=GN_EPS), reads=[pv_.b],
                             writes=[rn[i2].b])
                        k.op(k.dve, lambda: nc.vector.reciprocal(out=rn[i2][:, 0:N], in_=rn[i2][:, 0:N]),
                             reads=[rn[i2].b], writes=[rn[i2].b])
                        k.op(k.dve, lambda: nc.vector.scalar_tensor_tensor(out=wk_[:, 0:N], in0=wk_[:, 0:N],
                                                                           scalar=V("ln_g", hp, 1), in1=rn[i2][:, 0:N],
                                                                           op0=ALU.mult, op1=ALU.mult),
                             reads=[wk_.b, rn[i2].b, vecs.b], writes=[wk_.b])
                        pa2 = nxt("A")
                        k.op(k.pe, lambda: nc.tensor.matmul(pa2[:, 0:N], lhsT=a2c[ob_, hs], rhs=ta[ob_, 0:N], start=True,
                                                            stop=True), reads=[a2c.b, ta.b], writes=[pa2.b])
                        k.op(k.act, lambda: nc.scalar.activation(out=ao[i2][:, 0:N], in_=pa2[:, 0:N], func=AF.Sigmoid,
                                                                 bias=V("a0", (1 - d) * 8 + hp, 1)),
                             reads=[pa2.b, vecs.b], writes=[ao[i2].b])
                        k.op(k.pool, lambda: nc.gpsimd.tensor_scalar(out=t3[i2][:, 0:N], in0=ao[i2][:, 0:N],
                                                                     scalar1=V("k_a", hp, 1), scalar2=omka[:, hp:hp + 1],
                                                                     op0=ALU.mult, op1=ALU.add),
                             reads=[ao[i2].b, vecs.b, omka.b], writes=[t3[i2].b])
                        k.op(k.pool, lambda: nc.gpsimd.tensor_tensor(out=t3[i2][:, 0:N], in0=t3[i2][:, 0:N],
                                                                     in1=Kt[:, hp, 0:N], op=ALU.mult),
                             reads=[t3[i2].b, Kt.b], writes=[t3[i2].b])
                        k.op(k.pool, lambda: nc.gpsimd.tensor_tensor(out=t3[i2][:, 0:N], in0=t3[i2][:, 0:N],
                                                                     in1=kd[i2][:, 0:N], op=ALU.add),
                             reads=[t3[i2].b, kd[i2].b], writes=[t3[i2].b])
                        k.op(k.dve, lambda: nc.vector.scalar_tensor_tensor(out=sqb[i2][:, 0:N], in0=t3[i2][:, 0:N],
                                                                           scalar=V("r_k", hp, 1), in1=Rt[:, hp, 0:N],
                                                                           op0=ALU.mult, op1=ALU.mult),
                             reads=[t3[i2].b, Rt.b, vecs.b], writes=[sqb[i2].b])
                        pb_ = nxt("C")
                        k.op(k.pe, lambda: nc.tensor.matmul(pb_[:, 0:N], lhsT=bones_bf, rhs=sqb[i2][:, 0:N], start=True,
                                                            stop=True), reads=[sqb[i2].b, cbf.b], writes=[pb_.b])
                        k.op(k.dve, lambda: nc.vector.tensor_tensor(out=t3[i2][:, 0:N], in0=pb_[:, 0:N],
                                                                    in1=Vt[:, hp, 0:N], op=ALU.mult),
                             reads=[pb_.b, Vt.b], writes=[t3[i2].b])
                        k.op(k.dve, lambda: nc.vector.scalar_tensor_tensor(out=wk_[:, 0:N], in0=wk_[:, 0:N],
                                                                           scalar=V("ln_b", hp, 1), in1=t3[i2][:, 0:N],
                                                                           op0=ALU.add, op1=ALU.add),
                             reads=[wk_.b, t3[i2].b, vecs.b], writes=[wk_.b])
                        pg_ = nxt("A")
                        k.op(k.pe, lambda: nc.tensor.matmul(pg_[:, 0:N], lhsT=g2[:, hs], rhs=tg[:, 0:N], start=True,
                                                            stop=True), reads=[g2.b, tg.b], writes=[pg_.b])
                        k.op(k.dve, lambda: nc.vector.tensor_tensor(out=OB[:, hp, 0:N], in0=wk_[:, 0:N], in1=pg_[:, 0:N],
                                                                    op=ALU.mult), reads=[wk_.b, pg_.b], writes=[OB.b])
                    if sweepB and readout:
                        for n in range(8):
                            i2 = n % 2
                            k.dma(k.sp, x2b[i2][:, 0:N], x2T_d[n, :, t0:t0 + N], x2b[i2].b, writes=[x2b[i2].b])
                            po_ = nxt("A")
                            for c in range(8):
                                k.op(k.pe, lambda: nc.tensor.matmul(po_[:, 0:N], lhsT=wo1[:, c, n * 128:(n + 1) * 128],
                                                                    rhs=OB[:, c, 0:N], start=(c == 0), stop=(c == 7)),
                                     reads=[wo1.b, OB.b], writes=[po_.b], inc=(c == 7))
                            k.op(k.dve, lambda: nc.vector.scalar_tensor_tensor(
                                out=x2b[i2][:, 0:N], in0=po_[:, 0:N], scalar=modv[:, 1, 0, 2, n:n + 1],
                                in1=x2b[i2][:, 0:N], op0=ALU.mult, op1=ALU.add), reads=[po_.b, modv.b, x2b[i2].b],
                                writes=[x2b[i2].b])
                            if DBG == "":
                                k.dma(k.pool, x3T_d[n, :, t0:t0 + N], x2b[i2][:, 0:N], x2b[i2].b, reads=[x2b[i2].b])

                lat = [(CL + i * NB, NB, 0) for i in range(L // NB)]
                k.op(k.pool, lambda: nc.gpsimd.memset(ST[:, :, :], 0.0), writes=[ST.b])
                block(0, CL, 1, 0, False)
                for (t0, N, s_) in lat:
                    block(t0, N, s_, 0, False)
                    if RW_STOP < 99:
                        break
                if RW_STOP < 99:
                    return
                k.barrier()
                k.op(k.pool, lambda: nc.gpsimd.memset(ST[:, :, :], 0.0), writes=[ST.b])
                block(0, CL, 1, 1, True)
                for (t0, N, s_) in reversed(lat):
                    block(t0, N, s_, 1, True)

        blocks0 = [(0, CL, 1)] + [(CL + i * 512, 512, 0) for i in range(L // 512)]

        if stage >= 1:
            with contextlib.ExitStack() as s1:
                wq = tl("wqkv", [128, 8, 1536], BF16, s1)
                with contextlib.ExitStack() as s2:
                    stg = [tl(f"p1_stg{i}", [128, 1536], F32, s2) for i in range(2)]
                    load_weight_bf16(wq, lambda i: wq[:, i, :], lambda i: wqkv_d[i * 128:(i + 1) * 128, :], 8, 1536, stg,
                                     [k.dve, k.pool])
                k.barrier()
                xs = [tl(f"xs{i}", [128, D], F32, s1) for i in range(2)]
                xT = tl("p1_xT", [128, 8, 512], F32, s1)
                hT = tl("p1_hT", [128, 8, 512], BF16, s1)
                sq = tl("p1_sq", [128, 8, 512], BF16, s1)
                rstd = tl("p1_rstd", [128, 512], F32, s1)
                tmp = [tl(f"p1_tmp{i}", [128, 512], F32, s1) for i in range(2)]
                qk_out = tl("p1_qk", [128, 10, 512], BF16, s1)
                v_out = tl("p1_v", [128, 4, 256], BF16, s1)
                ct = tl("p1_ct", [128, 512], F32, s1)
                stb = tl("p1_st", [128, 512], F32, s1)
                hsq = [tl(f"p1_hsq{i}", [128, 512], BF16, s1) for i in range(2)]
                hr = [tl(f"p1_hr{i}", [128, 512], F32, s1) for i in range(2)]
                qg = [tl(f"p1_qg{i}", [128, 512], F32, s1) for i in range(2)]
                qgb = [tl(f"p1_qgb{i}", [128, 512], BF16, s1) for i in range(2)]
                t1 = [tl(f"p1_t1{i}", [128, 512], F32, s1) for i in range(2)]
                t2 = [tl(f"p1_t2{i}", [128, 512], F32, s1) for i in range(2)]
                for bi, (t0, N, s) in enumerate(blocks0):
                    src = ctx_d if s == 1 else x_d
                    r0 = t0 if s == 1 else t0 - CL
                    for tt in range(N // 128):
                        xi = xs[tt % 2]
                        k.dma(k.sp, xi[:, :], src[r0 + tt * 128:r0 + (tt + 1) * 128, :], xi.b, writes=[xi.b])
                        for half in range(2):
                            pt = nxt("C")
                            for cc in range(4):
                                c = half * 4 + cc
                                k.op(k.pe, lambda: nc.tensor.transpose(pt[:, cc * 128:(cc + 1) * 128],
                                                                       xi[:, c * 128:(c + 1) * 128], ident),
                                     reads=[xi.b, cst.b], writes=[pt.b], inc=(cc == 3))
                            k.op(k.act, lambda: nc.scalar.copy(
                                out=xT[:, half * 4:(half + 1) * 4, tt * 128:(tt + 1) * 128],
                                in_=pt[:, :].rearrange("p (c t) -> p c t", t=128)), reads=[pt.b], writes=[xT.b])
                    k.dma(k.pool, xT_d[:, :, t0:t0 + N].rearrange("c p t -> p c t"), xT[:, :, 0:N], xT.b, reads=[xT.b])
                    if s == 0:
                        k.dma(k.sp, ct[:, :], ctab_d[:, r0:r0 + 512], ct.b, writes=[ct.b])
                        k.dma(k.sp, stb[:, :], stab_d[:, r0:r0 + 512], stb.b, writes=[stb.b])
                    norm_mod(xT, N, 0, s, 0, hT, sq, rstd, tmp)
                    for n in range(10):
                        pq = nxt("A")
                        for c in range(8):
                            k.op(k.pe, lambda: nc.tensor.matmul(pq[:, 0:N], lhsT=wq[:, c, n * 128:(n + 1) * 128],
                                                                rhs=hT[:, c, 0:N], start=(c == 0), stop=(c == 7)),
                                 reads=[wq.b, hT.b], writes=[pq.b], inc=(c == 7))
                        i2 = n % 2
                        k.op(k.act, lambda: nc.scalar.activation(out=hsq[i2][:, 0:N], in_=pq[:, 0:N], func=AF.Square),
                             reads=[pq.b], writes=[hsq[i2].b])
                        ph = nxt("C")
                        k.op(k.pe, lambda: nc.tensor.matmul(ph[:, 0:N], lhsT=bones_bf, rhs=hsq[i2][:, 0:N], start=True,
                                                            stop=True), reads=[hsq[i2].b, cbf.b], writes=[ph.b])
                        k.op(k.act, lambda: nc.scalar.activation(out=hr[i2][:, 0:N], in_=ph[:, 0:N], func=AF.Sqrt,
                                                                 scale=1.0 / 64, bias=EPS), reads=[ph.b],
                             writes=[hr[i2].b])
                        k.op(k.dve, lambda: nc.vector.reciprocal(out=hr[i2][:, 0:N], in_=hr[i2][:, 0:N]),
                             reads=[hr[i2].b], writes=[hr[i2].b])
                        gcol = qg8[:, 0:1] if n < 8 else qg8[:, 1:2]
                        k.op(k.dve, lambda: nc.vector.scalar_tensor_tensor(
                            out=qg[i2][:, 0:N], in0=pq[:, 0:N], scalar=gcol, in1=hr[i2][:, 0:N], op0=ALU.mult,
                            op1=ALU.mult), reads=[pq.b, qg8.b, hr[i2].b], writes=[qg[i2].b])
                        if s == 1:
                            k.op(k.pool, lambda: nc.gpsimd.tensor_copy(out=qk_out[:, n, 0:N], in_=qg[i2][:, 0:N]),
                                 reads=[qg[i2].b], writes=[qk_out.b])
                        else:
                            k.op(k.pool, lambda: nc.gpsimd.tensor_copy(out=qgb[i2][:, 0:N], in_=qg[i2][:, 0:N]),
                                 reads=[qg[i2].b], writes=[qgb[i2].b])
                            pp = nxt("C")
                            k.op(k.pe, lambda: nc.tensor.matmul(pp[:, 0:N], lhsT=perm_bf, rhs=qgb[i2][:, 0:N],
                                                                start=True, stop=True), reads=[qgb[i2].b, cbf.b],
                                 writes=[pp.b])
                            k.op(k.pool, lambda: nc.gpsimd.tensor_tensor(out=t1[i2][:, 0:N], in0=qg[i2][:, 0:N],
                                                                         in1=ct[:, 0:N], op=ALU.mult),
                                 reads=[qg[i2].b, ct.b], writes=[t1[i2].b])
                            k.op(k.dve, lambda: nc.vector.tensor_tensor(out=t2[i2][:, 0:N], in0=pp[:, 0:N],
                                                                        in1=stb[:, 0:N], op=ALU.mult),
                                 reads=[pp.b, stb.b], writes=[t2[i2].b])
                            k.op(k.dve, lambda: nc.vector.tensor_tensor(out=qk_out[:, n, 0:N], in0=t1[i2][:, 0:N],
                                                                        in1=t2[i2][:, 0:N], op=ALU.add),
                                 reads=[t1[i2].b, t2[i2].b], writes=[qk_out.b])
                    for tt in range(N // 128):
                        pv = nxt("A")
                        for c in range(8):
                            k.op(k.pe, lambda: nc.tensor.matmul(pv[:, 0:256], lhsT=hT[:, c, tt * 128:(tt + 1) * 128],
                                                                rhs=wq[:, c, 1280:1536], start=(c == 0), stop=(c == 7)),
                                 reads=[wq.b, hT.b], writes=[pv.b], inc=(c == 7))
                        k.op(k.act, lambda: nc.scalar.copy(out=v_out[:, tt, :], in_=pv[:, 0:256]), reads=[pv.b],
                             writes=[v_out.b])
                    k.dma(k.pool, qT_d[:, :, t0:t0 + N].rearrange("c p t -> p c t"), qk_out[:, 0:8, 0:N], qk_out.b,
                          reads=[qk_out.b])
                    k.dma(k.pool, kT_d[:, :, t0:t0 + N].rearrange("c p t -> p c t"), qk_out[:, 8:10, 0:N], qk_out.b,
                          reads=[qk_out.b])
                    k.dma(k.pool, v_d[t0:t0 + N, :].rearrange("(j p) d -> p j d", p=128), v_out[:, 0:N // 128, :],
                          v_out.b, reads=[v_out.b])

        def phase_barrier():
            k.barrier()

        phase_barrier()

        if stage >= 2:
            with contextlib.ExitStack() as s1:
                KT = tl("KT", [128, 2, TT], BF16, s1)
                VA = tl("VA", [128, NKT, 4, 128], BF16, s1)
                wo = tl("wo0", [128, 8, D], BF16, s1)
                with contextlib.ExitStack() as s2:
                    stg = [tl(f"p2_stg{i}", [128, D], F32, s2) for i in range(2)]
                    load_weight_bf16(wo, lambda i: wo[:, i, :], lambda i: wo0_d[i * 128:(i + 1) * 128, :], 8, D, stg,
                                     [k.dve, k.pool])
                k.barrier()
                qT = [tl(f"p2_qT{i}", [128, 8, 512], BF16, s1) for i in range(2)]
                PT = [tl(f"p2_PT{i}", [128, 512], BF16, s1) for i in range(3)]
                oT = tl("p2_oT", [128, 8, 512], BF16, s1)
                xT = tl("p2_xT", [128, 8, 512], F32, s1)
                rec = [tl(f"p2_rec{i}", [128, 512], F32, s1) for i in range(2)]
                k.dma(k.sp, KT[:, :, :], kT_d[:, :, :].rearrange("c p t -> p c t"), KT.b, writes=[KT.b])
                k.op(k.pool, lambda: nc.gpsimd.memset(VA[:, :, :, :], 1.0), writes=[VA.b])
                for g in range(4):
                    off = 0 if g % 2 == 0 else 64
                    for j0 in range(0, NKT, 8):
                        j1 = min(NKT, j0 + 8)
                        k.dma(k.sp, VA[:, j0:j1, g, off:off + 64],
                              v_d[j0 * 128:j1 * 128, g * 64:(g + 1) * 64].rearrange("(j p) d -> p j d", p=128), VA.b,
                              writes=[VA.b])
                for bi, (t0, N, s) in enumerate(blocks0):
                    q = qT[bi % 2]
                    k.dma(k.sp, q[:, :, 0:N], qT_d[:, :, t0:t0 + N].rearrange("c p t -> p c t"), q.b, writes=[q.b])
                    k.dma(k.sp, xT[:, :, 0:N], xT_d[:, :, t0:t0 + N].rearrange("c p t -> p c t"), xT.b, writes=[xT.b])
                    nkt = 2 if s == 1 else NKT
                    hidx = 0
                    for c in range(8):
                        for sl in range(2):
                            g = (c // 4) * 2 + sl
                            po = psB[hidx % 2]
                            lo, hi = sl * 64, (sl + 1) * 64
                            pss = [None] * nkt
                            for j in range(nkt + 2):
                                if j < nkt:
                                    ps_ = nxt("A")
                                    pss[j] = ps_
                                    k.op(k.pe, lambda: nc.tensor.matmul(ps_[:, 0:N],
                                                                        lhsT=KT[lo:hi, g // 2, j * 128:(j + 1) * 128],
                                                                        rhs=q[lo:hi, c, 0:N], start=True, stop=True),
                                         reads=[KT.b, q.b], writes=[ps_.b])
                                    pt_ = PT[j % 3]
                                    k.op(k.act, lambda: nc.scalar.activation(out=pt_[:, 0:N], in_=ps_[:, 0:N],
                                                                             func=AF.Exp), reads=[ps_.b],
                                         writes=[pt_.b])
                                if j >= 2:
                                    jj = j - 2
                                    pt2 = PT[jj % 3]
                                    k.op(k.pe, lambda: nc.tensor.matmul(po[:, 0:N], lhsT=VA[:, jj, g, :],
                                                                        rhs=pt2[:, 0:N], start=(jj == 0),
                                                                        stop=(jj == nkt - 1)),
                                         reads=[VA.b, pt2.b], writes=[po.b], inc=(jj == nkt - 1))
                            rc = rec[hidx % 2]
                            olo, ohi = (64, 128) if sl == 0 else (0, 64)
                            k.op(k.dve, lambda: nc.vector.reciprocal(out=rc[lo:hi, 0:N], in_=po[olo:ohi, 0:N]),
                                 reads=[po.b], writes=[rc.b])
                            k.op(k.dve, lambda: nc.vector.tensor_tensor(out=oT[lo:hi, c, 0:N], in0=po[lo:hi, 0:N],
                                                                        in1=rc[lo:hi, 0:N], op=ALU.mult),
                                 reads=[po.b, rc.b], writes=[oT.b])
                            hidx += 1
                    for n in range(8):
                        py = nxt("C")
                        for c in range(8):
                            k.op(k.pe, lambda: nc.tensor.matmul(py[:, 0:N], lhsT=wo[:, c, n * 128:(n + 1) * 128],
                                                                rhs=oT[:, c, 0:N], start=(c == 0), stop=(c == 7)),
                                 reads=[wo.b, oT.b], writes=[py.b], inc=(c == 7))
                        k.op(k.dve, lambda: nc.vector.scalar_tensor_tensor(
                            out=xT[:, n, 0:N], in0=py[:, 0:N], scalar=modv[:, 0, s, 2, n:n + 1], in1=xT[:, n, 0:N],
                            op0=ALU.mult, op1=ALU.add), reads=[py.b, modv.b, xT.b], writes=[xT.b])
                    k.dma(k.pool, x1T_d[:, :, t0:t0 + N].rearrange("c p t -> p c t"), xT[:, :, 0:N], xT.b, reads=[xT.b])
            phase_barrier()

        if stage >= 3:
            zt = tl("zeros", [128, 8, 1], F32)
            k.op(k.pool, lambda: nc.gpsimd.memset(zt[:, :, :], 0.0), writes=[zt.b])
            for (dd, n_) in ((h1c_d, CL), (h1l_d, L)):
                for col in (0, n_ + 1):
                    k.dma(k.pool, dd[:, :, col:col + 1].rearrange("c p t -> p c t"), zt[:, :, :], zt.b, reads=[zt.b],
                          allow_slow_non_contiguous=True)

            def post_h1(t0, N, s_, xT, hT, sq, rstd, tmp, sg):
                rms_stats(xT, N, sq, rstd)
                dst = h1c_d if s_ == 1 else h1l_d
                r0 = (t0 if s_ == 1 else t0 - CL) + 1
                for c in range(8):
                    tb = tmp[c % 2]
                    ob = sg[c % 2]
                    k.op(k.dve, lambda: nc.vector.scalar_tensor_tensor(
                        out=tb[:, 0:N], in0=xT[:, c, 0:N], scalar=modv[:, 1, s_, 1, c:c + 1], in1=rstd[:, 0:N],
                        op0=ALU.mult, op1=ALU.mult), reads=[xT.b, modv.b, rstd.b], writes=[tb.b])
                    k.op(k.pool, lambda: nc.gpsimd.tensor_scalar(
                        out=ob[:, 0:N], in0=tb[:, 0:N], scalar1=modv[:, 1, s_, 0, c:c + 1], scalar2=None,
                        op0=ALU.add), reads=[tb.b, modv.b], writes=[ob.b])
                    k.dma(k.pool, dst[c, :, r0:r0 + N], ob[:, 0:N], ob.b, reads=[ob.b])
                    if os.environ.get("DBG", "") == "h1x":
                        k.dma(k.pool, x3T_d[c, :, t0:t0 + N], ob[:, 0:N], ob.b, reads=[ob.b])

            ffn_phase(0, x1T_d, x2T_d, blocks0, post_fn=post_h1)
            phase_barrier()

        if stage >= 4:
            rwkv_phase()
            phase_barrier()

        if stage >= 6:
            def post_final(t0, N, s_, xT, hT, sq, rstd, tmp, sg):
                rms_stats(xT, N, sq, rstd)
                fg = V("final_g", 0, 8)
                for c in range(8):
                    k.op(k.dve, lambda: nc.vector.scalar_tensor_tensor(
                        out=xT[:, c, 0:N], in0=xT[:, c, 0:N], scalar=fg[:, c:c + 1], in1=rstd[:, 0:N],
                        op0=ALU.mult, op1=ALU.mult), reads=[xT.b, vecs.b, rstd.b], writes=[xT.b])
                for tt in range(N // 128):
                    for half in range(2):
                        pt = nxt("C")
                        for cc in range(4):
                            c = half * 4 + cc
                            k.op(k.pe, lambda: nc.tensor.transpose(pt[:, cc * 128:(cc + 1) * 128],
                                                                   xT[:, c, tt * 128:(tt + 1) * 128], ident),
                                 reads=[xT.b, cst.b], writes=[pt.b], inc=(cc == 3))
                        o = sg[half]
                        k.op(k.act, lambda: nc.scalar.copy(out=o[:, :], in_=pt[:, :]), reads=[pt.b], writes=[o.b])
                        r = (t0 - CL) + tt * 128
                        k.dma(k.pool, out_d[r:r + 128, half * 512:(half + 1) * 512], o[:, :], o.b, reads=[o.b])

            ffn_phase(1, x3T_d, None, blocks0[1:], post_fn=post_final)
            phase_barrier()

        dbg_src = {1: xT_d, 2: x1T_d, 3: x2T_d, 5: x3T_d}.get(stage)
        if dbg_src is not None:
            with contextlib.ExitStack() as s1:
                xT = tl("o_xT", [128, 8, 512], F32, s1)
                ot = [tl(f"o_t{i}", [128, D], F32, s1) for i in range(2)]
                for bi in range(L // 512):
                    t0 = CL + bi * 512
                    k.dma(k.sp, xT[:, :, :], dbg_src[:, :, t0:t0 + 512].rearrange("c p t -> p c t"), xT.b,
                          writes=[xT.b])
                    for tt in range(4):
                        o = ot[tt % 2]
                        for half in range(2):
                            pt = nxt("C")
                            for cc in range(4):
                                c = half * 4 + cc
                                k.op(k.pe, lambda: nc.tensor.transpose(pt[:, cc * 128:(cc + 1) * 128],
                                                                       xT[:, c, tt * 128:(tt + 1) * 128], ident),
                                     reads=[xT.b, cst.b], writes=[pt.b], inc=(cc == 3))
                            k.op(k.act, lambda: nc.scalar.copy(out=o[:, half * 512:(half + 1) * 512], in_=pt[:, :]),
                                 reads=[pt.b], writes=[o.b])
                        r = bi * 512 + tt * 128
                        k.dma(k.pool, out_d[r:r + 128, :], o[:, :], o.b, reads=[o.b])
        k.finish()
        print(f"[build] L={L} stage={stage} inst={k.n_inst} waits={k.n_wait}")
    return nc


_CACHE = {}


def prep_inputs(inp, b, L):
    wqkv_p, wo_p = _CACHE.get("w") or host_weights(inp)
    _CACHE["w"] = (wqkv_p, wo_p)
    cm, ct, stb = _CACHE.get(("c", L)) or make_consts(L)
    _CACHE[("c", L)] = (cm, ct, stb)
    f = lambda a: np.ascontiguousarray(np.asarray(a, np.float32))
    return {
        "x": f(inp["x"][b, :L]), "ctx": f(inp["ctx"][b]), "vecs": pack_vecs(inp, b), "consts": cm, "ctab": ct,
        "stab": stb, "mod_w": f(inp["mod_w"]), "wqkv": wqkv_p, "wo0": wo_p, "ffn_wg": f(inp["ffn_wg"]),
        "ffn_wu": f(inp["ffn_wu"]), "ffn_wd": f(inp["ffn_wd"]),
        "wrkv": f(inp["rwkv_wrkv"][0]), "wo1": f(inp["rwkv_wo"][0]),
        "w1c": f(np.concatenate([inp["rwkv_w1"][0][0], inp["rwkv_w1"][0][1]], axis=1)),
        "a1c": f(np.concatenate([inp["rwkv_a1"][0][0], inp["rwkv_a1"][0][1]], axis=1)),
        "g1": f(inp["rwkv_g1"][0]),
        "w2c": f(np.concatenate([inp["rwkv_w2"][0][0], inp["rwkv_w2"][0][1]], axis=0)),
        "a2c": f(np.concatenate([inp["rwkv_a2"][0][0], inp["rwkv_a2"][0][1]], axis=0)),
        "g2": f(inp["rwkv_g2"][0]), "masks": make_masks(),
    }


def kernel(**inputs):
    B, L, _ = inputs["x"].shape
    inp = {kk: np.asarray(v) for kk, v in inputs.items()}
    nc = build(L)
    in_maps = [prep_inputs(inp, b, L) for b in range(B)]
    res = run_bass_kernel_spmd(nc, in_maps, core_ids=list(range(B)))
    return np.stack([np.asarray(r["out"], np.float32) for r in res.results], axis=0)
```

```python
import contextlib
import numpy as np
import concourse.bass as bass
import concourse.mybir as mybir
from concourse.bass_utils import run_bass_kernel_spmd

F32 = mybir.dt.float32
BF16 = mybir.dt.bfloat16
AF = mybir.ActivationFunctionType
ALU = mybir.AluOpType

D = 1024
NCH = 8
CL = 256
DFF = 2816
NF = 22
EPS = 1e-6
GN_EPS = 64e-5


class Buf:
    __slots__ = ("name", "w", "r", "dsem", "dcnt")

    def __init__(self, name):
        self.name = name
        self.w = None
        self.r = {}
        self.dsem = None
        self.dcnt = 0


class Eng:
    def __init__(self, key, eng, sem):
        self.key = key
        self.eng = eng
        self.sem = sem
        self.cnt = 0
        self.waited = {}


class K:
    def __init__(self, nc, stack):
        self.nc = nc
        self.stack = stack
        self.pe = Eng("pe", nc.tensor, self._sem("c_pe"))
        self.act = Eng("act", nc.scalar, self._sem("c_act"))
        self.dve = Eng("dve", nc.vector, self._sem("c_dve"))
        self.pool = Eng("pool", nc.gpsimd, self._sem("c_pool"))
        self.sp = Eng("sp", nc.sync, self._sem("c_sp"))
        self.n_inst = 0
        self.n_wait = 0
        self.dma_bufs = []

    def _sem(self, name):
        return self.stack.enter_context(self.nc.semaphore(name))

    def _need(self, e, sem, val):
        if val > e.waited.get(id(sem), 0):
            e.eng.wait_ge(sem, val)
            e.waited[id(sem)] = val
            self.n_wait += 1

    def _pre(self, e, reads, writes, is_dma=False):
        for b in reads:
            if b.w is not None:
                sem, val, key = b.w
                self._need(e, sem, val)
        for b in writes:
            if b.w is not None:
                sem, val, key = b.w
                if is_dma or key != e.key:
                    self._need(e, sem, val)
            for key, (sem, val) in b.r.items():
                if is_dma or key != e.key:
                    self._need(e, sem, val)

    def op(self, e, fn, reads=(), writes=(), inc=True):
        self._pre(e, reads, writes)
        ins = fn()
        self.n_inst += 1
        if inc:
            ins.then_inc(e.sem, 1)
            e.cnt += 1
            val = e.cnt
        else:
            val = e.cnt + 1
        for b in reads:
            b.r[e.key] = (e.sem, val)
        for b in writes:
            b.w = (e.sem, val, e.key)
            b.r = {}
        return ins

    def dma(self, q, out_ap, in_ap, sb, reads=(), writes=(), **kw):
        if sb.dsem is None:
            sb.dsem = self._sem("d_" + sb.name)
            self.dma_bufs.append(sb)
        self._pre(q, reads, writes, is_dma=True)
        ins = q.eng.dma_start(out=out_ap, in_=in_ap, **kw)
        ins.then_inc(sb.dsem, 16)
        sb.dcnt += 16
        self.n_inst += 1
        key = "dma_" + sb.name
        for b in reads:
            b.r[key] = (sb.dsem, sb.dcnt)
        for b in writes:
            b.w = (sb.dsem, sb.dcnt, key)
            b.r = {}
        return ins

    def finish(self):
        for sb in self.dma_bufs:
            self._need(self.sp, sb.dsem, sb.dcnt)

    def barrier(self):
        engs = [self.pe, self.act, self.dve, self.pool, self.sp]
        for e in engs:
            for o in engs:
                if o is not e and o.cnt > 0:
                    self._need(e, o.sem, o.cnt)
            for sb in self.dma_bufs:
                if sb.dcnt > 0:
                    self._need(e, sb.dsem, sb.dcnt)


class T:
    def __init__(self, k, stack, name, shape, dt, psum=False):
        nc = k.nc
        if psum:
            self.t = stack.enter_context(nc.psum_tensor("t_" + name, list(shape), dt))
        else:
            self.t = stack.enter_context(nc.sbuf_tensor("t_" + name, list(shape), dt))
        self.b = Buf(name)

    def __getitem__(self, key):
        return self.t[key]


def _perm_d():
    i = np.arange(64)
    return ((i % 32) // 16) * 32 + (i // 32) * 16 + (i % 16)


HEAD_OF = [[0, 1, 2, 3, 8, 9, 10, 11], [4, 5, 6, 7, 12, 13, 14, 15]]


def _fm(v):
    v = np.asarray(v, np.float32).reshape(-1, 128)
    return np.ascontiguousarray(v.T)


class VecPack:
    def __init__(self):
        self.cols = []
        self.off = {}
        self.n = 0

    def add(self, name, arr):
        arr = np.asarray(arr, np.float32)
        assert arr.shape[0] == 128
        self.off[name] = (self.n, arr.shape[1])
        self.cols.append(arr)
        self.n += arr.shape[1]

    def build(self):
        return np.ascontiguousarray(np.concatenate(self.cols, axis=1))


def vec_layout():
    names = [("c", 8), ("c_ctx", 8), ("mod_b0", 48), ("mod_b1", 48), ("n1g0", 8), ("n1g1", 8), ("n2g0", 8),
             ("n2g1", 8), ("final_g", 8), ("qg", 1), ("kg", 1), ("mix", 48), ("w0", 16), ("a0", 16), ("k_k", 8),
             ("k_a", 8), ("r_k", 8), ("ln_g", 8), ("ln_b", 8)]
    off = {}
    n = 0
    for nm, w in names:
        off[nm] = (n, w)
        n += w
    return off, n


def pack_vecs(inp, b):
    pd = _perm_d()
    vp = VecPack()
    vp.add("c", _fm(inp["c"][b]))
    vp.add("c_ctx", _fm(inp["c_ctx"]))
    vp.add("mod_b0", _fm(inp["mod_b"][0]))
    vp.add("mod_b1", _fm(inp["mod_b"][1]))
    vp.add("n1g0", _fm(inp["norm1_g"][0]))
    vp.add("n1g1", _fm(inp["norm1_g"][1]))
    vp.add("n2g0", _fm(inp["norm2_g"][0]))
    vp.add("n2g1", _fm(inp["norm2_g"][1]))
    vp.add("final_g", _fm(inp["final_g"]))
    vp.add("qg", np.tile(inp["attn_q_gain"][0][pd], 2).reshape(128, 1))
    vp.add("kg", np.tile(inp["attn_k_gain"][0][pd], 2).reshape(128, 1))
    vp.add("mix", np.concatenate([_fm(inp["rwkv_mix"][0][j]) for j in range(6)], axis=1))
    vp.add("w0", np.concatenate([_fm(inp["rwkv_w0"][0][d]) for d in range(2)], axis=1))
    vp.add("a0", np.concatenate([_fm(inp["rwkv_a0"][0][d]) for d in range(2)], axis=1))
    vp.add("k_k", _fm(inp["rwkv_k_k"][0]))
    vp.add("k_a", _fm(inp["rwkv_k_a"][0]))
    vp.add("r_k", _fm(inp["rwkv_r_k"][0].reshape(-1)))
    vp.add("ln_g", _fm(inp["rwkv_ln_g"][0]))
    vp.add("ln_b", _fm(inp["rwkv_ln_b"][0]))
    off, n = vec_layout()
    assert vp.off == off
    return vp.build()


def make_consts(L):
    c = {}
    c["ident"] = np.eye(128, dtype=np.float32)
    p = np.zeros((128, 128), np.float32)
    for m in range(128):
        p[m ^ 32, m] = 1.0
    c["perm32"] = p
    bo = np.zeros((128, 128), np.float32)
    bo[:64, :64] = 1.0
    bo[64:, 64:] = 1.0
    c["blockones"] = bo
    c["allones"] = np.ones((128, 128), np.float32)
    cm = np.concatenate([c["ident"], c["perm32"], c["blockones"], c["allones"]], axis=1)
    t = np.arange(L)
    row = (t // 64).astype(np.float32)
    col = (t % 64).astype(np.float32)
    inv_freq = (np.float32(10000.0) ** (-np.arange(16, dtype=np.float32) / np.float32(16))).astype(np.float32)
    ang = np.stack([row[:, None] * inv_freq, col[:, None] * inv_freq], axis=1)
    cos = np.cos(ang).astype(np.float32)
    sin = np.sin(ang).astype(np.float32)
    ct = np.zeros((128, L), np.float32)
    stb = np.zeros((128, L), np.float32)
    for r in range(128):
        i = r % 64
        a = (i % 32) // 16
        pp = i % 16
        ct[r] = cos[:, a, pp]
        stb[r] = -sin[:, a, pp] if i < 32 else sin[:, a, pp]
    return np.ascontiguousarray(cm), ct, stb


def make_masks():
    i = np.arange(64)
    us = (i[:, None] < i[None, :]).astype(np.float32)
    ui = (i[:, None] <= i[None, :]).astype(np.float32)
    m = np.concatenate([us, ui, us.T, ui.T], axis=1)
    return np.ascontiguousarray(np.concatenate([m, m], axis=0))


def host_weights(inp):
    pd = _perm_d()
    wqkv = np.asarray(inp["attn_wqkv"][0], np.float32)
    qcols = []
    for c in range(8):
        for s in range(2):
            h = HEAD_OF[s][c]
            qcols.append(h * 64 + pd)
    qcols = np.concatenate(qcols)
    kcols = np.concatenate([1024 + g * 64 + pd for g in range(4)])
    vcols = 1280 + np.arange(256)
    wqkv_p = np.ascontiguousarray(wqkv[:, np.concatenate([qcols, kcols, vcols])])
    orows = np.concatenate([HEAD_OF[s][c] * 64 + np.arange(64) for c in range(8) for s in range(2)])
    wo_p = np.ascontiguousarray(np.asarray(inp["attn_wo"][0], np.float32)[orows, :])
    return wqkv_p, wo_p


def build(L, stage=99):
    TT = CL + L
    NKT = TT // 128
    nc = bass.Bass("TRN2", target_bir_lowering=False)
    voff, NV = vec_layout()

    def din(name, shape, dt=F32):
        return nc.dram_tensor(name, list(shape), dt, kind="ExternalInput").ap()

    def dscr(name, shape, dt=F32):
        return nc.dram_tensor(name, list(shape), dt).ap()

    x_d = din("x", [L, D])
    ctx_d = din("ctx", [CL, D])
    vecs_d = din("vecs", [128, NV])
    consts_d = din("consts", [128, 512])
    ctab_d = din("ctab", [128, L])
    stab_d = din("stab", [128, L])
    modw_d = din("mod_w", [2, D, 6 * D])
    wqkv_d = din("wqkv", [D, 1536])
    wo0_d = din("wo0", [D, D])
    wg_d = din("ffn_wg", [2, D, DFF])
    wu_d = din("ffn_wu", [2, D, DFF])
    wd_d = din("ffn_wd", [2, DFF, D])
    wrkv_d = din("wrkv", [3, D, D])
    wo1_d = din("wo1", [D, D])
    w1c_d = din("w1c", [D, 128])
    a1c_d = din("a1c", [D, 128])
    g1_d = din("g1", [D, 128])
    w2c_d = din("w2c", [128, D])
    a2c_d = din("a2c", [128, D])
    g2_d = din("g2", [128, D])
    masks_d = din("masks", [128, 256])
    out_d = nc.dram_tensor("out", [L, D], F32, kind="ExternalOutput").ap()

    xT_d = dscr("xT_s", [NCH, 128, TT])
    qT_d = dscr("qT_s", [NCH, 128, TT], BF16)
    kT_d = dscr("kT_s", [2, 128, TT], BF16)
    v_d = dscr("v_s", [TT, 256], BF16)
    x1T_d = dscr("x1T_s", [NCH, 128, TT])
    x2T_d = dscr("x2T_s", [NCH, 128, TT])
    h1c_d = dscr("h1c_s", [NCH, 128, CL + 2])
    h1l_d = dscr("h1l_s", [NCH, 128, L + 2])
    yf_d = dscr("yf_s", [NCH, 128, L])
    x3T_d = dscr("x3T_s", [NCH, 128, TT])

    st = contextlib.ExitStack()
    with st:
        k = K(nc, st)

        uniq = {"n": 0}

        def tl(name, shape, dt, stack=None, psum=False):
            uniq["n"] += 1
            return T(k, stack or st, f"{name}_{uniq['n']}", shape, dt, psum=psum)

        vecs = tl("vecs", [128, NV], F32)
        k.dma(k.sp, vecs[:, :], vecs_d[:, :], vecs.b, writes=[vecs.b])
        cst = tl("cst", [128, 512], F32)
        k.dma(k.sp, cst[:, :], consts_d[:, :], cst.b, writes=[cst.b])
        cbf = tl("cbf", [128, 512], BF16)
        k.op(k.dve, lambda: nc.vector.tensor_copy(out=cbf[:, :], in_=cst[:, :]), reads=[cst.b], writes=[cbf.b])
        ident = cst[:, 0:128]
        perm_bf = cbf[:, 128:256]
        bones_bf = cbf[:, 256:384]
        ones_bf = cbf[:, 384:512]
        modv = tl("modv", [128, 2, 2, 6, 8], F32)
        qg8 = tl("qg8", [128, 2], F32)

        def V(name, i=0, n=1):
            o, w = voff[name]
            return vecs[:, o + i:o + i + n]

        k.op(k.dve, lambda: nc.vector.tensor_scalar(out=qg8[:, 0:1], in0=V("qg"), scalar1=0.125, scalar2=None,
                                                    op0=ALU.mult), reads=[vecs.b], writes=[qg8.b])
        k.op(k.dve, lambda: nc.vector.tensor_copy(out=qg8[:, 1:2], in_=V("kg")), reads=[vecs.b], writes=[qg8.b])

        psA = [tl(f"psA{i}", [128, 512], F32, psum=True) for i in range(3)]
        psB = [tl(f"psB{i}", [128, 512], F32, psum=True) for i in range(2)]
        psC = [tl(f"psC{i}", [128, 512], F32, psum=True) for i in range(3)]
        rr = {"A": 0, "B": 0, "C": 0}

        def nxt(which):
            lst = {"A": psA, "B": psB, "C": psC}[which]
            i = rr[which]
            rr[which] = (i + 1) % len(lst)
            return lst[i]

        with contextlib.ExitStack() as s0:
            sc = tl("sc", [128, 8, 2], F32, s0)
            k.op(k.act, lambda: nc.scalar.activation(out=sc[:, :, 0], in_=V("c", 0, 8), func=AF.Silu),
                 reads=[vecs.b], writes=[sc.b])
            k.op(k.act, lambda: nc.scalar.activation(out=sc[:, :, 1], in_=V("c_ctx", 0, 8), func=AF.Silu),
                 reads=[vecs.b], writes=[sc.b])
            mst = [tl(f"mst{i}", [128, 6 * D], F32, s0) for i in range(2)]
            macc = tl("macc", [128, 96], F32, s0)
            for l in range(2):
                for kc in range(8):
                    ms = mst[kc % 2]
                    k.dma(k.sp, ms[:, :], modw_d[l, kc * 128:(kc + 1) * 128, :], ms.b, writes=[ms.b])
                    pm = nxt("C")
                    for n in range(48):
                        k.op(k.pe, lambda: nc.tensor.matmul(pm[:, 2 * n:2 * n + 2], lhsT=ms[:, n * 128:(n + 1) * 128],
                                                            rhs=sc[:, kc, :], start=True, stop=True),
                             reads=[ms.b, sc.b], writes=[pm.b], inc=(n == 47))
                    if kc == 0:
                        k.op(k.dve, lambda: nc.vector.tensor_copy(out=macc[:, :], in_=pm[:, 0:96]),
                             reads=[pm.b], writes=[macc.b])
                    else:
                        k.op(k.dve, lambda: nc.vector.tensor_tensor(out=macc[:, :], in0=macc[:, :], in1=pm[:, 0:96],
                                                                    op=ALU.add),
                             reads=[pm.b, macc.b], writes=[macc.b])
                mb = V(f"mod_b{l}", 0, 48)
                for s in range(2):
                    mv = macc[:, :].rearrange("p (n s) -> p n s", s=2)[:, :, s]
                    tmp = tl(f"mtmp{l}{s}", [128, 48], F32, s0)
                    k.op(k.dve, lambda: nc.vector.tensor_tensor(out=tmp[:, :], in0=mv, in1=mb, op=ALU.add),
                         reads=[macc.b, vecs.b], writes=[tmp.b])
                    for (dst, src) in ((0, 0), (2, 2), (3, 3), (5, 5)):
                        k.op(k.dve, lambda: nc.vector.tensor_copy(out=modv[:, l, s, dst, :],
                                                                  in_=tmp[:, src * 8:(src + 1) * 8]),
                             reads=[tmp.b], writes=[modv.b])
                    for (dst, src, gname) in ((1, 1, f"n1g{l}"), (4, 4, f"n2g{l}")):
                        k.op(k.dve, lambda: nc.vector.scalar_tensor_tensor(
                            out=modv[:, l, s, dst, :], in0=tmp[:, src * 8:(src + 1) * 8], scalar=1.0,
                            in1=V(gname, 0, 8), op0=ALU.add, op1=ALU.mult),
                            reads=[tmp.b, vecs.b], writes=[modv.b])

        k.barrier()
        def load_weight_bf16(dst, dst_view_fn, src_rows_fn, nrow_tiles, ncols, stg, engs):
            for i in range(nrow_tiles):
                sg = stg[i % len(stg)]
                k.dma(k.sp, sg[:, 0:ncols], src_rows_fn(i), sg.b, writes=[sg.b])
                e = engs[i % len(engs)]
                if e is k.act:
                    k.op(e, lambda: nc.scalar.copy(out=dst_view_fn(i), in_=sg[:, 0:ncols]), reads=[sg.b], writes=[dst.b])
                elif e is k.dve:
                    k.op(e, lambda: nc.vector.tensor_copy(out=dst_view_fn(i), in_=sg[:, 0:ncols]), reads=[sg.b],
                         writes=[dst.b])
                else:
                    k.op(e, lambda: nc.gpsimd.tensor_copy(out=dst_view_fn(i), in_=sg[:, 0:ncols]), reads=[sg.b],
                         writes=[dst.b])

        def rms_stats(xT, N, sq, rstd):
            for c in range(8):
                k.op(k.act, lambda: nc.scalar.activation(out=sq[:, c, 0:N], in_=xT[:, c, 0:N], func=AF.Square),
                     reads=[xT.b], writes=[sq.b])
            ps = nxt("C")
            for c in range(8):
                k.op(k.pe, lambda: nc.tensor.matmul(ps[:, 0:N], lhsT=ones_bf, rhs=sq[:, c, 0:N], start=(c == 0),
                                                    stop=(c == 7)),
                     reads=[sq.b, cbf.b], writes=[ps.b], inc=(c == 7))
            k.op(k.act, lambda: nc.scalar.activation(out=rstd[:, 0:N], in_=ps[:, 0:N], func=AF.Sqrt, scale=1.0 / D,
                                                     bias=EPS),
                 reads=[ps.b], writes=[rstd.b])
            k.op(k.dve, lambda: nc.vector.reciprocal(out=rstd[:, 0:N], in_=rstd[:, 0:N]), reads=[rstd.b],
                 writes=[rstd.b])

        def norm_mod(xT, N, l, s, which, hT, sq, rstd, tmp):
            rms_stats(xT, N, sq, rstd)
            jsh, jgm = (0, 1) if which == 0 else (3, 4)
            for c in range(8):
                tb = tmp[c % len(tmp)]
                k.op(k.dve, lambda: nc.vector.scalar_tensor_tensor(
                    out=tb[:, 0:N], in0=xT[:, c, 0:N], scalar=modv[:, l, s, jgm, c:c + 1], in1=rstd[:, 0:N],
                    op0=ALU.mult, op1=ALU.mult), reads=[xT.b, modv.b, rstd.b], writes=[tb.b])
                k.op(k.pool, lambda: nc.gpsimd.tensor_scalar(
                    out=hT[:, c, 0:N], in0=tb[:, 0:N], scalar1=modv[:, l, s, jsh, c:c + 1], scalar2=None,
                    op0=ALU.add), reads=[tb.b, modv.b], writes=[hT.b])

        def ffn_phase(l, src_d, dst_d, blocks, post_fn=None):
            with contextlib.ExitStack() as sp_:
                wg = tl("wg", [128, 8, DFF], BF16, sp_)
                wu = tl("wu", [128, 8, DFF], BF16, sp_)
                wd = tl("wd", [128, NF, D], BF16, sp_)
                with contextlib.ExitStack() as s2:
                    stg = [tl(f"f_stg{i}", [128, DFF], F32, s2) for i in range(2)]
                    engs = [k.dve, k.pool, k.act]
                    load_weight_bf16(wg, lambda i: wg[:, i, :], lambda i: wg_d[l, i * 128:(i + 1) * 128, :], 8, DFF,
                                     stg, engs)
                    load_weight_bf16(wu, lambda i: wu[:, i, :], lambda i: wu_d[l, i * 128:(i + 1) * 128, :], 8, DFF,
                                     stg, engs)
                    load_weight_bf16(wd, lambda i: wd[:, i, :], lambda i: wd_d[l, i * 128:(i + 1) * 128, :], NF, D,
                                     stg, engs)
                k.barrier()
                xT = tl("f_xT", [128, 8, 512], F32, sp_)
                hT = tl("f_hT", [128, 8, 512], BF16, sp_)
                aT = tl("f_aT", [128, NF, 512], BF16, sp_)
                rstd = tl("f_rstd", [128, 512], F32, sp_)
                tmp = [tl(f"f_tmp{i}", [128, 512], F32, sp_) for i in range(2)]
                sg = [tl(f"f_sg{i}", [128, 512], F32, sp_) for i in range(2)]
                for (t0, N, s) in blocks:
                    k.dma(k.sp, xT[:, :, 0:N], src_d[:, :, t0:t0 + N].rearrange("c p t -> p c t"), xT.b,
                          writes=[xT.b])
                    sq = aT
                    norm_mod(xT, N, l, s, 1, hT, T_alias(aT, "sq"), rstd, tmp)
                    for f in range(NF):
                        pg = nxt("A")
                        for c in range(8):
                            k.op(k.pe, lambda: nc.tensor.matmul(pg[:, 0:N], lhsT=wg[:, c, f * 128:(f + 1) * 128],
                                                                rhs=hT[:, c, 0:N], start=(c == 0), stop=(c == 7)),
                                 reads=[wg.b, hT.b], writes=[pg.b], inc=(c == 7))
                        pu = nxt("C")
                        for c in range(8):
                            k.op(k.pe, lambda: nc.tensor.matmul(pu[:, 0:N], lhsT=wu[:, c, f * 128:(f + 1) * 128],
                                                                rhs=hT[:, c, 0:N], start=(c == 0), stop=(c == 7)),
                                 reads=[wu.b, hT.b], writes=[pu.b], inc=(c == 7))
                        sgi = sg[f % 2]
                        k.op(k.act, lambda: nc.scalar.activation(out=sgi[:, 0:N], in_=pg[:, 0:N], func=AF.Silu),
                             reads=[pg.b], writes=[sgi.b])
                        k.op(k.dve, lambda: nc.vector.tensor_tensor(out=aT[:, f, 0:N], in0=sgi[:, 0:N],
                                                                    in1=pu[:, 0:N], op=ALU.mult),
                             reads=[sgi.b, pu.b], writes=[aT.b])
                    for n in range(8):
                        pd_ = nxt("B")
                        for f in range(NF):
                            k.op(k.pe, lambda: nc.tensor.matmul(pd_[:, 0:N], lhsT=wd[:, f, n * 128:(n + 1) * 128],
                                                                rhs=aT[:, f, 0:N], start=(f == 0), stop=(f == NF - 1)),
                                 reads=[wd.b, aT.b], writes=[pd_.b], inc=(f == NF - 1))
                        k.op(k.dve, lambda: nc.vector.scalar_tensor_tensor(
                            out=xT[:, n, 0:N], in0=pd_[:, 0:N], scalar=modv[:, l, s, 5, n:n + 1], in1=xT[:, n, 0:N],
                            op0=ALU.mult, op1=ALU.add), reads=[pd_.b, modv.b, xT.b], writes=[xT.b])
                    if dst_d is not None:
                        k.dma(k.pool, dst_d[:, :, t0:t0 + N].rearrange("c p t -> p c t"), xT[:, :, 0:N], xT.b,
                              reads=[xT.b])
                    if post_fn is not None:
                        post_fn(t0, N, s, xT, hT, T_alias(aT, "sq"), rstd, tmp, sg)

        def T_alias(t, name):
            return t


        def rwkv_phase():
            NB = 256
            C0 = float(np.exp(-0.5))
            with contextlib.ExitStack() as s1:
                Wr = tl("Wr", [128, 8, D], BF16, s1)
                Wk = tl("Wk", [128, 8, D], BF16, s1)
                Wv = tl("Wv", [128, 8, D], BF16, s1)
                wo1 = tl("wo1", [128, 8, D], BF16, s1)
                w1c = tl("w1c", [128, 8, 128], BF16, s1)
                a1c = tl("a1c", [128, 8, 128], BF16, s1)
                g1 = tl("g1", [128, 8, 128], BF16, s1)
                w2c = tl("w2c", [128, D], BF16, s1)
                a2c = tl("a2c", [128, D], BF16, s1)
                g2 = tl("g2", [128, D], BF16, s1)
                with contextlib.ExitStack() as s2:
                    stg = [tl(f"r_stg{i}", [128, D], F32, s2) for i in range(2)]
                    engs = [k.dve, k.pool, k.act]
                    for (wt, j) in ((Wr, 0), (Wk, 1), (Wv, 2)):
                        load_weight_bf16(wt, lambda i: wt[:, i, :], lambda i: wrkv_d[j, i * 128:(i + 1) * 128, :], 8, D,
                                         stg, engs)
                    load_weight_bf16(wo1, lambda i: wo1[:, i, :], lambda i: wo1_d[i * 128:(i + 1) * 128, :], 8, D, stg,
                                     engs)
                    for (wt, src) in ((w1c, w1c_d), (a1c, a1c_d), (g1, g1_d)):
                        load_weight_bf16(wt, lambda i: wt[:, i, :], lambda i: src[i * 128:(i + 1) * 128, :], 8, 128, stg,
                                         engs)
                    for (wt, src) in ((w2c, w2c_d), (a2c, a2c_d), (g2, g2_d)):
                        load_weight_bf16(wt, lambda i: wt[:, :], lambda i: src[:, :], 1, D, stg, engs)
                k.barrier()
                msk = tl("msk", [128, 256], F32, s1)
                k.dma(k.sp, msk[:, :], masks_d[:, :], msk.b, writes=[msk.b])
                rst = tl("rst", [128, NB], F32, s1)
                k.op(k.pool, lambda: nc.gpsimd.memset(rst[:, :], 1.0), writes=[rst.b])
                k.op(k.pool, lambda: nc.gpsimd.memset(rst[:, :].rearrange("p (c t) -> p c t", t=64)[:, :, 0:1], 0.0),
                     writes=[rst.b])
                omka = tl("omka", [128, 8], F32, s1)
                k.op(k.dve, lambda: nc.vector.tensor_scalar(out=omka[:, :], in0=V("k_a", 0, 8), scalar1=-1.0,
                                                            scalar2=1.0, op0=ALU.mult, op1=ALU.add),
                     reads=[vecs.b], writes=[omka.b])
                ident_bf = cbf[:, 0:128]

                U = tl("r_U", [128, 8, NB + 2], F32, s1)
                XX = tl("r_XX", [128, 8, NB], F32, s1)
                LB = tl("r_LB", [128, 8, NB], BF16, s1)
                Kt = tl("r_Kt", [128, 8, NB], F32, s1)
                Vt = tl("r_Vt", [128, 8, NB], BF16, s1)
                Rt = tl("r_Rt", [128, 8, NB], BF16, s1)
                OB = tl("r_OB", [128, 8, NB], BF16, s1)
                tw = tl("r_tw", [128, NB], BF16, s1)
                ta = tl("r_ta", [128, NB], BF16, s1)
                tg = tl("r_tg", [128, NB], BF16, s1)

                def f32t(nm, n=1):
                    return [tl(f"r_{nm}{i}", [128, NB], F32, s1) for i in range(n)]

                def dbl(lst):
                    return lst if len(lst) == 2 else [lst[0], lst[0]]

                sig = dbl(f32t("sig", 1)); aa = dbl(f32t("aa", 1)); ao = dbl(f32t("ao", 1)); kkr = dbl(f32t("kkr", 1))
                rn = dbl(f32t("rn", 1)); kkn = dbl(f32t("kkn", 1)); kd = f32t("kd", 2); bet = dbl(f32t("bet", 1))
                cum = dbl(f32t("cum", 1)); xe = dbl(f32t("xe", 1)); xi = dbl(f32t("xi", 1)); Ee = dbl(f32t("Ee", 1))
                ai = dbl(f32t("ai", 1)); ri = dbl(f32t("ri", 1)); t3 = dbl(f32t("t3", 1)); yf = f32t("yf", 2)
                gn = dbl(f32t("gn", 1)); x2b = f32t("x2b", 2)
                sqb = [tl(f"r_sqb{i}", [128, NB], BF16, s1) for i in range(2)]
                wtot = [tl(f"r_wtot{i}", [128, 4], F32, s1) for i in range(2)]
                wt0 = [tl(f"r_wt0{i}", [64, 4], F32, s1) for i in range(4)]
                AR = [tl(f"r_AR{i}", [128, 4, 128], BF16, s1) for i in range(2)]
                BK = [tl(f"r_BK{i}", [128, 4, 128], BF16, s1) for i in range(2)]
                TMa = [tl(f"r_TMa{i}", [64, 4, 128], BF16, s1) for i in range(2)]
                Vtm = [tl(f"r_Vtm{i}", [64, 4, 128], BF16, s1) for i in range(2)]
                NSET = 4
                Xs = [[tl(f"r_X{g}_{i}", [64, 4, 192], BF16, s1) for i in range(2)] for g in range(NSET)]
                Ws = [[tl(f"r_W{g}_{i}", [64, 4, 128], BF16, s1) for i in range(2)] for g in range(NSET)]
                AKT = [tl(f"r_AKT{i}", [64, 4, 64], BF16, s1) for i in range(NSET)]
                KA = [tl(f"r_KA{i}", [64, 4, 128], F32, s1) for i in range(NSET)]
                M1 = [tl(f"r_M1{i}", [64, 4, 64], F32, s1) for i in range(NSET)]
                QT = [tl(f"r_QT{i}", [64, 4, 64], F32, s1) for i in range(NSET)]
                ZN = [tl(f"r_ZN{i}", [64, 4, 128], BF16, s1) for i in range(NSET)]
                STP = [[tl(f"r_STP{g}_{i}", [64, 64], F32, s1) for i in range(2)] for g in range(NSET)]
                RC0 = [tl(f"r_RC0{i}", [64, 4, 64], BF16, s1) for i in range(NSET)]
                STh = [tl(f"r_STh{i}", [64, 64], F32, s1) for i in range(16)]
                cnt = {"u": 0}

                def bc4(ap2d):
                    return ap2d.unsqueeze(1).to_broadcast([ap2d.shape[0], 4, ap2d.shape[1]])

                def lerp(j, N):
                    for c in range(8):
                        e = k.dve
                        eng = nc.vector
                        k.op(e, lambda: eng.scalar_tensor_tensor(
                            out=LB[:, c, 0:N], in0=XX[:, c, 0:N], scalar=V("mix", j * 8 + c, 1), in1=U[:, c, 1:N + 1],
                            op0=ALU.mult, op1=ALU.add), reads=[XX.b, U.b, vecs.b], writes=[LB.b])

                def proj_full(Wt, dst, N):
                    for n in range(8):
                        ps = nxt("A")
                        for c in range(8):
                            k.op(k.pe, lambda: nc.tensor.matmul(ps[:, 0:N], lhsT=Wt[:, c, n * 128:(n + 1) * 128],
                                                                rhs=LB[:, c, 0:N], start=(c == 0), stop=(c == 7)),
                                 reads=[Wt.b, LB.b], writes=[ps.b], inc=(c == 7))
                        k.op(k.act, lambda: nc.scalar.copy(out=dst[:, n, 0:N], in_=ps[:, 0:N]), reads=[ps.b],
                             writes=[dst.b])

                def proj_lora(Wt, dst, N, func):
                    ps = nxt("A")
                    for c in range(8):
                        k.op(k.pe, lambda: nc.tensor.matmul(ps[:, 0:N], lhsT=Wt[:, c, :], rhs=LB[:, c, 0:N],
                                                            start=(c == 0), stop=(c == 7)),
                             reads=[Wt.b, LB.b], writes=[ps.b], inc=(c == 7))
                    k.op(k.act, lambda: nc.scalar.activation(out=dst[:, 0:N], in_=ps[:, 0:N], func=func),
                         reads=[ps.b], writes=[dst.b])


                def block(t0, N, s_, d, sweepB):
                    readout = (s_ == 0)
                    src = h1c_d if s_ == 1 else h1l_d
                    r0 = t0 if s_ == 1 else t0 - CL
                    nch = N // 64
                    k.dma(k.sp, U[:, :, 0:N + 2], src[:, :, r0:r0 + N + 2].rearrange("c p t -> p c t"), U.b,
                          writes=[U.b])
                    k.op(k.dve, lambda: nc.vector.tensor_tensor(out=XX[:, :, 0:N], in0=U[:, :, 0:N], in1=U[:, :, 2:N + 2],
                                                                op=ALU.add), reads=[U.b], writes=[XX.b])
                    k.op(k.dve, lambda: nc.vector.scalar_tensor_tensor(out=XX[:, :, 0:N], in0=XX[:, :, 0:N], scalar=0.5,
                                                                       in1=U[:, :, 1:N + 1], op0=ALU.mult,
                                                                       op1=ALU.subtract), reads=[XX.b, U.b],
                         writes=[XX.b])
                    lerp(2, N); proj_full(Wk, Kt, N)
                    lerp(3, N); proj_full(Wv, Vt, N)
                    if readout:
                        lerp(0, N); proj_full(Wr, Rt, N)
                    else:
                        k.op(k.pool, lambda: nc.gpsimd.memset(Rt[:, :, :], 0.0), writes=[Rt.b])
                    lerp(1, N); proj_lora(w1c, tw, N, AF.Tanh)
                    lerp(4, N); proj_lora(a1c, ta, N, AF.Copy)
                    if sweepB and readout:
                        lerp(5, N); proj_lora(g1, tg, N, AF.Sigmoid)
                    if d == 0:
                        m_lt, m_le, m_ltT = msk[:, 0:64], msk[:, 64:128], msk[:, 128:192]
                    else:
                        m_lt, m_le, m_ltT = msk[:, 128:192], msk[:, 192:256], msk[:, 0:64]
                    db = slice(d * 64, (d + 1) * 64)
                    ob_ = slice((1 - d) * 64, (2 - d) * 64)
                    def hp_prep(hp):
                        i2 = hp % 2
                        hs = slice(hp * 128, (hp + 1) * 128)
                        pz = nxt("A")
                        k.op(k.pe, lambda: nc.tensor.matmul(pz[:, 0:N], lhsT=w2c[db, hs], rhs=tw[db, 0:N], start=True,
                                                            stop=True), reads=[w2c.b, tw.b], writes=[pz.b])
                        k.op(k.act, lambda: nc.scalar.activation(out=sig[i2][:, 0:N], in_=pz[:, 0:N], func=AF.Sigmoid,
                                                                 bias=V("w0", d * 8 + hp, 1)), reads=[pz.b, vecs.b],
                             writes=[sig[i2].b])
                        pa = nxt("A")
                        k.op(k.pe, lambda: nc.tensor.matmul(pa[:, 0:N], lhsT=a2c[db, hs], rhs=ta[db, 0:N], start=True,
                                                            stop=True), reads=[a2c.b, ta.b], writes=[pa.b])
                        k.op(k.act, lambda: nc.scalar.activation(out=aa[i2][:, 0:N], in_=pa[:, 0:N], func=AF.Sigmoid,
                                                                 bias=V("a0", d * 8 + hp, 1)), reads=[pa.b, vecs.b],
                             writes=[aa[i2].b])
                        k.op(k.dve, lambda: nc.vector.tensor_scalar(out=kkr[i2][:, 0:N], in0=Kt[:, hp, 0:N],
                                                                    scalar1=V("k_k", hp, 1), scalar2=None,
                                                                    op0=ALU.mult), reads=[Kt.b, vecs.b],
                             writes=[kkr[i2].b])
                        k.op(k.act, lambda: nc.scalar.activation(out=sqb[i2][:, 0:N], in_=kkr[i2][:, 0:N],
                                                                 func=AF.Square), reads=[kkr[i2].b], writes=[sqb[i2].b])
                        pn = nxt("C")
                        k.op(k.pe, lambda: nc.tensor.matmul(pn[:, 0:N], lhsT=bones_bf, rhs=sqb[i2][:, 0:N], start=True,
                                                            stop=True), reads=[sqb[i2].b, cbf.b], writes=[pn.b])
                        k.op(k.dve, lambda: nc.vector.tensor_scalar(out=rn[i2][:, 0:N], in0=pn[:, 0:N], scalar1=1e-24,
                                                                    scalar2=None, op0=ALU.max), reads=[pn.b],
                             writes=[rn[i2].b])
                        k.op(k.act, lambda: nc.scalar.activation(out=rn[i2][:, 0:N], in_=rn[i2][:, 0:N], func=AF.Sqrt),
                             reads=[rn[i2].b], writes=[rn[i2].b])
                        k.op(k.dve, lambda: nc.vector.reciprocal(out=rn[i2][:, 0:N], in_=rn[i2][:, 0:N]),
                             reads=[rn[i2].b], writes=[rn[i2].b])
                        k.op(k.dve, lambda: nc.vector.tensor_tensor(out=kkn[i2][:, 0:N], in0=kkr[i2][:, 0:N],
                                                                    in1=rn[i2][:, 0:N], op=ALU.mult),
                             reads=[kkr[i2].b, rn[i2].b], writes=[kkn[i2].b])
                        k.op(k.pool, lambda: nc.gpsimd.tensor_scalar(out=t3[i2][:, 0:N], in0=aa[i2][:, 0:N],
                                                                     scalar1=V("k_a", hp, 1), scalar2=omka[:, hp:hp + 1],
                                                                     op0=ALU.mult, op1=ALU.add),
                             reads=[aa[i2].b, vecs.b, omka.b], writes=[t3[i2].b])
                        k.op(k.pool, lambda: nc.gpsimd.tensor_tensor(out=kd[i2][:, 0:N], in0=t3[i2][:, 0:N],
                                                                     in1=Kt[:, hp, 0:N], op=ALU.mult),
                             reads=[t3[i2].b, Kt.b], writes=[kd[i2].b])
                        k.op(k.pool, lambda: nc.gpsimd.tensor_tensor(out=bet[i2][:, 0:N], in0=kkn[i2][:, 0:N],
                                                                     in1=aa[i2][:, 0:N], op=ALU.mult),
                             reads=[kkn[i2].b, aa[i2].b], writes=[bet[i2].b])
                        k.op(k.dve, lambda: nc.vector.tensor_tensor_scan(out=cum[i2][:, 0:N], data0=rst[:, 0:N],
                                                                         data1=sig[i2][:, 0:N], initial=0.0,
                                                                         op0=ALU.mult, op1=ALU.add),
                             reads=[rst.b, sig[i2].b], writes=[cum[i2].b])
                        c3 = cum[i2][:, 0:N].rearrange("p (c t) -> p c t", t=64)
                        if d == 0:
                            k.op(k.dve, lambda: nc.vector.tensor_tensor(
                                out=xe[i2][:, 0:N].rearrange("p (c t) -> p c t", t=64),
                                in0=c3[:, :, 63:64].to_broadcast([128, nch, 64]), in1=c3, op=ALU.subtract),
                                reads=[cum[i2].b], writes=[xe[i2].b])
                            k.op(k.dve, lambda: nc.vector.tensor_tensor(out=xi[i2][:, 0:N], in0=xe[i2][:, 0:N],
                                                                        in1=sig[i2][:, 0:N], op=ALU.add),
                                 reads=[xe[i2].b, sig[i2].b], writes=[xi[i2].b])
                            xe_, xi_ = xe[i2], xi[i2]
                        else:
                            k.op(k.dve, lambda: nc.vector.tensor_tensor(out=xe[i2][:, 0:N], in0=cum[i2][:, 0:N],
                                                                        in1=sig[i2][:, 0:N], op=ALU.subtract),
                                 reads=[cum[i2].b, sig[i2].b], writes=[xe[i2].b])
                            xe_, xi_ = xe[i2], cum[i2]
                        k.op(k.act, lambda: nc.scalar.activation(out=Ee[i2][:, 0:N], in_=xe_[:, 0:N], func=AF.Exp,
                                                                 scale=-C0), reads=[xe_.b], writes=[Ee[i2].b])
                        k.op(k.act, lambda: nc.scalar.activation(out=ai[i2][:, 0:N], in_=xi_[:, 0:N], func=AF.Exp,
                                                                 scale=C0), reads=[xi_.b], writes=[ai[i2].b])
                        k.op(k.act, lambda: nc.scalar.activation(out=ri[i2][:, 0:N], in_=xe_[:, 0:N], func=AF.Exp,
                                                                 scale=C0), reads=[xe_.b], writes=[ri[i2].b])
                        k.op(k.act, lambda: nc.scalar.activation(out=wtot[i2][:, 0:nch], in_=c3[:, :, 63], func=AF.Exp,
                                                                 scale=-C0), reads=[cum[i2].b], writes=[wtot[i2].b])
                        ARv, BKv = AR[i2], BK[i2]

                        def v3(t_):
                            return t_[:, 0:N].rearrange("p (c t) -> p c t", t=64)

                        k.op(k.dve, lambda: nc.vector.scalar_tensor_tensor(out=ARv[:, 0:nch, 0:64], in0=v3(kkn[i2]),
                                                                           scalar=-1.0, in1=v3(ai[i2]), op0=ALU.mult,
                                                                           op1=ALU.mult), reads=[kkn[i2].b, ai[i2].b],
                             writes=[ARv.b])
                        k.op(k.pool, lambda: nc.gpsimd.tensor_tensor(
                            out=ARv[:, 0:nch, 64:128], in0=Rt[:, hp, 0:N].rearrange("p (c t) -> p c t", t=64),
                            in1=v3(ri[i2]), op=ALU.mult), reads=[Rt.b, ri[i2].b], writes=[ARv.b])
                        k.op(k.dve, lambda: nc.vector.tensor_tensor(out=BKv[:, 0:nch, 0:64], in0=v3(bet[i2]),
                                                                    in1=v3(Ee[i2]), op=ALU.mult),
                             reads=[bet[i2].b, Ee[i2].b], writes=[BKv.b])
                        k.op(k.pool, lambda: nc.gpsimd.tensor_tensor(out=BKv[:, 0:nch, 64:128], in0=v3(kd[i2]),
                                                                     in1=v3(Ee[i2]), op=ALU.mult),
                             reads=[kd[i2].b, Ee[i2].b], writes=[BKv.b])
                        pta = nxt("C"); ptb = nxt("C"); ptk = nxt("C"); ptv = nxt("A")
                        ptab = pta[0:64, :].bitcast(BF16).rearrange("p (c t) -> p c t", t=256)
                        ptbb = ptb[0:64, :].bitcast(BF16).rearrange("p (c t) -> p c t", t=256)
                        ptkb = ptk[0:64, :].bitcast(BF16).rearrange("p (c t) -> p c t", t=256)
                        ptvb = ptv[0:64, :].bitcast(BF16).rearrange("p (c t) -> p c t", t=256)
                        for ci in range(nch):
                            last = (ci == nch - 1)
                            k.op(k.pe, lambda: nc.tensor.transpose(ptab[:, ci, 0:128], ARv[:, ci, 0:64], ident_bf),
                                 reads=[ARv.b, cbf.b], writes=[pta.b], inc=last)
                            k.op(k.pe, lambda: nc.tensor.transpose(ptbb[:, ci, 0:128], BKv[:, ci, 0:64], ident_bf),
                                 reads=[BKv.b, cbf.b], writes=[ptb.b], inc=last)
                            k.op(k.pe, lambda: nc.tensor.transpose(ptkb[:, ci, 0:128], BKv[:, ci, 64:128], ident_bf),
                                 reads=[BKv.b, cbf.b], writes=[ptk.b], inc=last)
                            k.op(k.pe, lambda: nc.tensor.transpose(ptvb[:, ci, 0:128], Vt[:, hp, ci * 64:(ci + 1) * 64],
                                                                   ident_bf),
                                 reads=[Vt.b, cbf.b], writes=[ptv.b], inc=last)
                        k.op(k.act, lambda: nc.scalar.copy(out=TMa[i2][:, 0:nch, :], in_=ptab[:, 0:nch, 0:128]),
                             reads=[pta.b], writes=[TMa[i2].b])
                        for h_ in range(2):
                            us_ = (i2 * 2 + h_) % NSET
                            hb_ = slice(h_ * 64, (h_ + 1) * 64)
                            k.op(k.act, lambda: nc.scalar.copy(out=Ws[us_][0][:, 0:nch, 0:64], in_=ptbb[:, 0:nch, hb_]),
                                 reads=[ptb.b], writes=[Ws[us_][0].b])
                            k.op(k.dve, lambda: nc.vector.tensor_copy(out=KA[us_][:, 0:nch, 0:64],
                                                                      in_=ptkb[:, 0:nch, hb_]),
                                 reads=[ptk.b], writes=[KA[us_].b])
                        k.op(k.act, lambda: nc.scalar.copy(out=Vtm[i2][:, 0:nch, :], in_=ptvb[:, 0:nch, 0:128]),
                             reads=[ptv.b], writes=[Vtm[i2].b])

                    def head_gen(hp, h, py):
                        i2 = hp % 2
                        hs = slice(hp * 128, (hp + 1) * 128)
                        ARv, BKv = AR[i2], BK[i2]
                        nb_ = [64, nch, 64]
                        hb = slice(h * 64, (h + 1) * 64)
                        u2 = (i2 * 2 + h) % NSET
                        X, W0, W1 = Xs[u2][0], Ws[u2][0], Ws[u2][1]
                        p2 = nxt("A"); p1 = nxt("A")
                        p2v = p2[:, :].rearrange("p (c t) -> p c t", t=128)
                        p1v = p1[0:64, :].rearrange("p (c t) -> p c t", t=128)
                        for ci in range(nch):
                            k.op(k.pe, lambda: nc.tensor.matmul(p2v[:, ci, :], lhsT=BKv[hb, ci, :], rhs=ARv[hb, ci, :],
                                                                start=True, stop=True), reads=[BKv.b, ARv.b],
                                 writes=[p2.b], inc=(ci == nch - 1))
                        for ci in range(nch):
                            k.op(k.pe, lambda: nc.tensor.matmul(p1v[:, ci, :], lhsT=ARv[hb, ci, 0:64],
                                                                rhs=BKv[hb, ci, :], start=True, stop=True),
                                 reads=[BKv.b, ARv.b], writes=[p1.b], inc=(ci == nch - 1))
                        nb_ = [64, nch, 64]
                        k.op(k.dve, lambda: nc.vector.tensor_tensor(
                            out=X[:, 0:nch, 64:128], in0=p2v[0:64, 0:nch, 0:64],
                            in1=m_lt[0:64, :].unsqueeze(1).to_broadcast(nb_), op=ALU.mult),
                            reads=[p2.b, msk.b], writes=[X.b])
                        k.op(k.dve, lambda: nc.vector.tensor_tensor(
                            out=W0[:, 0:nch, 64:128], in0=p2v[0:64, 0:nch, 64:128],
                            in1=m_le[0:64, :].unsqueeze(1).to_broadcast(nb_), op=ALU.mult),
                            reads=[p2.b, msk.b], writes=[W0.b])
                        k.op(k.dve, lambda: nc.vector.tensor_tensor(
                            out=KA[u2][:, 0:nch, 64:128], in0=p2v[64:128, 0:nch, 64:128],
                            in1=m_le[64:128, :].unsqueeze(1).to_broadcast(nb_), op=ALU.mult),
                            reads=[p2.b, msk.b], writes=[KA[u2].b])
                        k.op(k.dve, lambda: nc.vector.tensor_tensor(
                            out=X[:, 0:nch, 0:64], in0=p1v[:, 0:nch, 0:64],
                            in1=m_ltT[0:64, :].unsqueeze(1).to_broadcast(nb_), op=ALU.mult),
                            reads=[p1.b, msk.b], writes=[X.b])
                        k.op(k.pool, lambda: nc.gpsimd.tensor_copy(out=X[:, 0:nch, 128:192], in_=X[:, 0:nch, 0:64]),
                             reads=[X.b], writes=[X.b])
                        k.op(k.dve, lambda: nc.vector.tensor_tensor(
                            out=AKT[u2][:, 0:nch, :], in0=p1v[:, 0:nch, 64:128],
                            in1=m_ltT[0:64, :].unsqueeze(1).to_broadcast(nb_), op=ALU.mult),
                            reads=[p1.b, msk.b], writes=[AKT[u2].b])
                        yield
                        Wc, Wn = W0, W1
                        Xc, Xn = X, Xs[u2][1]
                        for j in range(6):
                            pw = nxt("A")
                            pwv = pw[0:64, :].rearrange("p (c t) -> p c t", t=128)
                            for ci in range(nch):
                                k.op(k.pe, lambda: nc.tensor.matmul(pwv[:, ci, :], lhsT=ident_bf[0:64, 0:64],
                                                                    rhs=Wc[:, ci, :], start=True, stop=False),
                                     reads=[Wc.b, cbf.b], writes=[pw.b], inc=False)
                                k.op(k.pe, lambda: nc.tensor.matmul(pwv[:, ci, :], lhsT=Xc[:, ci, 0:64],
                                                                    rhs=Wc[:, ci, :], start=False, stop=True),
                                     reads=[Wc.b, Xc.b], writes=[pw.b], inc=(ci == nch - 1))
                            k.op(k.act, lambda: nc.scalar.copy(out=Wn[:, 0:nch, :], in_=pwv[:, 0:nch, :]),
                                 reads=[pw.b], writes=[Wn.b])
                            Wc, Wn = Wn, Wc
                            if j < 5:
                                pq_ = nxt("C")
                                pqv = pq_[:, :].rearrange("p (c t) -> p c t", t=128)
                                for ci in range(nch):
                                    k.op(k.pe, lambda: nc.tensor.matmul(pqv[:, ci, :], lhsT=Xc[:, ci, 0:128],
                                                                        rhs=Xc[:, ci, 64:192], start=True, stop=True),
                                         reads=[Xc.b], writes=[pq_.b], inc=(ci == nch - 1))
                                k.op(k.act, lambda: nc.scalar.copy(out=Xn[:, 0:nch, 64:128],
                                                                   in_=pqv[0:64, 0:nch, 0:64]), reads=[pq_.b],
                                     writes=[Xn.b])
                                k.op(k.dve, lambda: nc.vector.tensor_copy(out=Xn[:, 0:nch, 0:64],
                                                                          in_=pqv[64:128, 0:nch, 64:128]),
                                     reads=[pq_.b], writes=[Xn.b])
                                k.op(k.pool, lambda: nc.gpsimd.tensor_copy(out=Xn[:, 0:nch, 128:192],
                                                                           in_=Xn[:, 0:nch, 0:64]),
                                     reads=[Xn.b], writes=[Xn.b])
                                Xc, Xn = Xn, Xc
                            yield
                        pfa = nxt("A"); pfb = nxt("A")
                        pfav = pfa[0:64, :].rearrange("p (c t) -> p c t", t=128)
                        pfbv = pfb[0:64, :].rearrange("p (c t) -> p c t", t=128)
                        for ci in range(nch):
                            k.op(k.pe, lambda: nc.tensor.matmul(pfav[:, ci, :], lhsT=TMa[i2][:, ci, hb],
                                                                rhs=Wc[:, ci, :], start=True, stop=True),
                                 reads=[TMa[i2].b, Wc.b], writes=[pfa.b], inc=(ci == nch - 1))
                        for ci in range(nch):
                            k.op(k.pe, lambda: nc.tensor.matmul(pfbv[:, ci, :], lhsT=AKT[u2][:, ci, :],
                                                                rhs=Wc[:, ci, :], start=True, stop=True),
                                 reads=[AKT[u2].b, Wc.b], writes=[pfb.b], inc=(ci == nch - 1))
                        k.op(k.dve, lambda: nc.vector.tensor_tensor(
                            out=M1[u2][:, 0:nch, :], in0=pfav[:, 0:nch, 0:64],
                            in1=ident[0:64, 0:64].unsqueeze(1).to_broadcast(nb_), op=ALU.add),
                            reads=[pfa.b, cst.b], writes=[M1[u2].b])
                        if h == 0:
                            rcs, rcb = ARv[0:64, 0:nch, 64:128], ARv.b
                        else:
                            k.op(k.dve, lambda: nc.vector.tensor_copy(out=RC0[u2][:, 0:nch, :],
                                                                      in_=ARv[64:128, 0:nch, 64:128]),
                                 reads=[ARv.b], writes=[RC0[u2].b])
                            rcs, rcb = RC0[u2][:, 0:nch, :], RC0[u2].b
                        k.op(k.dve, lambda: nc.vector.tensor_tensor(out=QT[u2][:, 0:nch, :], in0=pfav[:, 0:nch, 64:128],
                                                                    in1=rcs, op=ALU.add),
                             reads=[pfa.b, rcb], writes=[QT[u2].b])
                        k.op(k.dve, lambda: nc.vector.tensor_tensor(out=ZN[u2][:, 0:nch, :], in0=pfbv[:, 0:nch, :],
                                                                    in1=KA[u2][:, 0:nch, :], op=ALU.add),
                             reads=[pfb.b, KA[u2].b], writes=[ZN[u2].b])
                        wti = wt0[u2]
                        k.op(k.dve, lambda: nc.vector.tensor_copy(out=wti[:, 0:nch], in_=wtot[i2][hb, 0:nch]),
                             reads=[wtot[i2].b], writes=[wti.b])
                        yield
                        hh = hp * 2 + h
                        order = range(nch) if d == 0 else range(nch - 1, -1, -1)
                        for oi, ci in enumerate(order):
                            stp = STP[u2][oi % 2]
                            k.op(k.dve, lambda: nc.vector.tensor_scalar(out=stp[:, :], in0=STh[hh][:, :],
                                                                        scalar1=wti[:, ci:ci + 1], scalar2=None,
                                                                        op0=ALU.mult), reads=[STh[hh].b, wti.b],
                                 writes=[stp.b])
                            pst = nxt("C")
                            k.op(k.pe, lambda: nc.tensor.matmul(pst[0:64, 0:64], lhsT=M1[u2][:, ci, :], rhs=stp[:, :],
                                                                start=True, stop=False), reads=[M1[u2].b, stp.b],
                                 writes=[pst.b], inc=False)
                            k.op(k.pe, lambda: nc.tensor.matmul(pst[0:64, 0:64], lhsT=ZN[u2][:, ci, 0:64],
                                                                rhs=Vtm[i2][:, ci, hb], start=False, stop=True),
                                 reads=[ZN[u2].b, Vtm[i2].b], writes=[pst.b])
                            if readout:
                                k.op(k.pe, lambda: nc.tensor.matmul(py[hb, ci * 64:(ci + 1) * 64], lhsT=stp[:, :],
                                                                    rhs=QT[u2][:, ci, :], start=True, stop=False),
                                     reads=[stp.b, QT[u2].b], writes=[py.b], inc=False)
                                k.op(k.pe, lambda: nc.tensor.matmul(py[hb, ci * 64:(ci + 1) * 64],
                                                                    lhsT=Vtm[i2][:, ci, hb],
                                                                    rhs=ZN[u2][:, ci, 64:128], start=False, stop=True),
                                     reads=[Vtm[i2].b, ZN[u2].b], writes=[py.b])
                            k.op(k.act, lambda: nc.scalar.copy(out=STh[hh][:, :], in_=pst[0:64, 0:64]), reads=[pst.b],
                                 writes=[STh[hh].b])
                            yield

                    def hp_readout(hp, py):
                        i2 = hp % 2
                        hs = slice(hp * 128, (hp + 1) * 128)
                        if not readout:
                            return
                        tr0 = t0 - CL
                        if not sweepB:
                            k.op(k.act, lambda: nc.scalar.copy(out=yf[i2][:, 0:N], in_=py[:, 0:N]), reads=[py.b],
                                 writes=[yf[i2].b])
                            k.dma(k.pool, yf_d[hp, :, tr0:tr0 + N], yf[i2][:, 0:N], yf[i2].b, reads=[yf[i2].b])
                            return
                        k.dma(k.sp, yf[i2][:, 0:N], yf_d[hp, :, tr0:tr0 + N], yf[i2].b, writes=[yf[i2].b])
                        wk_ = gn[i2]
                        k.op(k.dve, lambda: nc.vector.tensor_tensor(out=wk_[:, 0:N], in0=py[:, 0:N], in1=yf[i2][:, 0:N],
                                                                    op=ALU.add), reads=[py.b, yf[i2].b], writes=[wk_.b])
                        k.op(k.act, lambda: nc.scalar.copy(out=sqb[i2][:, 0:N], in_=wk_[:, 0:N]), reads=[wk_.b],
                             writes=[sqb[i2].b])
                        pm_ = nxt("C")
                        k.op(k.pe, lambda: nc.tensor.matmul(pm_[:, 0:N], lhsT=bones_bf, rhs=sqb[i2][:, 0:N], start=True,
                                                            stop=True), reads=[sqb[i2].b, cbf.b], writes=[pm_.b])
                        k.op(k.dve, lambda: nc.vector.scalar_tensor_tensor(out=wk_[:, 0:N], in0=pm_[:, 0:N],
                                                                           scalar=-1.0 / 64, in1=wk_[:, 0:N],
                                                                           op0=ALU.mult, op1=ALU.add),
                             reads=[pm_.b, wk_.b], writes=[wk_.b])
                        k.op(k.act, lambda: nc.scalar.activation(out=sqb[i2][:, 0:N], in_=wk_[:, 0:N], func=AF.Square),
                             reads=[wk_.b], writes=[sqb[i2].b])
                        pv_ = nxt("C")
                        k.op(k.pe, lambda: nc.tensor.matmul(pv_[:, 0:N], lhsT=bones_bf, rhs=sqb[i2][:, 0:N], start=True,
                                                            stop=True), reads=[sqb[i2].b, cbf.b], writes=[pv_.b])
                        k.op(k.act, lambda: nc.scalar.activation(out=rn[i2][:, 0:N], in_=pv_[:, 0:N], func=AF.Sqrt,
                                                                 scale=1.0 / 64, bias=GN_EPS), reads=[pv_.b],
                             writes=[rn[i2].b])
                        k.op(k.dve, lambda: nc.vector.reciprocal(out=rn[i2][:, 0:N], in_=rn[i2][:, 0:N]),
                             reads=[rn[i2].b], writes=[rn[i2].b])
                        k.op(k.dve, lambda: nc.vector.scalar_tensor_tensor(out=wk_[:, 0:N], in0=wk_[:, 0:N],
                                                                           scalar=V("ln_g", hp, 1), in1=rn[i2][:, 0:N],
                                                                           op0=ALU.mult, op1=ALU.mult),
                             reads=[wk_.b, rn[i2].b, vecs.b], writes=[wk_.b])
                        pa2 = nxt("A")
                        k.op(k.pe, lambda: nc.tensor.matmul(pa2[:, 0:N], lhsT=a2c[ob_, hs], rhs=ta[ob_, 0:N], start=True,
                                                            stop=True), reads=[a2c.b, ta.b], writes=[pa2.b])
                        k.op(k.act, lambda: nc.scalar.activation(out=ao[i2][:, 0:N], in_=pa2[:, 0:N], func=AF.Sigmoid,
                                                                 bias=V("a0", (1 - d) * 8 + hp, 1)),
                             reads=[pa2.b, vecs.b], writes=[ao[i2].b])
                        k.op(k.pool, lambda: nc.gpsimd.tensor_scalar(out=t3[i2][:, 0:N], in0=ao[i2][:, 0:N],
                                                                     scalar1=V("k_a", hp, 1), scalar2=omka[:, hp:hp + 1],
                                                                     op0=ALU.mult, op1=ALU.add),
                             reads=[ao[i2].b, vecs.b, omka.b], writes=[t3[i2].b])
                        k.op(k.pool, lambda: nc.gpsimd.tensor_tensor(out=t3[i2][:, 0:N], in0=t3[i2][:, 0:N],
                                                                     in1=Kt[:, hp, 0:N], op=ALU.mult),
                             reads=[t3[i2].b, Kt.b], writes=[t3[i2].b])
                        k.op(k.pool, lambda: nc.gpsimd.tensor_tensor(out=t3[i2][:, 0:N], in0=t3[i2][:, 0:N],
                                                                     in1=kd[i2][:, 0:N], op=ALU.add),
                             reads=[t3[i2].b, kd[i2].b], writes=[t3[i2].b])
                        k.op(k.dve, lambda: nc.vector.scalar_tensor_tensor(out=sqb[i2][:, 0:N], in0=t3[i2][:, 0:N],
                                                                           scalar=V("r_k", hp, 1), in1=Rt[:, hp, 0:N],
                                                                           op0=ALU.mult, op1=ALU.mult),
                             reads=[t3[i2].b, Rt.b, vecs.b], writes=[sqb[i2].b])
                        pb_ = nxt("C")
                        k.op(k.pe, lambda: nc.tensor.matmul(pb_[:, 0:N], lhsT=bones_bf, rhs=sqb[i2][:, 0:N], start=True,
                                                            stop=True), reads=[sqb[i2].b, cbf.b], writes=[pb_.b])
                        k.op(k.dve, lambda: nc.vector.tensor_tensor(out=t3[i2][:, 0:N], in0=pb_[:, 0:N],
                                                                    in1=Vt[:, hp, 0:N], op=ALU.mult),
                             reads=[pb_.b, Vt.b], writes=[t3[i2].b])
                        k.op(k.dve, lambda: nc.vector.scalar_tensor_tensor(out=wk_[:, 0:N], in0=wk_[:, 0:N],
                                                                           scalar=V("ln_b", hp, 1), in1=t3[i2][:, 0:N],
                                                                           op0=ALU.add, op1=ALU.add),
                             reads=[wk_.b, t3[i2].b, vecs.b], writes=[wk_.b])
                        pg_ = nxt("A")
                        k.op(k.pe, lambda: nc.tensor.matmul(pg_[:, 0:N], lhsT=g2[:, hs], rhs=tg[:, 0:N], start=True,
                                                            stop=True), reads=[g2.b, tg.b], writes=[pg_.b])
                        k.op(k.dve, lambda: nc.vector.tensor_tensor(out=OB[:, hp, 0:N], in0=wk_[:, 0:N], in1=pg_[:, 0:N],
                                                                    op=ALU.mult), reads=[wk_.b, pg_.b], writes=[OB.b])

                    def drive(gens):
                        while gens:
                            for g_ in list(gens):
                                try:
                                    next(g_)
                                except StopIteration:
                                    gens.remove(g_)

                    def step(gens):
                        for g_ in list(gens):
                            try:
                                next(g_)
                            except StopIteration:
                                gens.remove(g_)

                    active = []
                    nxt_hp = 0
                    while nxt_hp < 8 or active:
                        while nxt_hp < 8 and len(active) < 2:
                            hp_prep(nxt_hp)
                            py_ = psB[nxt_hp % 2] if readout else None
                            active.append([nxt_hp, [head_gen(nxt_hp, 0, py_), head_gen(nxt_hp, 1, py_)], py_])
                            nxt_hp += 1
                        for a_ in active:
                            step(a_[1])
                        while active and not active[0][1]:
                            hp_readout(active[0][0], active[0][2])
                            active.pop(0)
                    if sweepB and readout:
                        for n in range(8):
                            i2 = n % 2
                            k.dma(k.sp, x2b[i2][:, 0:N], x2T_d[n, :, t0:t0 + N], x2b[i2].b, writes=[x2b[i2].b])
                            po_ = nxt("A")
                            for c in range(8):
                                k.op(k.pe, lambda: nc.tensor.matmul(po_[:, 0:N], lhsT=wo1[:, c, n * 128:(n + 1) * 128],
                                                                    rhs=OB[:, c, 0:N], start=(c == 0), stop=(c == 7)),
                                     reads=[wo1.b, OB.b], writes=[po_.b], inc=(c == 7))
                            k.op(k.dve, lambda: nc.vector.scalar_tensor_tensor(
                                out=x2b[i2][:, 0:N], in0=po_[:, 0:N], scalar=modv[:, 1, 0, 2, n:n + 1],
                                in1=x2b[i2][:, 0:N], op0=ALU.mult, op1=ALU.add), reads=[po_.b, modv.b, x2b[i2].b],
                                writes=[x2b[i2].b])
                            k.dma(k.pool, x3T_d[n, :, t0:t0 + N], x2b[i2][:, 0:N], x2b[i2].b, reads=[x2b[i2].b])

                lat = [(CL + i * NB, NB, 0) for i in range(L // NB)]
                for st_ in STh:
                    k.op(k.pool, lambda: nc.gpsimd.memset(st_[:, :], 0.0), writes=[st_.b])
                block(0, CL, 1, 0, False)
                for (t0, N, s_) in lat:
                    block(t0, N, s_, 0, False)
                k.barrier()
                for st_ in STh:
                    k.op(k.pool, lambda: nc.gpsimd.memset(st_[:, :], 0.0), writes=[st_.b])
                block(0, CL, 1, 1, True)
                for (t0, N, s_) in reversed(lat):
                    block(t0, N, s_, 1, True)

        blocks0 = [(0, CL, 1)] + [(CL + i * 512, 512, 0) for i in range(L // 512)]

        if stage >= 1:
            with contextlib.ExitStack() as s1:
                wq = tl("wqkv", [128, 8, 1536], BF16, s1)
                with contextlib.ExitStack() as s2:
                    stg = [tl(f"p1_stg{i}", [128, 1536], F32, s2) for i in range(2)]
                    load_weight_bf16(wq, lambda i: wq[:, i, :], lambda i: wqkv_d[i * 128:(i + 1) * 128, :], 8, 1536, stg,
                                     [k.dve, k.pool])
                k.barrier()
                xs = [tl(f"xs{i}", [128, D], F32, s1) for i in range(2)]
                xT = tl("p1_xT", [128, 8, 512], F32, s1)
                hT = tl("p1_hT", [128, 8, 512], BF16, s1)
                sq = tl("p1_sq", [128, 8, 512], BF16, s1)
                rstd = tl("p1_rstd", [128, 512], F32, s1)
                tmp = [tl(f"p1_tmp{i}", [128, 512], F32, s1) for i in range(2)]
                qk_out = tl("p1_qk", [128, 10, 512], BF16, s1)
                v_out = tl("p1_v", [128, 4, 256], BF16, s1)
                ct = tl("p1_ct", [128, 512], F32, s1)
                stb = tl("p1_st", [128, 512], F32, s1)
                hsq = [tl(f"p1_hsq{i}", [128, 512], BF16, s1) for i in range(2)]
                hr = [tl(f"p1_hr{i}", [128, 512], F32, s1) for i in range(2)]
                qg = [tl(f"p1_qg{i}", [128, 512], F32, s1) for i in range(2)]
                qgb = [tl(f"p1_qgb{i}", [128, 512], BF16, s1) for i in range(2)]
                t1 = [tl(f"p1_t1{i}", [128, 512], F32, s1) for i in range(2)]
                t2 = [tl(f"p1_t2{i}", [128, 512], F32, s1) for i in range(2)]
                for bi, (t0, N, s) in enumerate(blocks0):
                    src = ctx_d if s == 1 else x_d
                    r0 = t0 if s == 1 else t0 - CL
                    for tt in range(N // 128):
                        xi = xs[tt % 2]
                        k.dma(k.sp, xi[:, :], src[r0 + tt * 128:r0 + (tt + 1) * 128, :], xi.b, writes=[xi.b])
                        for half in range(2):
                            pt = nxt("C")
                            for cc in range(4):
                                c = half * 4 + cc
                                k.op(k.pe, lambda: nc.tensor.transpose(pt[:, cc * 128:(cc + 1) * 128],
                                                                       xi[:, c * 128:(c + 1) * 128], ident),
                                     reads=[xi.b, cst.b], writes=[pt.b], inc=(cc == 3))
                            k.op(k.act, lambda: nc.scalar.copy(
                                out=xT[:, half * 4:(half + 1) * 4, tt * 128:(tt + 1) * 128],
                                in_=pt[:, :].rearrange("p (c t) -> p c t", t=128)), reads=[pt.b], writes=[xT.b])
                    k.dma(k.pool, xT_d[:, :, t0:t0 + N].rearrange("c p t -> p c t"), xT[:, :, 0:N], xT.b, reads=[xT.b])
                    if s == 0:
                        k.dma(k.sp, ct[:, :], ctab_d[:, r0:r0 + 512], ct.b, writes=[ct.b])
                        k.dma(k.sp, stb[:, :], stab_d[:, r0:r0 + 512], stb.b, writes=[stb.b])
                    norm_mod(xT, N, 0, s, 0, hT, sq, rstd, tmp)
                    for n in range(10):
                        pq = nxt("A")
                        for c in range(8):
                            k.op(k.pe, lambda: nc.tensor.matmul(pq[:, 0:N], lhsT=wq[:, c, n * 128:(n + 1) * 128],
                                                                rhs=hT[:, c, 0:N], start=(c == 0), stop=(c == 7)),
                                 reads=[wq.b, hT.b], writes=[pq.b], inc=(c == 7))
                        i2 = n % 2
                        k.op(k.act, lambda: nc.scalar.activation(out=hsq[i2][:, 0:N], in_=pq[:, 0:N], func=AF.Square),
                             reads=[pq.b], writes=[hsq[i2].b])
                        ph = nxt("C")
                        k.op(k.pe, lambda: nc.tensor.matmul(ph[:, 0:N], lhsT=bones_bf, rhs=hsq[i2][:, 0:N], start=True,
                                                            stop=True), reads=[hsq[i2].b, cbf.b], writes=[ph.b])
                        k.op(k.act, lambda: nc.scalar.activation(out=hr[i2][:, 0:N], in_=ph[:, 0:N], func=AF.Sqrt,
                                                                 scale=1.0 / 64, bias=EPS), reads=[ph.b],
                             writes=[hr[i2].b])
                        k.op(k.dve, lambda: nc.vector.reciprocal(out=hr[i2][:, 0:N], in_=hr[i2][:, 0:N]),
                             reads=[hr[i2].b], writes=[hr[i2].b])
                        gcol = qg8[:, 0:1] if n < 8 else qg8[:, 1:2]
                        k.op(k.dve, lambda: nc.vector.scalar_tensor_tensor(
                            out=qg[i2][:, 0:N], in0=pq[:, 0:N], scalar=gcol, in1=hr[i2][:, 0:N], op0=ALU.mult,
                            op1=ALU.mult), reads=[pq.b, qg8.b, hr[i2].b], writes=[qg[i2].b])
                        if s == 1:
                            k.op(k.pool, lambda: nc.gpsimd.tensor_copy(out=qk_out[:, n, 0:N], in_=qg[i2][:, 0:N]),
                                 reads=[qg[i2].b], writes=[qk_out.b])
                        else:
                            k.op(k.pool, lambda: nc.gpsimd.tensor_copy(out=qgb[i2][:, 0:N], in_=qg[i2][:, 0:N]),
                                 reads=[qg[i2].b], writes=[qgb[i2].b])
                            pp = nxt("C")
                            k.op(k.pe, lambda: nc.tensor.matmul(pp[:, 0:N], lhsT=perm_bf, rhs=qgb[i2][:, 0:N],
                                                                start=True, stop=True), reads=[qgb[i2].b, cbf.b],
                                 writes=[pp.b])
                            k.op(k.pool, lambda: nc.gpsimd.tensor_tensor(out=t1[i2][:, 0:N], in0=qg[i2][:, 0:N],
                                                                         in1=ct[:, 0:N], op=ALU.mult),
                                 reads=[qg[i2].b, ct.b], writes=[t1[i2].b])
                            k.op(k.dve, lambda: nc.vector.tensor_tensor(out=t2[i2][:, 0:N], in0=pp[:, 0:N],
                                                                        in1=stb[:, 0:N], op=ALU.mult),
                                 reads=[pp.b, stb.b], writes=[t2[i2].b])
                            k.op(k.dve, lambda: nc.vector.tensor_tensor(out=qk_out[:, n, 0:N], in0=t1[i2][:, 0:N],
                                                                        in1=t2[i2][:, 0:N], op=ALU.add),
                                 reads=[t1[i2].b, t2[i2].b], writes=[qk_out.b])
                    for tt in range(N // 128):
                        pv = nxt("A")
                        for c in range(8):
                            k.op(k.pe, lambda: nc.tensor.matmul(pv[:, 0:256], lhsT=hT[:, c, tt * 128:(tt + 1) * 128],
                                                                rhs=wq[:, c, 1280:1536], start=(c == 0), stop=(c == 7)),
                                 reads=[wq.b, hT.b], writes=[pv.b], inc=(c == 7))
                        k.op(k.act, lambda: nc.scalar.copy(out=v_out[:, tt, :], in_=pv[:, 0:256]), reads=[pv.b],
                             writes=[v_out.b])
                    k.dma(k.pool, qT_d[:, :, t0:t0 + N].rearrange("c p t -> p c t"), qk_out[:, 0:8, 0:N], qk_out.b,
                          reads=[qk_out.b])
                    k.dma(k.pool, kT_d[:, :, t0:t0 + N].rearrange("c p t -> p c t"), qk_out[:, 8:10, 0:N], qk_out.b,
                          reads=[qk_out.b])
                    k.dma(k.pool, v_d[t0:t0 + N, :].rearrange("(j p) d -> p j d", p=128), v_out[:, 0:N // 128, :],
                          v_out.b, reads=[v_out.b])

        def phase_barrier():
            k.barrier()

        phase_barrier()

        if stage >= 2:
            with contextlib.ExitStack() as s1:
                KT = tl("KT", [128, 2, TT], BF16, s1)
                VA = tl("VA", [128, NKT, 4, 128], BF16, s1)
                wo = tl("wo0", [128, 8, D], BF16, s1)
                with contextlib.ExitStack() as s2:
                    stg = [tl(f"p2_stg{i}", [128, D], F32, s2) for i in range(2)]
                    load_weight_bf16(wo, lambda i: wo[:, i, :], lambda i: wo0_d[i * 128:(i + 1) * 128, :], 8, D, stg,
                                     [k.dve, k.pool])
                k.barrier()
                qT = [tl(f"p2_qT{i}", [128, 8, 512], BF16, s1) for i in range(2)]
                PT = [tl(f"p2_PT{i}", [128, 512], BF16, s1) for i in range(3)]
                oT = tl("p2_oT", [128, 8, 512], BF16, s1)
                xT = tl("p2_xT", [128, 8, 512], F32, s1)
                rec = [tl(f"p2_rec{i}", [128, 512], F32, s1) for i in range(2)]
                k.dma(k.sp, KT[:, :, :], kT_d[:, :, :].rearrange("c p t -> p c t"), KT.b, writes=[KT.b])
                k.op(k.pool, lambda: nc.gpsimd.memset(VA[:, :, :, :], 1.0), writes=[VA.b])
                for g in range(4):
                    off = 0 if g % 2 == 0 else 64
                    for j0 in range(0, NKT, 8):
                        j1 = min(NKT, j0 + 8)
                        k.dma(k.sp, VA[:, j0:j1, g, off:off + 64],
                              v_d[j0 * 128:j1 * 128, g * 64:(g + 1) * 64].rearrange("(j p) d -> p j d", p=128), VA.b,
                              writes=[VA.b])
                for bi, (t0, N, s) in enumerate(blocks0):
                    q = qT[bi % 2]
                    k.dma(k.sp, q[:, :, 0:N], qT_d[:, :, t0:t0 + N].rearrange("c p t -> p c t"), q.b, writes=[q.b])
                    k.dma(k.sp, xT[:, :, 0:N], xT_d[:, :, t0:t0 + N].rearrange("c p t -> p c t"), xT.b, writes=[xT.b])
                    nkt = 2 if s == 1 else NKT
                    hidx = 0
                    for c in range(8):
                        for sl in range(2):
                            g = (c // 4) * 2 + sl
                            po = psB[hidx % 2]
                            lo, hi = sl * 64, (sl + 1) * 64
                            pss = [None] * nkt
                            for j in range(nkt + 2):
                                if j < nkt:
                                    ps_ = nxt("A")
                                    pss[j] = ps_
                                    k.op(k.pe, lambda: nc.tensor.matmul(ps_[:, 0:N],
                                                                        lhsT=KT[lo:hi, g // 2, j * 128:(j + 1) * 128],
                                                                        rhs=q[lo:hi, c, 0:N], start=True, stop=True),
                                         reads=[KT.b, q.b], writes=[ps_.b])
                                    pt_ = PT[j % 3]
                                    k.op(k.act, lambda: nc.scalar.activation(out=pt_[:, 0:N], in_=ps_[:, 0:N],
                                                                             func=AF.Exp), reads=[ps_.b],
                                         writes=[pt_.b])
                                if j >= 2:
                                    jj = j - 2
                                    pt2 = PT[jj % 3]
                                    k.op(k.pe, lambda: nc.tensor.matmul(po[:, 0:N], lhsT=VA[:, jj, g, :],
                                                                        rhs=pt2[:, 0:N], start=(jj == 0),
                                                                        stop=(jj == nkt - 1)),
                                         reads=[VA.b, pt2.b], writes=[po.b], inc=(jj == nkt - 1))
                            rc = rec[hidx % 2]
                            olo, ohi = (64, 128) if sl == 0 else (0, 64)
                            k.op(k.dve, lambda: nc.vector.reciprocal(out=rc[lo:hi, 0:N], in_=po[olo:ohi, 0:N]),
                                 reads=[po.b], writes=[rc.b])
                            k.op(k.dve, lambda: nc.vector.tensor_tensor(out=oT[lo:hi, c, 0:N], in0=po[lo:hi, 0:N],
                                                                        in1=rc[lo:hi, 0:N], op=ALU.mult),
                                 reads=[po.b, rc.b], writes=[oT.b])
                            hidx += 1
                    for n in range(8):
                        py = nxt("C")
                        for c in range(8):
                            k.op(k.pe, lambda: nc.tensor.matmul(py[:, 0:N], lhsT=wo[:, c, n * 128:(n + 1) * 128],
                                                                rhs=oT[:, c, 0:N], start=(c == 0), stop=(c == 7)),
                                 reads=[wo.b, oT.b], writes=[py.b], inc=(c == 7))
                        k.op(k.dve, lambda: nc.vector.scalar_tensor_tensor(
                            out=xT[:, n, 0:N], in0=py[:, 0:N], scalar=modv[:, 0, s, 2, n:n + 1], in1=xT[:, n, 0:N],
                            op0=ALU.mult, op1=ALU.add), reads=[py.b, modv.b, xT.b], writes=[xT.b])
                    k.dma(k.pool, x1T_d[:, :, t0:t0 + N].rearrange("c p t -> p c t"), xT[:, :, 0:N], xT.b, reads=[xT.b])
            phase_barrier()

        if stage >= 3:
            zt = tl("zeros", [128, 8, 1], F32)
            k.op(k.pool, lambda: nc.gpsimd.memset(zt[:, :, :], 0.0), writes=[zt.b])
            for (dd, n_) in ((h1c_d, CL), (h1l_d, L)):
                for col in (0, n_ + 1):
                    k.dma(k.pool, dd[:, :, col:col + 1].rearrange("c p t -> p c t"), zt[:, :, :], zt.b, reads=[zt.b],
                          allow_slow_non_contiguous=True)

            def post_h1(t0, N, s_, xT, hT, sq, rstd, tmp, sg):
                rms_stats(xT, N, sq, rstd)
                dst = h1c_d if s_ == 1 else h1l_d
                r0 = (t0 if s_ == 1 else t0 - CL) + 1
                for c in range(8):
                    tb = tmp[c % 2]
                    ob = sg[c % 2]
                    k.op(k.dve, lambda: nc.vector.scalar_tensor_tensor(
                        out=tb[:, 0:N], in0=xT[:, c, 0:N], scalar=modv[:, 1, s_, 1, c:c + 1], in1=rstd[:, 0:N],
                        op0=ALU.mult, op1=ALU.mult), reads=[xT.b, modv.b, rstd.b], writes=[tb.b])
                    k.op(k.pool, lambda: nc.gpsimd.tensor_scalar(
                        out=ob[:, 0:N], in0=tb[:, 0:N], scalar1=modv[:, 1, s_, 0, c:c + 1], scalar2=None,
                        op0=ALU.add), reads=[tb.b, modv.b], writes=[ob.b])
                    k.dma(k.pool, dst[c, :, r0:r0 + N], ob[:, 0:N], ob.b, reads=[ob.b])

            ffn_phase(0, x1T_d, x2T_d, blocks0, post_fn=post_h1)
            phase_barrier()

        if stage >= 4:
            rwkv_phase()
            phase_barrier()

        if stage >= 6:
            def post_final(t0, N, s_, xT, hT, sq, rstd, tmp, sg):
                rms_stats(xT, N, sq, rstd)
                fg = V("final_g", 0, 8)
                for c in range(8):
                    k.op(k.dve, lambda: nc.vector.scalar_tensor_tensor(
                        out=xT[:, c, 0:N], in0=xT[:, c, 0:N], scalar=fg[:, c:c + 1], in1=rstd[:, 0:N],
                        op0=ALU.mult, op1=ALU.mult), reads=[xT.b, vecs.b, rstd.b], writes=[xT.b])
                for tt in range(N // 128):
                    for half in range(2):
                        pt = nxt("C")
                        for cc in range(4):
                            c = half * 4 + cc
                            k.op(k.pe, lambda: nc.tensor.transpose(pt[:, cc * 128:(cc + 1) * 128],
                                                                   xT[:, c, tt * 128:(tt + 1) * 128], ident),
                                 reads=[xT.b, cst.b], writes=[pt.b], inc=(cc == 3))
                        o = sg[half]
                        k.op(k.act, lambda: nc.scalar.copy(out=o[:, :], in_=pt[:, :]), reads=[pt.b], writes=[o.b])
                        r = (t0 - CL) + tt * 128
                        k.dma(k.pool, out_d[r:r + 128, half * 512:(half + 1) * 512], o[:, :], o.b, reads=[o.b])

            ffn_phase(1, x3T_d, None, blocks0[1:], post_fn=post_final)
            phase_barrier()

        dbg_src = {1: xT_d, 2: x1T_d, 3: x2T_d, 5: x3T_d}.get(stage)
        if dbg_src is not None:
            with contextlib.ExitStack() as s1:
                xT = tl("o_xT", [128, 8, 512], F32, s1)
                ot = [tl(f"o_t{i}", [128, D], F32, s1) for i in range(2)]
                for bi in range(L // 512):
                    t0 = CL + bi * 512
                    k.dma(k.sp, xT[:, :, :], dbg_src[:, :, t0:t0 + 512].rearrange("c p t -> p c t"), xT.b,
                          writes=[xT.b])
                    for tt in range(4):
                        o = ot[tt % 2]
                        for half in range(2):
                            pt = nxt("C")
                            for cc in range(4):
                                c = half * 4 + cc
                                k.op(k.pe, lambda: nc.tensor.transpose(pt[:, cc * 128:(cc + 1) * 128],
                                                                       xT[:, c, tt * 128:(tt + 1) * 128], ident),
                                     reads=[xT.b, cst.b], writes=[pt.b], inc=(cc == 3))
                            k.op(k.act, lambda: nc.scalar.copy(out=o[:, half * 512:(half + 1) * 512], in_=pt[:, :]),
                                 reads=[pt.b], writes=[o.b])
                        r = bi * 512 + tt * 128
                        k.dma(k.pool, out_d[r:r + 128, :], o[:, :], o.b, reads=[o.b])
        k.finish()
        print(f"[build] L={L} stage={stage} inst={k.n_inst} waits={k.n_wait}")
    return nc


_CACHE = {}


def prep_inputs(inp, b, L):
    wqkv_p, wo_p = _CACHE.get("w") or host_weights(inp)
    _CACHE["w"] = (wqkv_p, wo_p)
    cm, ct, stb = _CACHE.get(("c", L)) or make_consts(L)
    _CACHE[("c", L)] = (cm, ct, stb)
    f = lambda a: np.ascontiguousarray(np.asarray(a, np.float32))
    return {
        "x": f(inp["x"][b, :L]), "ctx": f(inp["ctx"][b]), "vecs": pack_vecs(inp, b), "consts": cm, "ctab": ct,
        "stab": stb, "mod_w": f(inp["mod_w"]), "wqkv": wqkv_p, "wo0": wo_p, "ffn_wg": f(inp["ffn_wg"]),
        "ffn_wu": f(inp["ffn_wu"]), "ffn_wd": f(inp["ffn_wd"]),
        "wrkv": f(inp["rwkv_wrkv"][0]), "wo1": f(inp["rwkv_wo"][0]),
        "w1c": f(np.concatenate([inp["rwkv_w1"][0][0], inp["rwkv_w1"][0][1]], axis=1)),
        "a1c": f(np.concatenate([inp["rwkv_a1"][0][0], inp["rwkv_a1"][0][1]], axis=1)),
        "g1": f(inp["rwkv_g1"][0]),
        "w2c": f(np.concatenate([inp["rwkv_w2"][0][0], inp["rwkv_w2"][0][1]], axis=0)),
        "a2c": f(np.concatenate([inp["rwkv_a2"][0][0], inp["rwkv_a2"][0][1]], axis=0)),
        "g2": f(inp["rwkv_g2"][0]), "masks": make_masks(),
    }


def kernel(**inputs):
    B, L, _ = inputs["x"].shape
    inp = {kk: np.asarray(v) for kk, v in inputs.items()}
    nc = build(L)
    in_maps = [prep_inputs(inp, b, L) for b in range(B)]
    res = run_bass_kernel_spmd(nc, in_maps, core_ids=list(range(B)))
    return np.stack([np.asarray(r["out"], np.float32) for r in res.results], axis=0)
```

```python
import contextlib
import numpy as np
import concourse.bass as bass
import concourse.mybir as mybir
from concourse.bass_utils import run_bass_kernel_spmd

F32 = mybir.dt.float32
BF16 = mybir.dt.bfloat16
AF = mybir.ActivationFunctionType
ALU = mybir.AluOpType

D = 1024
NCH = 8
CL = 256
DFF = 2816
NF = 22
EPS = 1e-6
GN_EPS = 64e-5


class Buf:
    __slots__ = ("name", "w", "r", "dsem", "dcnt")

    def __init__(self, name):
        self.name = name
        self.w = None
        self.r = {}
        self.dsem = None
        self.dcnt = 0


class Eng:
    def __init__(self, key, eng, sem):
        self.key = key
        self.eng = eng
        self.sem = sem
        self.cnt = 0
        self.waited = {}


class K:
    def __init__(self, nc, stack):
        self.nc = nc
        self.stack = stack
        self.pe = Eng("pe", nc.tensor, self._sem("c_pe"))
        self.act = Eng("act", nc.scalar, self._sem("c_act"))
        self.dve = Eng("dve", nc.vector, self._sem("c_dve"))
        self.pool = Eng("pool", nc.gpsimd, self._sem("c_pool"))
        self.sp = Eng("sp", nc.sync, self._sem("c_sp"))
        self.n_inst = 0
        self.n_wait = 0
        self.dma_bufs = []

    def _sem(self, name):
        return self.stack.enter_context(self.nc.semaphore(name))

    def _need(self, e, sem, val):
        if val > e.waited.get(id(sem), 0):
            e.eng.wait_ge(sem, val)
            e.waited[id(sem)] = val
            self.n_wait += 1

    def _pre(self, e, reads, writes, is_dma=False):
        for b in reads:
            if b.w is not None:
                sem, val, key = b.w
                self._need(e, sem, val)
        for b in writes:
            if b.w is not None:
                sem, val, key = b.w
                if is_dma or key != e.key:
                    self._need(e, sem, val)
            for key, (sem, val) in b.r.items():
                if is_dma or key != e.key:
                    self._need(e, sem, val)

    def op(self, e, fn, reads=(), writes=(), inc=True):
        self._pre(e, reads, writes)
        ins = fn()
        self.n_inst += 1
        if inc:
            ins.then_inc(e.sem, 1)
            e.cnt += 1
            val = e.cnt
        else:
            val = e.cnt + 1
        for b in reads:
            b.r[e.key] = (e.sem, val)
        for b in writes:
            b.w = (e.sem, val, e.key)
            b.r = {}
        return ins

    def dma(self, q, out_ap, in_ap, sb, reads=(), writes=(), **kw):
        if sb.dsem is None:
            sb.dsem = self._sem("d_" + sb.name)
            self.dma_bufs.append(sb)
        self._pre(q, reads, writes, is_dma=True)
        ins = q.eng.dma_start(out=out_ap, in_=in_ap, **kw)
        ins.then_inc(sb.dsem, 16)
        sb.dcnt += 16
        self.n_inst += 1
        key = "dma_" + sb.name
        for b in reads:
            b.r[key] = (sb.dsem, sb.dcnt)
        for b in writes:
            b.w = (sb.dsem, sb.dcnt, key)
            b.r = {}
        return ins

    def finish(self):
        for sb in self.dma_bufs:
            self._need(self.sp, sb.dsem, sb.dcnt)

    def barrier(self):
        engs = [self.pe, self.act, self.dve, self.pool, self.sp]
        for e in engs:
            for o in engs:
                if o is not e and o.cnt > 0:
                    self._need(e, o.sem, o.cnt)
            for sb in self.dma_bufs:
                if sb.dcnt > 0:
                    self._need(e, sb.dsem, sb.dcnt)


class T:
    def __init__(self, k, stack, name, shape, dt, psum=False):
        nc = k.nc
        if psum:
            self.t = stack.enter_context(nc.psum_tensor("t_" + name, list(shape), dt))
        else:
            self.t = stack.enter_context(nc.sbuf_tensor("t_" + name, list(shape), dt))
        self.b = Buf(name)

    def __getitem__(self, key):
        return self.t[key]


def _perm_d():
    i = np.arange(64)
    return ((i % 32) // 16) * 32 + (i // 32) * 16 + (i % 16)


HEAD_OF = [[0, 1, 2, 3, 8, 9, 10, 11], [4, 5, 6, 7, 12, 13, 14, 15]]


def _fm(v):
    v = np.asarray(v, np.float32).reshape(-1, 128)
    return np.ascontiguousarray(v.T)


class VecPack:
    def __init__(self):
        self.cols = []
        self.off = {}
        self.n = 0

    def add(self, name, arr):
        arr = np.asarray(arr, np.float32)
        assert arr.shape[0] == 128
        self.off[name] = (self.n, arr.shape[1])
        self.cols.append(arr)
        self.n += arr.shape[1]

    def build(self):
        return np.ascontiguousarray(np.concatenate(self.cols, axis=1))


def vec_layout():
    names = [("c", 8), ("c_ctx", 8), ("mod_b0", 48), ("mod_b1", 48), ("n1g0", 8), ("n1g1", 8), ("n2g0", 8),
             ("n2g1", 8), ("final_g", 8), ("qg", 1), ("kg", 1), ("mix", 48), ("w0", 16), ("a0", 16), ("k_k", 8),
             ("k_a", 8), ("r_k", 8), ("ln_g", 8), ("ln_b", 8)]
    off = {}
    n = 0
    for nm, w in names:
        off[nm] = (n, w)
        n += w
    return off, n


def pack_vecs(inp, b):
    pd = _perm_d()
    vp = VecPack()
    vp.add("c", _fm(inp["c"][b]))
    vp.add("c_ctx", _fm(inp["c_ctx"]))
    vp.add("mod_b0", _fm(inp["mod_b"][0]))
    vp.add("mod_b1", _fm(inp["mod_b"][1]))
    vp.add("n1g0", _fm(inp["norm1_g"][0]))
    vp.add("n1g1", _fm(inp["norm1_g"][1]))
    vp.add("n2g0", _fm(inp["norm2_g"][0]))
    vp.add("n2g1", _fm(inp["norm2_g"][1]))
    vp.add("final_g", _fm(inp["final_g"]))
    vp.add("qg", np.tile(inp["attn_q_gain"][0][pd], 2).reshape(128, 1))
    vp.add("kg", np.tile(inp["attn_k_gain"][0][pd], 2).reshape(128, 1))
    vp.add("mix", np.concatenate([_fm(inp["rwkv_mix"][0][j]) for j in range(6)], axis=1))
    vp.add("w0", np.concatenate([_fm(inp["rwkv_w0"][0][d]) for d in range(2)], axis=1))
    vp.add("a0", np.concatenate([_fm(inp["rwkv_a0"][0][d]) for d in range(2)], axis=1))
    vp.add("k_k", _fm(inp["rwkv_k_k"][0]))
    vp.add("k_a", _fm(inp["rwkv_k_a"][0]))
    vp.add("r_k", _fm(inp["rwkv_r_k"][0].reshape(-1)))
    vp.add("ln_g", _fm(inp["rwkv_ln_g"][0]))
    vp.add("ln_b", _fm(inp["rwkv_ln_b"][0]))
    off, n = vec_layout()
    assert vp.off == off
    return vp.build()


def make_consts(L):
    c = {}
    c["ident"] = np.eye(128, dtype=np.float32)
    p = np.zeros((128, 128), np.float32)
    for m in range(128):
        p[m ^ 32, m] = 1.0
    c["perm32"] = p
    bo = np.zeros((128, 128), np.float32)
    bo[:64, :64] = 1.0
    bo[64:, 64:] = 1.0
    c["blockones"] = bo
    c["allones"] = np.ones((128, 128), np.float32)
    cm = np.concatenate([c["ident"], c["perm32"], c["blockones"], c["allones"]], axis=1)
    t = np.arange(L)
    row = (t // 64).astype(np.float32)
    col = (t % 64).astype(np.float32)
    inv_freq = (np.float32(10000.0) ** (-np.arange(16, dtype=np.float32) / np.float32(16))).astype(np.float32)
    ang = np.stack([row[:, None] * inv_freq, col[:, None] * inv_freq], axis=1)
    cos = np.cos(ang).astype(np.float32)
    sin = np.sin(ang).astype(np.float32)
    ct = np.zeros((128, L), np.float32)
    stb = np.zeros((128, L), np.float32)
    for r in range(128):
        i = r % 64
        a = (i % 32) // 16
        pp = i % 16
        ct[r] = cos[:, a, pp]
        stb[r] = -sin[:, a, pp] if i < 32 else sin[:, a, pp]
    return np.ascontiguousarray(cm), ct, stb


def make_masks():
    i = np.arange(64)
    us = (i[:, None] < i[None, :]).astype(np.float32)
    ui = (i[:, None] <= i[None, :]).astype(np.float32)
    m = np.concatenate([us, ui, us.T, ui.T], axis=1)
    return np.ascontiguousarray(np.concatenate([m, m], axis=0))


def host_weights(inp):
    pd = _perm_d()
    wqkv = np.asarray(inp["attn_wqkv"][0], np.float32)
    qcols = []
    for c in range(8):
        for s in range(2):
            h = HEAD_OF[s][c]
            qcols.append(h * 64 + pd)
    qcols = np.concatenate(qcols)
    kcols = np.concatenate([1024 + g * 64 + pd for g in range(4)])
    vcols = 1280 + np.arange(256)
    wqkv_p = np.ascontiguousarray(wqkv[:, np.concatenate([qcols, kcols, vcols])])
    orows = np.concatenate([HEAD_OF[s][c] * 64 + np.arange(64) for c in range(8) for s in range(2)])
    wo_p = np.ascontiguousarray(np.asarray(inp["attn_wo"][0], np.float32)[orows, :])
    return wqkv_p, wo_p


def build(L, stage=99):
    TT = CL + L
    NKT = TT // 128
    nc = bass.Bass("TRN2", target_bir_lowering=False)
    voff, NV = vec_layout()

    def din(name, shape, dt=F32):
        return nc.dram_tensor(name, list(shape), dt, kind="ExternalInput").ap()

    def dscr(name, shape, dt=F32):
        return nc.dram_tensor(name, list(shape), dt).ap()

    x_d = din("x", [L, D])
    ctx_d = din("ctx", [CL, D])
    vecs_d = din("vecs", [128, NV])
    consts_d = din("consts", [128, 512])
    ctab_d = din("ctab", [128, L])
    stab_d = din("stab", [128, L])
    modw_d = din("mod_w", [2, D, 6 * D])
    wqkv_d = din("wqkv", [D, 1536])
    wo0_d = din("wo0", [D, D])
    wg_d = din("ffn_wg", [2, D, DFF])
    wu_d = din("ffn_wu", [2, D, DFF])
    wd_d = din("ffn_wd", [2, DFF, D])
    wrkv_d = din("wrkv", [3, D, D])
    wo1_d = din("wo1", [D, D])
    w1c_d = din("w1c", [D, 128])
    a1c_d = din("a1c", [D, 128])
    g1_d = din("g1", [D, 128])
    w2c_d = din("w2c", [128, D])
    a2c_d = din("a2c", [128, D])
    g2_d = din("g2", [128, D])
    masks_d = din("masks", [128, 256])
    out_d = nc.dram_tensor("out", [L, D], F32, kind="ExternalOutput").ap()

    xT_d = dscr("xT_s", [NCH, 128, TT])
    qT_d = dscr("qT_s", [NCH, 128, TT], BF16)
    kT_d = dscr("kT_s", [2, 128, TT], BF16)
    v_d = dscr("v_s", [TT, 256], BF16)
    x1T_d = dscr("x1T_s", [NCH, 128, TT])
    x2T_d = dscr("x2T_s", [NCH, 128, TT])
    h1c_d = dscr("h1c_s", [NCH, 128, CL + 2])
    h1l_d = dscr("h1l_s", [NCH, 128, L + 2])
    yf_d = dscr("yf_s", [NCH, 128, L])
    x3T_d = dscr("x3T_s", [NCH, 128, TT])

    st = contextlib.ExitStack()
    with st:
        k = K(nc, st)

        uniq = {"n": 0}

        def tl(name, shape, dt, stack=None, psum=False):
            uniq["n"] += 1
            return T(k, stack or st, f"{name}_{uniq['n']}", shape, dt, psum=psum)

        vecs = tl("vecs", [128, NV], F32)
        k.dma(k.sp, vecs[:, :], vecs_d[:, :], vecs.b, writes=[vecs.b])
        cst = tl("cst", [128, 512], F32)
        k.dma(k.sp, cst[:, :], consts_d[:, :], cst.b, writes=[cst.b])
        cbf = tl("cbf", [128, 512], BF16)
        k.op(k.dve, lambda: nc.vector.tensor_copy(out=cbf[:, :], in_=cst[:, :]), reads=[cst.b], writes=[cbf.b])
        ident = cst[:, 0:128]
        perm_bf = cbf[:, 128:256]
        bones_bf = cbf[:, 256:384]
        ones_bf = cbf[:, 384:512]
        modv = tl("modv", [128, 2, 2, 6, 8], F32)
        qg8 = tl("qg8", [128, 2], F32)

        def V(name, i=0, n=1):
            o, w = voff[name]
            return vecs[:, o + i:o + i + n]

        k.op(k.dve, lambda: nc.vector.tensor_scalar(out=qg8[:, 0:1], in0=V("qg"), scalar1=0.125, scalar2=None,
                                                    op0=ALU.mult), reads=[vecs.b], writes=[qg8.b])
        k.op(k.dve, lambda: nc.vector.tensor_copy(out=qg8[:, 1:2], in_=V("kg")), reads=[vecs.b], writes=[qg8.b])

        psA = [tl(f"psA{i}", [128, 512], F32, psum=True) for i in range(3)]
        psB = [tl(f"psB{i}", [128, 512], F32, psum=True) for i in range(2)]
        psC = [tl(f"psC{i}", [128, 512], F32, psum=True) for i in range(3)]
        rr = {"A": 0, "B": 0, "C": 0}

        def nxt(which):
            lst = {"A": psA, "B": psB, "C": psC}[which]
            i = rr[which]
            rr[which] = (i + 1) % len(lst)
            return lst[i]

        with contextlib.ExitStack() as s0:
            sc = tl("sc", [128, 8, 2], F32, s0)
            k.op(k.act, lambda: nc.scalar.activation(out=sc[:, :, 0], in_=V("c", 0, 8), func=AF.Silu),
                 reads=[vecs.b], writes=[sc.b])
            k.op(k.act, lambda: nc.scalar.activation(out=sc[:, :, 1], in_=V("c_ctx", 0, 8), func=AF.Silu),
                 reads=[vecs.b], writes=[sc.b])
            mst = [tl(f"mst{i}", [128, 6 * D], F32, s0) for i in range(2)]
            macc = tl("macc", [128, 96], F32, s0)
            for l in range(2):
                for kc in range(8):
                    ms = mst[kc % 2]
                    k.dma(k.sp, ms[:, :], modw_d[l, kc * 128:(kc + 1) * 128, :], ms.b, writes=[ms.b])
                    pm = nxt("C")
                    for n in range(48):
                        k.op(k.pe, lambda: nc.tensor.matmul(pm[:, 2 * n:2 * n + 2], lhsT=ms[:, n * 128:(n + 1) * 128],
                                                            rhs=sc[:, kc, :], start=True, stop=True),
                             reads=[ms.b, sc.b], writes=[pm.b], inc=(n == 47))
                    if kc == 0:
                        k.op(k.dve, lambda: nc.vector.tensor_copy(out=macc[:, :], in_=pm[:, 0:96]),
                             reads=[pm.b], writes=[macc.b])
                    else:
                        k.op(k.dve, lambda: nc.vector.tensor_tensor(out=macc[:, :], in0=macc[:, :], in1=pm[:, 0:96],
                                                                    op=ALU.add),
                             reads=[pm.b, macc.b], writes=[macc.b])
                mb = V(f"mod_b{l}", 0, 48)
                for s in range(2):
                    mv = macc[:, :].rearrange("p (n s) -> p n s", s=2)[:, :, s]
                    tmp = tl(f"mtmp{l}{s}", [128, 48], F32, s0)
                    k.op(k.dve, lambda: nc.vector.tensor_tensor(out=tmp[:, :], in0=mv, in1=mb, op=ALU.add),
                         reads=[macc.b, vecs.b], writes=[tmp.b])
                    for (dst, src) in ((0, 0), (2, 2), (3, 3), (5, 5)):
                        k.op(k.dve, lambda: nc.vector.tensor_copy(out=modv[:, l, s, dst, :],
                                                                  in_=tmp[:, src * 8:(src + 1) * 8]),
                             reads=[tmp.b], writes=[modv.b])
                    for (dst, src, gname) in ((1, 1, f"n1g{l}"), (4, 4, f"n2g{l}")):
                        k.op(k.dve, lambda: nc.vector.scalar_tensor_tensor(
                            out=modv[:, l, s, dst, :], in0=tmp[:, src * 8:(src + 1) * 8], scalar=1.0,
                            in1=V(gname, 0, 8), op0=ALU.add, op1=ALU.mult),
                            reads=[tmp.b, vecs.b], writes=[modv.b])

        k.barrier()
        def load_weight_bf16(dst, dst_view_fn, src_rows_fn, nrow_tiles, ncols, stg, engs):
            for i in range(nrow_tiles):
                sg = stg[i % len(stg)]
                k.dma(k.sp, sg[:, 0:ncols], src_rows_fn(i), sg.b, writes=[sg.b])
                e = engs[i % len(engs)]
                if e is k.act:
                    k.op(e, lambda: nc.scalar.copy(out=dst_view_fn(i), in_=sg[:, 0:ncols]), reads=[sg.b], writes=[dst.b])
                elif e is k.dve:
                    k.op(e, lambda: nc.vector.tensor_copy(out=dst_view_fn(i), in_=sg[:, 0:ncols]), reads=[sg.b],
                         writes=[dst.b])
                else:
                    k.op(e, lambda: nc.gpsimd.tensor_copy(out=dst_view_fn(i), in_=sg[:, 0:ncols]), reads=[sg.b],
                         writes=[dst.b])

        def rms_stats(xT, N, sq, rstd):
            for c in range(8):
                k.op(k.act, lambda: nc.scalar.activation(out=sq[:, c, 0:N], in_=xT[:, c, 0:N], func=AF.Square),
                     reads=[xT.b], writes=[sq.b])
            ps = nxt("C")
            for c in range(8):
                k.op(k.pe, lambda: nc.tensor.matmul(ps[:, 0:N], lhsT=ones_bf, rhs=sq[:, c, 0:N], start=(c == 0),
                                                    stop=(c == 7)),
                     reads=[sq.b, cbf.b], writes=[ps.b], inc=(c == 7))
            k.op(k.act, lambda: nc.scalar.activation(out=rstd[:, 0:N], in_=ps[:, 0:N], func=AF.Sqrt, scale=1.0 / D,
                                                     bias=EPS),
                 reads=[ps.b], writes=[rstd.b])
            k.op(k.dve, lambda: nc.vector.reciprocal(out=rstd[:, 0:N], in_=rstd[:, 0:N]), reads=[rstd.b],
                 writes=[rstd.b])

        def norm_mod(xT, N, l, s, which, hT, sq, rstd, tmp):
            rms_stats(xT, N, sq, rstd)
            jsh, jgm = (0, 1) if which == 0 else (3, 4)
            for c in range(8):
                tb = tmp[c % len(tmp)]
                k.op(k.dve, lambda: nc.vector.scalar_tensor_tensor(
                    out=tb[:, 0:N], in0=xT[:, c, 0:N], scalar=modv[:, l, s, jgm, c:c + 1], in1=rstd[:, 0:N],
                    op0=ALU.mult, op1=ALU.mult), reads=[xT.b, modv.b, rstd.b], writes=[tb.b])
                k.op(k.act, lambda: nc.scalar.activation(out=hT[:, c, 0:N], in_=tb[:, 0:N], func=AF.Identity,
                                                         bias=modv[:, l, s, jsh, c:c + 1]),
                     reads=[tb.b, modv.b], writes=[hT.b])

        def ffn_phase(l, src_d, dst_d, blocks, post_fn=None):
            with contextlib.ExitStack() as sp_:
                wg = tl("wg", [128, 8, DFF], BF16, sp_)
                wu = tl("wu", [128, 8, DFF], BF16, sp_)
                wd = tl("wd", [128, NF, D], BF16, sp_)
                with contextlib.ExitStack() as s2:
                    stg = [tl(f"f_stg{i}", [128, DFF], F32, s2) for i in range(2)]
                    engs = [k.dve, k.act]
                    load_weight_bf16(wg, lambda i: wg[:, i, :], lambda i: wg_d[l, i * 128:(i + 1) * 128, :], 8, DFF,
                                     stg, engs)
                    load_weight_bf16(wu, lambda i: wu[:, i, :], lambda i: wu_d[l, i * 128:(i + 1) * 128, :], 8, DFF,
                                     stg, engs)
                    load_weight_bf16(wd, lambda i: wd[:, i, :], lambda i: wd_d[l, i * 128:(i + 1) * 128, :], NF, D,
                                     stg, engs)
                k.barrier()
                xT = tl("f_xT", [128, 8, 512], F32, sp_)
                hT = tl("f_hT", [128, 8, 512], BF16, sp_)
                aT = tl("f_aT", [128, NF, 512], BF16, sp_)
                rstd = tl("f_rstd", [128, 512], F32, sp_)
                tmp = [tl(f"f_tmp{i}", [128, 512], F32, sp_) for i in range(2)]
                sg = [tl(f"f_sg{i}", [128, 512], F32, sp_) for i in range(2)]
                for (t0, N, s) in blocks:
                    k.dma(k.sp, xT[:, :, 0:N], src_d[:, :, t0:t0 + N].rearrange("c p t -> p c t"), xT.b,
                          writes=[xT.b])
                    sq = aT
                    norm_mod(xT, N, l, s, 1, hT, T_alias(aT, "sq"), rstd, tmp)
                    for f in range(NF):
                        pg = nxt("A")
                        for c in range(8):
                            k.op(k.pe, lambda: nc.tensor.matmul(pg[:, 0:N], lhsT=wg[:, c, f * 128:(f + 1) * 128],
                                                                rhs=hT[:, c, 0:N], start=(c == 0), stop=(c == 7)),
                                 reads=[wg.b, hT.b], writes=[pg.b], inc=(c == 7))
                        pu = nxt("C")
                        for c in range(8):
                            k.op(k.pe, lambda: nc.tensor.matmul(pu[:, 0:N], lhsT=wu[:, c, f * 128:(f + 1) * 128],
                                                                rhs=hT[:, c, 0:N], start=(c == 0), stop=(c == 7)),
                                 reads=[wu.b, hT.b], writes=[pu.b], inc=(c == 7))
                        sgi = sg[f % 2]
                        k.op(k.act, lambda: nc.scalar.activation(out=sgi[:, 0:N], in_=pg[:, 0:N], func=AF.Silu),
                             reads=[pg.b], writes=[sgi.b])
                        k.op(k.dve, lambda: nc.vector.tensor_tensor(out=aT[:, f, 0:N], in0=sgi[:, 0:N],
                                                                    in1=pu[:, 0:N], op=ALU.mult),
                             reads=[sgi.b, pu.b], writes=[aT.b])
                    for n in range(8):
                        pd_ = nxt("B")
                        for f in range(NF):
                            k.op(k.pe, lambda: nc.tensor.matmul(pd_[:, 0:N], lhsT=wd[:, f, n * 128:(n + 1) * 128],
                                                                rhs=aT[:, f, 0:N], start=(f == 0), stop=(f == NF - 1)),
                                 reads=[wd.b, aT.b], writes=[pd_.b], inc=(f == NF - 1))
                        k.op(k.dve, lambda: nc.vector.scalar_tensor_tensor(
                            out=xT[:, n, 0:N], in0=pd_[:, 0:N], scalar=modv[:, l, s, 5, n:n + 1], in1=xT[:, n, 0:N],
                            op0=ALU.mult, op1=ALU.add), reads=[pd_.b, modv.b, xT.b], writes=[xT.b])
                    if dst_d is not None:
                        k.dma(k.pool, dst_d[:, :, t0:t0 + N].rearrange("c p t -> p c t"), xT[:, :, 0:N], xT.b,
                              reads=[xT.b])
                    if post_fn is not None:
                        post_fn(t0, N, s, xT, hT, T_alias(aT, "sq"), rstd, tmp, sg)

        def T_alias(t, name):
            return t


        def rwkv_phase():
            NB = 256
            C0 = float(np.exp(-0.5))
            with contextlib.ExitStack() as s1:
                Wr = tl("Wr", [128, 8, D], BF16, s1)
                Wk = tl("Wk", [128, 8, D], BF16, s1)
                Wv = tl("Wv", [128, 8, D], BF16, s1)
                wo1 = tl("wo1", [128, 8, D], BF16, s1)
                w1c = tl("w1c", [128, 8, 128], BF16, s1)
                a1c = tl("a1c", [128, 8, 128], BF16, s1)
                g1 = tl("g1", [128, 8, 128], BF16, s1)
                w2c = tl("w2c", [128, D], BF16, s1)
                a2c = tl("a2c", [128, D], BF16, s1)
                g2 = tl("g2", [128, D], BF16, s1)
                with contextlib.ExitStack() as s2:
                    stg = [tl(f"r_stg{i}", [128, D], F32, s2) for i in range(2)]
                    engs = [k.dve, k.act]
                    for (wt, j) in ((Wr, 0), (Wk, 1), (Wv, 2)):
                        load_weight_bf16(wt, lambda i: wt[:, i, :], lambda i: wrkv_d[j, i * 128:(i + 1) * 128, :], 8, D,
                                         stg, engs)
                    load_weight_bf16(wo1, lambda i: wo1[:, i, :], lambda i: wo1_d[i * 128:(i + 1) * 128, :], 8, D, stg,
                                     engs)
                    for (wt, src) in ((w1c, w1c_d), (a1c, a1c_d), (g1, g1_d)):
                        load_weight_bf16(wt, lambda i: wt[:, i, :], lambda i: src[i * 128:(i + 1) * 128, :], 8, 128, stg,
                                         engs)
                    for (wt, src) in ((w2c, w2c_d), (a2c, a2c_d), (g2, g2_d)):
                        load_weight_bf16(wt, lambda i: wt[:, :], lambda i: src[:, :], 1, D, stg, engs)
                k.barrier()
                msk = tl("msk", [128, 256], F32, s1)
                k.dma(k.sp, msk[:, :], masks_d[:, :], msk.b, writes=[msk.b])
                rst = tl("rst", [128, NB], F32, s1)
                k.op(k.pool, lambda: nc.gpsimd.memset(rst[:, :], 1.0), writes=[rst.b])
                k.op(k.pool, lambda: nc.gpsimd.memset(rst[:, :].rearrange("p (c t) -> p c t", t=64)[:, :, 0:1], 0.0),
                     writes=[rst.b])
                omka = tl("omka", [128, 8], F32, s1)
                k.op(k.dve, lambda: nc.vector.tensor_scalar(out=omka[:, :], in0=V("k_a", 0, 8), scalar1=-1.0,
                                                            scalar2=1.0, op0=ALU.mult, op1=ALU.add),
                     reads=[vecs.b], writes=[omka.b])
                ident_bf = cbf[:, 0:128]

                U = tl("r_U", [128, 8, NB + 2], F32, s1)
                XX = tl("r_XX", [128, 8, NB], F32, s1)
                LB = tl("r_LB", [128, 8, NB], BF16, s1)
                Kt = tl("r_Kt", [128, 8, NB], F32, s1)
                Vt = tl("r_Vt", [128, 8, NB], BF16, s1)
                Rt = tl("r_Rt", [128, 8, NB], BF16, s1)
                OB = tl("r_OB", [128, 8, NB], BF16, s1)
                tw = tl("r_tw", [128, NB], BF16, s1)
                ta = tl("r_ta", [128, NB], BF16, s1)
                tg = tl("r_tg", [128, NB], BF16, s1)

                def f32t(nm, n=1):
                    return [tl(f"r_{nm}{i}", [128, NB], F32, s1) for i in range(n)]

                def dbl(lst):
                    return lst if len(lst) == 2 else [lst[0], lst[0]]

                sig = dbl(f32t("sig", 1)); aa = dbl(f32t("aa", 1)); ao = dbl(f32t("ao", 1)); kkr = dbl(f32t("kkr", 1))
                rn = dbl(f32t("rn", 1)); kkn = dbl(f32t("kkn", 1)); kd = f32t("kd", 2); bet = dbl(f32t("bet", 1))
                cum = dbl(f32t("cum", 1)); xe = dbl(f32t("xe", 1)); xi = dbl(f32t("xi", 1)); Ee = dbl(f32t("Ee", 1))
                ai = dbl(f32t("ai", 1)); ri = dbl(f32t("ri", 1)); t3 = dbl(f32t("t3", 1)); yf = f32t("yf", 2)
                gn = dbl(f32t("gn", 1)); x2b = f32t("x2b", 2)
                sqb = [tl(f"r_sqb{i}", [128, NB], BF16, s1) for i in range(2)]
                wtot = [tl(f"r_wtot{i}", [128, 4], F32, s1) for i in range(2)]
                wt0 = [tl(f"r_wt0{i}", [64, 4], F32, s1) for i in range(4)]
                AR = [tl(f"r_AR{i}", [128, 4, 128], BF16, s1) for i in range(2)]
                BK = [tl(f"r_BK{i}", [128, 4, 128], BF16, s1) for i in range(2)]
                TMa = [tl(f"r_TMa{i}", [64, 4, 128], BF16, s1) for i in range(2)]
                Vtm = [tl(f"r_Vtm{i}", [64, 4, 128], BF16, s1) for i in range(2)]
                NSET = 4
                Xs = [[tl(f"r_X{g}_{i}", [64, 4, 192], BF16, s1) for i in range(2)] for g in range(NSET)]
                Ws = [[tl(f"r_W{g}_{i}", [64, 4, 128], BF16, s1) for i in range(2)] for g in range(NSET)]
                AKT = [tl(f"r_AKT{i}", [64, 4, 64], BF16, s1) for i in range(NSET)]
                KA = [tl(f"r_KA{i}", [64, 4, 128], F32, s1) for i in range(NSET)]
                M1 = [tl(f"r_M1{i}", [64, 4, 64], F32, s1) for i in range(NSET)]
                QT = [tl(f"r_QT{i}", [64, 4, 64], F32, s1) for i in range(NSET)]
                ZN = [tl(f"r_ZN{i}", [64, 4, 128], BF16, s1) for i in range(NSET)]
                STP = [[tl(f"r_STP{g}_{i}", [64, 64], F32, s1) for i in range(2)] for g in range(NSET)]
                RC0 = [tl(f"r_RC0{i}", [64, 4, 64], BF16, s1) for i in range(NSET)]
                STh = [tl(f"r_STh{i}", [64, 64], F32, s1) for i in range(16)]
                cnt = {"u": 0}

                def bc4(ap2d):
                    return ap2d.unsqueeze(1).to_broadcast([ap2d.shape[0], 4, ap2d.shape[1]])

                def lerp(j, N):
                    for c in range(8):
                        e = k.dve
                        eng = nc.vector
                        k.op(e, lambda: eng.scalar_tensor_tensor(
                            out=LB[:, c, 0:N], in0=XX[:, c, 0:N], scalar=V("mix", j * 8 + c, 1), in1=U[:, c, 1:N + 1],
                            op0=ALU.mult, op1=ALU.add), reads=[XX.b, U.b, vecs.b], writes=[LB.b])

                def proj_full(Wt, dst, N):
                    for n in range(8):
                        ps = nxt("A")
                        for c in range(8):
                            k.op(k.pe, lambda: nc.tensor.matmul(ps[:, 0:N], lhsT=Wt[:, c, n * 128:(n + 1) * 128],
                                                                rhs=LB[:, c, 0:N], start=(c == 0), stop=(c == 7)),
                                 reads=[Wt.b, LB.b], writes=[ps.b], inc=(c == 7))
                        k.op(k.act, lambda: nc.scalar.copy(out=dst[:, n, 0:N], in_=ps[:, 0:N]), reads=[ps.b],
                             writes=[dst.b])

                def proj_lora(Wt, dst, N, func):
                    ps = nxt("A")
                    for c in range(8):
                        k.op(k.pe, lambda: nc.tensor.matmul(ps[:, 0:N], lhsT=Wt[:, c, :], rhs=LB[:, c, 0:N],
                                                            start=(c == 0), stop=(c == 7)),
                             reads=[Wt.b, LB.b], writes=[ps.b], inc=(c == 7))
                    k.op(k.act, lambda: nc.scalar.activation(out=dst[:, 0:N], in_=ps[:, 0:N], func=func),
                         reads=[ps.b], writes=[dst.b])


                def block(t0, N, s_, d, sweepB):
                    readout = (s_ == 0)
                    src = h1c_d if s_ == 1 else h1l_d
                    r0 = t0 if s_ == 1 else t0 - CL
                    nch = N // 64
                    k.dma(k.sp, U[:, :, 0:N + 2], src[:, :, r0:r0 + N + 2].rearrange("c p t -> p c t"), U.b,
                          writes=[U.b])
                    k.op(k.dve, lambda: nc.vector.tensor_tensor(out=XX[:, :, 0:N], in0=U[:, :, 0:N], in1=U[:, :, 2:N + 2],
                                                                op=ALU.add), reads=[U.b], writes=[XX.b])
                    k.op(k.dve, lambda: nc.vector.scalar_tensor_tensor(out=XX[:, :, 0:N], in0=XX[:, :, 0:N], scalar=0.5,
                                                                       in1=U[:, :, 1:N + 1], op0=ALU.mult,
                                                                       op1=ALU.subtract), reads=[XX.b, U.b],
                         writes=[XX.b])
                    lerp(2, N); proj_full(Wk, Kt, N)
                    lerp(3, N); proj_full(Wv, Vt, N)
                    if readout:
                        lerp(0, N); proj_full(Wr, Rt, N)
                    else:
                        k.op(k.pool, lambda: nc.gpsimd.memset(Rt[:, :, :], 0.0), writes=[Rt.b])
                    lerp(1, N); proj_lora(w1c, tw, N, AF.Tanh)
                    lerp(4, N); proj_lora(a1c, ta, N, AF.Copy)
                    if sweepB and readout:
                        lerp(5, N); proj_lora(g1, tg, N, AF.Sigmoid)
                    if d == 0:
                        m_lt, m_le, m_ltT = msk[:, 0:64], msk[:, 64:128], msk[:, 128:192]
                    else:
                        m_lt, m_le, m_ltT = msk[:, 128:192], msk[:, 192:256], msk[:, 0:64]
                    db = slice(d * 64, (d + 1) * 64)
                    ob_ = slice((1 - d) * 64, (2 - d) * 64)
                    def hp_prep(hp):
                        i2 = hp % 2
                        hs = slice(hp * 128, (hp + 1) * 128)
                        pz = nxt("A")
                        k.op(k.pe, lambda: nc.tensor.matmul(pz[:, 0:N], lhsT=w2c[db, hs], rhs=tw[db, 0:N], start=True,
                                                            stop=True), reads=[w2c.b, tw.b], writes=[pz.b])
                        k.op(k.act, lambda: nc.scalar.activation(out=sig[i2][:, 0:N], in_=pz[:, 0:N], func=AF.Sigmoid,
                                                                 bias=V("w0", d * 8 + hp, 1)), reads=[pz.b, vecs.b],
                             writes=[sig[i2].b])
                        pa = nxt("A")
                        k.op(k.pe, lambda: nc.tensor.matmul(pa[:, 0:N], lhsT=a2c[db, hs], rhs=ta[db, 0:N], start=True,
                                                            stop=True), reads=[a2c.b, ta.b], writes=[pa.b])
                        k.op(k.act, lambda: nc.scalar.activation(out=aa[i2][:, 0:N], in_=pa[:, 0:N], func=AF.Sigmoid,
                                                                 bias=V("a0", d * 8 + hp, 1)), reads=[pa.b, vecs.b],
                             writes=[aa[i2].b])
                        k.op(k.dve, lambda: nc.vector.tensor_scalar(out=kkr[i2][:, 0:N], in0=Kt[:, hp, 0:N],
                                                                    scalar1=V("k_k", hp, 1), scalar2=None,
                                                                    op0=ALU.mult), reads=[Kt.b, vecs.b],
                             writes=[kkr[i2].b])
                        k.op(k.act, lambda: nc.scalar.activation(out=sqb[i2][:, 0:N], in_=kkr[i2][:, 0:N],
                                                                 func=AF.Square), reads=[kkr[i2].b], writes=[sqb[i2].b])
                        pn = nxt("C")
                        k.op(k.pe, lambda: nc.tensor.matmul(pn[:, 0:N], lhsT=bones_bf, rhs=sqb[i2][:, 0:N], start=True,
                                                            stop=True), reads=[sqb[i2].b, cbf.b], writes=[pn.b])
                        k.op(k.dve, lambda: nc.vector.tensor_scalar(out=rn[i2][:, 0:N], in0=pn[:, 0:N], scalar1=1e-24,
                                                                    scalar2=None, op0=ALU.max), reads=[pn.b],
                             writes=[rn[i2].b])
                        k.op(k.act, lambda: nc.scalar.activation(out=rn[i2][:, 0:N], in_=rn[i2][:, 0:N], func=AF.Sqrt),
                             reads=[rn[i2].b], writes=[rn[i2].b])
                        k.op(k.dve, lambda: nc.vector.reciprocal(out=rn[i2][:, 0:N], in_=rn[i2][:, 0:N]),
                             reads=[rn[i2].b], writes=[rn[i2].b])
                        k.op(k.dve, lambda: nc.vector.tensor_tensor(out=kkn[i2][:, 0:N], in0=kkr[i2][:, 0:N],
                                                                    in1=rn[i2][:, 0:N], op=ALU.mult),
                             reads=[kkr[i2].b, rn[i2].b], writes=[kkn[i2].b])
                        k.op(k.dve, lambda: nc.vector.tensor_scalar(out=t3[i2][:, 0:N], in0=aa[i2][:, 0:N],
                                                                     scalar1=V("k_a", hp, 1), scalar2=omka[:, hp:hp + 1],
                                                                     op0=ALU.mult, op1=ALU.add),
                             reads=[aa[i2].b, vecs.b, omka.b], writes=[t3[i2].b])
                        k.op(k.dve, lambda: nc.vector.tensor_tensor(out=kd[i2][:, 0:N], in0=t3[i2][:, 0:N],
                                                                     in1=Kt[:, hp, 0:N], op=ALU.mult),
                             reads=[t3[i2].b, Kt.b], writes=[kd[i2].b])
                        k.op(k.dve, lambda: nc.vector.tensor_tensor(out=bet[i2][:, 0:N], in0=kkn[i2][:, 0:N],
                                                                     in1=aa[i2][:, 0:N], op=ALU.mult),
                             reads=[kkn[i2].b, aa[i2].b], writes=[bet[i2].b])
                        k.op(k.dve, lambda: nc.vector.tensor_tensor_scan(out=cum[i2][:, 0:N], data0=rst[:, 0:N],
                                                                         data1=sig[i2][:, 0:N], initial=0.0,
                                                                         op0=ALU.mult, op1=ALU.add),
                             reads=[rst.b, sig[i2].b], writes=[cum[i2].b])
                        c3 = cum[i2][:, 0:N].rearrange("p (c t) -> p c t", t=64)
                        if d == 0:
                            k.op(k.dve, lambda: nc.vector.tensor_tensor(
                                out=xe[i2][:, 0:N].rearrange("p (c t) -> p c t", t=64),
                                in0=c3[:, :, 63:64].to_broadcast([128, nch, 64]), in1=c3, op=ALU.subtract),
                                reads=[cum[i2].b], writes=[xe[i2].b])
                            k.op(k.dve, lambda: nc.vector.tensor_tensor(out=xi[i2][:, 0:N], in0=xe[i2][:, 0:N],
                                                                        in1=sig[i2][:, 0:N], op=ALU.add),
                                 reads=[xe[i2].b, sig[i2].b], writes=[xi[i2].b])
                            xe_, xi_ = xe[i2], xi[i2]
                        else:
                            k.op(k.dve, lambda: nc.vector.tensor_tensor(out=xe[i2][:, 0:N], in0=cum[i2][:, 0:N],
                                                                        in1=sig[i2][:, 0:N], op=ALU.subtract),
                                 reads=[cum[i2].b, sig[i2].b], writes=[xe[i2].b])
                            xe_, xi_ = xe[i2], cum[i2]
                        k.op(k.act, lambda: nc.scalar.activation(out=Ee[i2][:, 0:N], in_=xe_[:, 0:N], func=AF.Exp,
                                                                 scale=-C0), reads=[xe_.b], writes=[Ee[i2].b])
                        k.op(k.act, lambda: nc.scalar.activation(out=ai[i2][:, 0:N], in_=xi_[:, 0:N], func=AF.Exp,
                                                                 scale=C0), reads=[xi_.b], writes=[ai[i2].b])
                        k.op(k.act, lambda: nc.scalar.activation(out=ri[i2][:, 0:N], in_=xe_[:, 0:N], func=AF.Exp,
                                                                 scale=C0), reads=[xe_.b], writes=[ri[i2].b])
                        k.op(k.act, lambda: nc.scalar.activation(out=wtot[i2][:, 0:nch], in_=c3[:, :, 63], func=AF.Exp,
                                                                 scale=-C0), reads=[cum[i2].b], writes=[wtot[i2].b])
                        ARv, BKv = AR[i2], BK[i2]

                        def v3(t_):
                            return t_[:, 0:N].rearrange("p (c t) -> p c t", t=64)

                        k.op(k.dve, lambda: nc.vector.scalar_tensor_tensor(out=ARv[:, 0:nch, 0:64], in0=v3(kkn[i2]),
                                                                           scalar=-1.0, in1=v3(ai[i2]), op0=ALU.mult,
                                                                           op1=ALU.mult), reads=[kkn[i2].b, ai[i2].b],
                             writes=[ARv.b])
                        k.op(k.dve, lambda: nc.vector.tensor_tensor(
                            out=ARv[:, 0:nch, 64:128], in0=Rt[:, hp, 0:N].rearrange("p (c t) -> p c t", t=64),
                            in1=v3(ri[i2]), op=ALU.mult), reads=[Rt.b, ri[i2].b], writes=[ARv.b])
                        k.op(k.dve, lambda: nc.vector.tensor_tensor(out=BKv[:, 0:nch, 0:64], in0=v3(bet[i2]),
                                                                    in1=v3(Ee[i2]), op=ALU.mult),
                             reads=[bet[i2].b, Ee[i2].b], writes=[BKv.b])
                        k.op(k.dve, lambda: nc.vector.tensor_tensor(out=BKv[:, 0:nch, 64:128], in0=v3(kd[i2]),
                                                                     in1=v3(Ee[i2]), op=ALU.mult),
                             reads=[kd[i2].b, Ee[i2].b], writes=[BKv.b])
                        pta = nxt("C"); ptb = nxt("C"); ptk = nxt("C"); ptv = nxt("A")
                        ptab = pta[0:64, :].bitcast(BF16).rearrange("p (c t) -> p c t", t=256)
                        ptbb = ptb[0:64, :].bitcast(BF16).rearrange("p (c t) -> p c t", t=256)
                        ptkb = ptk[0:64, :].bitcast(BF16).rearrange("p (c t) -> p c t", t=256)
                        ptvb = ptv[0:64, :].bitcast(BF16).rearrange("p (c t) -> p c t", t=256)
                        for ci in range(nch):
                            last = (ci == nch - 1)
                            k.op(k.pe, lambda: nc.tensor.transpose(ptab[:, ci, 0:128], ARv[:, ci, 0:64], ident_bf),
                                 reads=[ARv.b, cbf.b], writes=[pta.b], inc=last)
                            k.op(k.pe, lambda: nc.tensor.transpose(ptbb[:, ci, 0:128], BKv[:, ci, 0:64], ident_bf),
                                 reads=[BKv.b, cbf.b], writes=[ptb.b], inc=last)
                            k.op(k.pe, lambda: nc.tensor.transpose(ptkb[:, ci, 0:128], BKv[:, ci, 64:128], ident_bf),
                                 reads=[BKv.b, cbf.b], writes=[ptk.b], inc=last)
                            k.op(k.pe, lambda: nc.tensor.transpose(ptvb[:, ci, 0:128], Vt[:, hp, ci * 64:(ci + 1) * 64],
                                                                   ident_bf),
                                 reads=[Vt.b, cbf.b], writes=[ptv.b], inc=last)
                        k.op(k.act, lambda: nc.scalar.copy(out=TMa[i2][:, 0:nch, :], in_=ptab[:, 0:nch, 0:128]),
                             reads=[pta.b], writes=[TMa[i2].b])
                        for h_ in range(2):
                            us_ = (i2 * 2 + h_) % NSET
                            hb_ = slice(h_ * 64, (h_ + 1) * 64)
                            k.op(k.act, lambda: nc.scalar.copy(out=Ws[us_][0][:, 0:nch, 0:64], in_=ptbb[:, 0:nch, hb_]),
                                 reads=[ptb.b], writes=[Ws[us_][0].b])
                            k.op(k.dve, lambda: nc.vector.tensor_copy(out=KA[us_][:, 0:nch, 0:64],
                                                                      in_=ptkb[:, 0:nch, hb_]),
                                 reads=[ptk.b], writes=[KA[us_].b])
                        k.op(k.act, lambda: nc.scalar.copy(out=Vtm[i2][:, 0:nch, :], in_=ptvb[:, 0:nch, 0:128]),
                             reads=[ptv.b], writes=[Vtm[i2].b])

                    def head_gen(hp, h, py):
                        i2 = hp % 2
                        hs = slice(hp * 128, (hp + 1) * 128)
                        ARv, BKv = AR[i2], BK[i2]
                        nb_ = [64, nch, 64]
                        hb = slice(h * 64, (h + 1) * 64)
                        u2 = (i2 * 2 + h) % NSET
                        X, W0, W1 = Xs[u2][0], Ws[u2][0], Ws[u2][1]
                        p2 = nxt("A"); p1 = nxt("A")
                        p2v = p2[:, :].rearrange("p (c t) -> p c t", t=128)
                        p1v = p1[0:64, :].rearrange("p (c t) -> p c t", t=128)
                        for ci in range(nch):
                            k.op(k.pe, lambda: nc.tensor.matmul(p2v[:, ci, :], lhsT=BKv[hb, ci, :], rhs=ARv[hb, ci, :],
                                                                start=True, stop=True), reads=[BKv.b, ARv.b],
                                 writes=[p2.b], inc=(ci == nch - 1))
                        for ci in range(nch):
                            k.op(k.pe, lambda: nc.tensor.matmul(p1v[:, ci, :], lhsT=ARv[hb, ci, 0:64],
                                                                rhs=BKv[hb, ci, :], start=True, stop=True),
                                 reads=[BKv.b, ARv.b], writes=[p1.b], inc=(ci == nch - 1))
                        nb_ = [64, nch, 64]
                        k.op(k.dve, lambda: nc.vector.tensor_tensor(
                            out=X[:, 0:nch, 64:128], in0=p2v[0:64, 0:nch, 0:64],
                            in1=m_lt[0:64, :].unsqueeze(1).to_broadcast(nb_), op=ALU.mult),
                            reads=[p2.b, msk.b], writes=[X.b])
                        k.op(k.dve, lambda: nc.vector.tensor_tensor(
                            out=W0[:, 0:nch, 64:128], in0=p2v[0:64, 0:nch, 64:128],
                            in1=m_le[0:64, :].unsqueeze(1).to_broadcast(nb_), op=ALU.mult),
                            reads=[p2.b, msk.b], writes=[W0.b])
                        k.op(k.dve, lambda: nc.vector.tensor_tensor(
                            out=KA[u2][:, 0:nch, 64:128], in0=p2v[64:128, 0:nch, 64:128],
                            in1=m_le[64:128, :].unsqueeze(1).to_broadcast(nb_), op=ALU.mult),
                            reads=[p2.b, msk.b], writes=[KA[u2].b])
                        k.op(k.dve, lambda: nc.vector.tensor_tensor(
                            out=X[:, 0:nch, 0:64], in0=p1v[:, 0:nch, 0:64],
                            in1=m_ltT[0:64, :].unsqueeze(1).to_broadcast(nb_), op=ALU.mult),
                            reads=[p1.b, msk.b], writes=[X.b])
                        k.op(k.dve, lambda: nc.vector.tensor_tensor(
                            out=X[:, 0:nch, 128:192], in0=p1v[:, 0:nch, 0:64],
                            in1=m_ltT[0:64, :].unsqueeze(1).to_broadcast(nb_), op=ALU.mult),
                            reads=[p1.b, msk.b], writes=[X.b])
                        k.op(k.dve, lambda: nc.vector.tensor_tensor(
                            out=AKT[u2][:, 0:nch, :], in0=p1v[:, 0:nch, 64:128],
                            in1=m_ltT[0:64, :].unsqueeze(1).to_broadcast(nb_), op=ALU.mult),
                            reads=[p1.b, msk.b], writes=[AKT[u2].b])
                        yield
                        Wc, Wn = W0, W1
                        Xc, Xn = X, Xs[u2][1]
                        for j in range(6):
                            pw = nxt("A")
                            pwv = pw[0:64, :].rearrange("p (c t) -> p c t", t=128)
                            for ci in range(nch):
                                k.op(k.pe, lambda: nc.tensor.matmul(pwv[:, ci, :], lhsT=ident_bf[0:64, 0:64],
                                                                    rhs=Wc[:, ci, :], start=True, stop=False),
                                     reads=[Wc.b, cbf.b], writes=[pw.b], inc=False)
                                k.op(k.pe, lambda: nc.tensor.matmul(pwv[:, ci, :], lhsT=Xc[:, ci, 0:64],
                                                                    rhs=Wc[:, ci, :], start=False, stop=True),
                                     reads=[Wc.b, Xc.b], writes=[pw.b], inc=(ci == nch - 1))
                            k.op(k.act, lambda: nc.scalar.copy(out=Wn[:, 0:nch, :], in_=pwv[:, 0:nch, :]),
                                 reads=[pw.b], writes=[Wn.b])
                            Wc, Wn = Wn, Wc
                            if j < 5:
                                pq_ = nxt("C")
                                pqv = pq_[:, :].rearrange("p (c t) -> p c t", t=128)
                                for ci in range(nch):
                                    k.op(k.pe, lambda: nc.tensor.matmul(pqv[:, ci, :], lhsT=Xc[:, ci, 0:128],
                                                                        rhs=Xc[:, ci, 64:192], start=True, stop=True),
                                         reads=[Xc.b], writes=[pq_.b], inc=(ci == nch - 1))
                                k.op(k.act, lambda: nc.scalar.copy(out=Xn[:, 0:nch, 64:128],
                                                                   in_=pqv[0:64, 0:nch, 0:64]), reads=[pq_.b],
                                     writes=[Xn.b])
                                k.op(k.dve, lambda: nc.vector.tensor_copy(out=Xn[:, 0:nch, 0:64],
                                                                          in_=pqv[64:128, 0:nch, 64:128]),
                                     reads=[pq_.b], writes=[Xn.b])
                                k.op(k.dve, lambda: nc.vector.tensor_copy(out=Xn[:, 0:nch, 128:192],
                                                                          in_=pqv[64:128, 0:nch, 64:128]),
                                     reads=[pq_.b], writes=[Xn.b])
                                Xc, Xn = Xn, Xc
                            yield
                        pfa = nxt("A"); pfb = nxt("A")
                        pfav = pfa[0:64, :].rearrange("p (c t) -> p c t", t=128)
                        pfbv = pfb[0:64, :].rearrange("p (c t) -> p c t", t=128)
                        for ci in range(nch):
                            k.op(k.pe, lambda: nc.tensor.matmul(pfav[:, ci, :], lhsT=TMa[i2][:, ci, hb],
                                                                rhs=Wc[:, ci, :], start=True, stop=True),
                                 reads=[TMa[i2].b, Wc.b], writes=[pfa.b], inc=(ci == nch - 1))
                        for ci in range(nch):
                            k.op(k.pe, lambda: nc.tensor.matmul(pfbv[:, ci, :], lhsT=AKT[u2][:, ci, :],
                                                                rhs=Wc[:, ci, :], start=True, stop=True),
                                 reads=[AKT[u2].b, Wc.b], writes=[pfb.b], inc=(ci == nch - 1))
                        k.op(k.dve, lambda: nc.vector.tensor_tensor(
                            out=M1[u2][:, 0:nch, :], in0=pfav[:, 0:nch, 0:64],
                            in1=ident[0:64, 0:64].unsqueeze(1).to_broadcast(nb_), op=ALU.add),
                            reads=[pfa.b, cst.b], writes=[M1[u2].b])
                        if h == 0:
                            rcs, rcb = ARv[0:64, 0:nch, 64:128], ARv.b
                        else:
                            k.op(k.dve, lambda: nc.vector.tensor_copy(out=RC0[u2][:, 0:nch, :],
                                                                      in_=ARv[64:128, 0:nch, 64:128]),
                                 reads=[ARv.b], writes=[RC0[u2].b])
                            rcs, rcb = RC0[u2][:, 0:nch, :], RC0[u2].b
                        k.op(k.dve, lambda: nc.vector.tensor_tensor(out=QT[u2][:, 0:nch, :], in0=pfav[:, 0:nch, 64:128],
                                                                    in1=rcs, op=ALU.add),
                             reads=[pfa.b, rcb], writes=[QT[u2].b])
                        k.op(k.dve, lambda: nc.vector.tensor_tensor(out=ZN[u2][:, 0:nch, :], in0=pfbv[:, 0:nch, :],
                                                                    in1=KA[u2][:, 0:nch, :], op=ALU.add),
                             reads=[pfb.b, KA[u2].b], writes=[ZN[u2].b])
                        wti = wt0[u2]
                        k.op(k.dve, lambda: nc.vector.tensor_copy(out=wti[:, 0:nch], in_=wtot[i2][hb, 0:nch]),
                             reads=[wtot[i2].b], writes=[wti.b])
                        yield
                        hh = hp * 2 + h
                        order = range(nch) if d == 0 else range(nch - 1, -1, -1)
                        for oi, ci in enumerate(order):
                            stp = STP[u2][oi % 2]
                            k.op(k.dve, lambda: nc.vector.tensor_scalar(out=stp[:, :], in0=STh[hh][:, :],
                                                                        scalar1=wti[:, ci:ci + 1], scalar2=None,
                                                                        op0=ALU.mult), reads=[STh[hh].b, wti.b],
                                 writes=[stp.b])
                            pst = nxt("C")
                            k.op(k.pe, lambda: nc.tensor.matmul(pst[0:64, 0:64], lhsT=M1[u2][:, ci, :], rhs=stp[:, :],
                                                                start=True, stop=False), reads=[M1[u2].b, stp.b],
                                 writes=[pst.b], inc=False)
                            k.op(k.pe, lambda: nc.tensor.matmul(pst[0:64, 0:64], lhsT=ZN[u2][:, ci, 0:64],
                                                                rhs=Vtm[i2][:, ci, hb], start=False, stop=True),
                                 reads=[ZN[u2].b, Vtm[i2].b], writes=[pst.b])
                            if readout:
                                k.op(k.pe, lambda: nc.tensor.matmul(py[hb, ci * 64:(ci + 1) * 64], lhsT=stp[:, :],
                                                                    rhs=QT[u2][:, ci, :], start=True, stop=False),
                                     reads=[stp.b, QT[u2].b], writes=[py.b], inc=False)
                                k.op(k.pe, lambda: nc.tensor.matmul(py[hb, ci * 64:(ci + 1) * 64],
                                                                    lhsT=Vtm[i2][:, ci, hb],
                                                                    rhs=ZN[u2][:, ci, 64:128], start=False, stop=True),
                                     reads=[Vtm[i2].b, ZN[u2].b], writes=[py.b])
                            k.op(k.act, lambda: nc.scalar.copy(out=STh[hh][:, :], in_=pst[0:64, 0:64]), reads=[pst.b],
                                 writes=[STh[hh].b])
                            yield

                    def hp_readout(hp, py):
                        i2 = hp % 2
                        hs = slice(hp * 128, (hp + 1) * 128)
                        if not readout:
                            return
                        tr0 = t0 - CL
                        if not sweepB:
                            k.op(k.act, lambda: nc.scalar.copy(out=yf[i2][:, 0:N], in_=py[:, 0:N]), reads=[py.b],
                                 writes=[yf[i2].b])
                            k.dma(k.pool, yf_d[hp, :, tr0:tr0 + N], yf[i2][:, 0:N], yf[i2].b, reads=[yf[i2].b])
                            return
                        k.dma(k.sp, yf[i2][:, 0:N], yf_d[hp, :, tr0:tr0 + N], yf[i2].b, writes=[yf[i2].b])
                        wk_ = gn[i2]
                        k.op(k.dve, lambda: nc.vector.tensor_tensor(out=wk_[:, 0:N], in0=py[:, 0:N], in1=yf[i2][:, 0:N],
                                                                    op=ALU.add), reads=[py.b, yf[i2].b], writes=[wk_.b])
                        k.op(k.act, lambda: nc.scalar.copy(out=sqb[i2][:, 0:N], in_=wk_[:, 0:N]), reads=[wk_.b],
                             writes=[sqb[i2].b])
                        pm_ = nxt("C")
                        k.op(k.pe, lambda: nc.tensor.matmul(pm_[:, 0:N], lhsT=bones_bf, rhs=sqb[i2][:, 0:N], start=True,
                                                            stop=True), reads=[sqb[i2].b, cbf.b], writes=[pm_.b])
                        k.op(k.dve, lambda: nc.vector.scalar_tensor_tensor(out=wk_[:, 0:N], in0=pm_[:, 0:N],
                                                                           scalar=-1.0 / 64, in1=wk_[:, 0:N],
                                                                           op0=ALU.mult, op1=ALU.add),
                             reads=[pm_.b, wk_.b], writes=[wk_.b])
                        k.op(k.act, lambda: nc.scalar.activation(out=sqb[i2][:, 0:N], in_=wk_[:, 0:N], func=AF.Square),
                             reads=[wk_.b], writes=[sqb[i2].b])
                        pv_ = nxt("C")
                        k.op(k.pe, lambda: nc.tensor.matmul(pv_[:, 0:N], lhsT=bones_bf, rhs=sqb[i2][:, 0:N], start=True,
                                                            stop=True), reads=[sqb[i2].b, cbf.b], writes=[pv_.b])
                        k.op(k.act, lambda: nc.scalar.activation(out=rn[i2][:, 0:N], in_=pv_[:, 0:N], func=AF.Sqrt,
                                                                 scale=1.0 / 64, bias=GN_EPS), reads=[pv_.b],
                             writes=[rn[i2].b])
                        k.op(k.dve, lambda: nc.vector.reciprocal(out=rn[i2][:, 0:N], in_=rn[i2][:, 0:N]),
                             reads=[rn[i2].b], writes=[rn[i2].b])
                        k.op(k.dve, lambda: nc.vector.scalar_tensor_tensor(out=wk_[:, 0:N], in0=wk_[:, 0:N],
                                                                           scalar=V("ln_g", hp, 1), in1=rn[i2][:, 0:N],
                                                                           op0=ALU.mult, op1=ALU.mult),
                             reads=[wk_.b, rn[i2].b, vecs.b], writes=[wk_.b])
                        pa2 = nxt("A")
                        k.op(k.pe, lambda: nc.tensor.matmul(pa2[:, 0:N], lhsT=a2c[ob_, hs], rhs=ta[ob_, 0:N], start=True,
                                                            stop=True), reads=[a2c.b, ta.b], writes=[pa2.b])
                        k.op(k.act, lambda: nc.scalar.activation(out=ao[i2][:, 0:N], in_=pa2[:, 0:N], func=AF.Sigmoid,
                                                                 bias=V("a0", (1 - d) * 8 + hp, 1)),
                             reads=[pa2.b, vecs.b], writes=[ao[i2].b])
                        k.op(k.dve, lambda: nc.vector.tensor_scalar(out=t3[i2][:, 0:N], in0=ao[i2][:, 0:N],
                                                                     scalar1=V("k_a", hp, 1), scalar2=omka[:, hp:hp + 1],
                                                                     op0=ALU.mult, op1=ALU.add),
                             reads=[ao[i2].b, vecs.b, omka.b], writes=[t3[i2].b])
                        k.op(k.dve, lambda: nc.vector.tensor_tensor(out=t3[i2][:, 0:N], in0=t3[i2][:, 0:N],
                                                                     in1=Kt[:, hp, 0:N], op=ALU.mult),
                             reads=[t3[i2].b, Kt.b], writes=[t3[i2].b])
                        k.op(k.dve, lambda: nc.vector.tensor_tensor(out=t3[i2][:, 0:N], in0=t3[i2][:, 0:N],
                                                                     in1=kd[i2][:, 0:N], op=ALU.add),
                             reads=[t3[i2].b, kd[i2].b], writes=[t3[i2].b])
                        k.op(k.dve, lambda: nc.vector.scalar_tensor_tensor(out=sqb[i2][:, 0:N], in0=t3[i2][:, 0:N],
                                                                           scalar=V("r_k", hp, 1), in1=Rt[:, hp, 0:N],
                                                                           op0=ALU.mult, op1=ALU.mult),
                             reads=[t3[i2].b, Rt.b, vecs.b], writes=[sqb[i2].b])
                        pb_ = nxt("C")
                        k.op(k.pe, lambda: nc.tensor.matmul(pb_[:, 0:N], lhsT=bones_bf, rhs=sqb[i2][:, 0:N], start=True,
                                                            stop=True), reads=[sqb[i2].b, cbf.b], writes=[pb_.b])
                        k.op(k.dve, lambda: nc.vector.tensor_tensor(out=t3[i2][:, 0:N], in0=pb_[:, 0:N],
                                                                    in1=Vt[:, hp, 0:N], op=ALU.mult),
                             reads=[pb_.b, Vt.b], writes=[t3[i2].b])
                        k.op(k.dve, lambda: nc.vector.scalar_tensor_tensor(out=wk_[:, 0:N], in0=wk_[:, 0:N],
                                                                           scalar=V("ln_b", hp, 1), in1=t3[i2][:, 0:N],
                                                                           op0=ALU.add, op1=ALU.add),
                             reads=[wk_.b, t3[i2].b, vecs.b], writes=[wk_.b])
                        pg_ = nxt("A")
                        k.op(k.pe, lambda: nc.tensor.matmul(pg_[:, 0:N], lhsT=g2[:, hs], rhs=tg[:, 0:N], start=True,
                                                            stop=True), reads=[g2.b, tg.b], writes=[pg_.b])
                        k.op(k.dve, lambda: nc.vector.tensor_tensor(out=OB[:, hp, 0:N], in0=wk_[:, 0:N], in1=pg_[:, 0:N],
                                                                    op=ALU.mult), reads=[wk_.b, pg_.b], writes=[OB.b])

                    def drive(gens):
                        while gens:
                            for g_ in list(gens):
                                try:
                                    next(g_)
                                except StopIteration:
                                    gens.remove(g_)

                    def step(gens):
                        for g_ in list(gens):
                            try:
                                next(g_)
                            except StopIteration:
                                gens.remove(g_)

                    active = []
                    nxt_hp = 0
                    while nxt_hp < 8 or active:
                        while nxt_hp < 8 and len(active) < 2:
                            hp_prep(nxt_hp)
                            py_ = psB[nxt_hp % 2] if readout else None
                            active.append([nxt_hp, [head_gen(nxt_hp, 0, py_), head_gen(nxt_hp, 1, py_)], py_])
                            nxt_hp += 1
                        for a_ in active:
                            step(a_[1])
                        while active and not active[0][1]:
                            hp_readout(active[0][0], active[0][2])
                            active.pop(0)
                    if sweepB and readout:
                        for n in range(8):
                            i2 = n % 2
                            k.dma(k.sp, x2b[i2][:, 0:N], x2T_d[n, :, t0:t0 + N], x2b[i2].b, writes=[x2b[i2].b])
                            po_ = nxt("A")
                            for c in range(8):
                                k.op(k.pe, lambda: nc.tensor.matmul(po_[:, 0:N], lhsT=wo1[:, c, n * 128:(n + 1) * 128],
                                                                    rhs=OB[:, c, 0:N], start=(c == 0), stop=(c == 7)),
                                     reads=[wo1.b, OB.b], writes=[po_.b], inc=(c == 7))
                            k.op(k.dve, lambda: nc.vector.scalar_tensor_tensor(
                                out=x2b[i2][:, 0:N], in0=po_[:, 0:N], scalar=modv[:, 1, 0, 2, n:n + 1],
                                in1=x2b[i2][:, 0:N], op0=ALU.mult, op1=ALU.add), reads=[po_.b, modv.b, x2b[i2].b],
                                writes=[x2b[i2].b])
                            k.dma(k.pool, x3T_d[n, :, t0:t0 + N], x2b[i2][:, 0:N], x2b[i2].b, reads=[x2b[i2].b])

                lat = [(CL + i * NB, NB, 0) for i in range(L // NB)]
                for st_ in STh:
                    k.op(k.pool, lambda: nc.gpsimd.memset(st_[:, :], 0.0), writes=[st_.b])
                block(0, CL, 1, 0, False)
                for (t0, N, s_) in lat:
                    block(t0, N, s_, 0, False)
                k.barrier()
                for st_ in STh:
                    k.op(k.pool, lambda: nc.gpsimd.memset(st_[:, :], 0.0), writes=[st_.b])
                block(0, CL, 1, 1, True)
                for (t0, N, s_) in reversed(lat):
                    block(t0, N, s_, 1, True)

        blocks0 = [(0, CL, 1)] + [(CL + i * 512, 512, 0) for i in range(L // 512)]

        if stage >= 1:
            with contextlib.ExitStack() as s1:
                wq = tl("wqkv", [128, 8, 1536], BF16, s1)
                with contextlib.ExitStack() as s2:
                    stg = [tl(f"p1_stg{i}", [128, 1536], F32, s2) for i in range(2)]
                    load_weight_bf16(wq, lambda i: wq[:, i, :], lambda i: wqkv_d[i * 128:(i + 1) * 128, :], 8, 1536, stg,
                                     [k.dve, k.act])
                k.barrier()
                xs = [tl(f"xs{i}", [128, D], F32, s1) for i in range(2)]
                xT = tl("p1_xT", [128, 8, 512], F32, s1)
                hT = tl("p1_hT", [128, 8, 512], BF16, s1)
                sq = tl("p1_sq", [128, 8, 512], BF16, s1)
                rstd = tl("p1_rstd", [128, 512], F32, s1)
                tmp = [tl(f"p1_tmp{i}", [128, 512], F32, s1) for i in range(2)]
                qk_out = tl("p1_qk", [128, 10, 512], BF16, s1)
                v_out = tl("p1_v", [128, 4, 256], BF16, s1)
                ct = tl("p1_ct", [128, 512], F32, s1)
                stb = tl("p1_st", [128, 512], F32, s1)
                hsq = [tl(f"p1_hsq{i}", [128, 512], BF16, s1) for i in range(2)]
                hr = [tl(f"p1_hr{i}", [128, 512], F32, s1) for i in range(2)]
                qg = [tl(f"p1_qg{i}", [128, 512], F32, s1) for i in range(2)]
                qgb = [tl(f"p1_qgb{i}", [128, 512], BF16, s1) for i in range(2)]
                t1 = [tl(f"p1_t1{i}", [128, 512], F32, s1) for i in range(2)]
                t2 = [tl(f"p1_t2{i}", [128, 512], F32, s1) for i in range(2)]
                for bi, (t0, N, s) in enumerate(blocks0):
                    src = ctx_d if s == 1 else x_d
                    r0 = t0 if s == 1 else t0 - CL
                    for tt in range(N // 128):
                        xi = xs[tt % 2]
                        k.dma(k.sp, xi[:, :], src[r0 + tt * 128:r0 + (tt + 1) * 128, :], xi.b, writes=[xi.b])
                        for half in range(2):
                            pt = nxt("C")
                            for cc in range(4):
                                c = half * 4 + cc
                                k.op(k.pe, lambda: nc.tensor.transpose(pt[:, cc * 128:(cc + 1) * 128],
                                                                       xi[:, c * 128:(c + 1) * 128], ident),
                                     reads=[xi.b, cst.b], writes=[pt.b], inc=(cc == 3))
                            k.op(k.act, lambda: nc.scalar.copy(
                                out=xT[:, half * 4:(half + 1) * 4, tt * 128:(tt + 1) * 128],
                                in_=pt[:, :].rearrange("p (c t) -> p c t", t=128)), reads=[pt.b], writes=[xT.b])
                    k.dma(k.pool, xT_d[:, :, t0:t0 + N].rearrange("c p t -> p c t"), xT[:, :, 0:N], xT.b, reads=[xT.b])
                    if s == 0:
                        k.dma(k.sp, ct[:, :], ctab_d[:, r0:r0 + 512], ct.b, writes=[ct.b])
                        k.dma(k.sp, stb[:, :], stab_d[:, r0:r0 + 512], stb.b, writes=[stb.b])
                    norm_mod(xT, N, 0, s, 0, hT, sq, rstd, tmp)
                    for n in range(10):
                        pq = nxt("A")
                        for c in range(8):
                            k.op(k.pe, lambda: nc.tensor.matmul(pq[:, 0:N], lhsT=wq[:, c, n * 128:(n + 1) * 128],
                                                                rhs=hT[:, c, 0:N], start=(c == 0), stop=(c == 7)),
                                 reads=[wq.b, hT.b], writes=[pq.b], inc=(c == 7))
                        i2 = n % 2
                        k.op(k.act, lambda: nc.scalar.activation(out=hsq[i2][:, 0:N], in_=pq[:, 0:N], func=AF.Square),
                             reads=[pq.b], writes=[hsq[i2].b])
                        ph = nxt("C")
                        k.op(k.pe, lambda: nc.tensor.matmul(ph[:, 0:N], lhsT=bones_bf, rhs=hsq[i2][:, 0:N], start=True,
                                                            stop=True), reads=[hsq[i2].b, cbf.b], writes=[ph.b])
                        k.op(k.act, lambda: nc.scalar.activation(out=hr[i2][:, 0:N], in_=ph[:, 0:N], func=AF.Sqrt,
                                                                 scale=1.0 / 64, bias=EPS), reads=[ph.b],
                             writes=[hr[i2].b])
                        k.op(k.dve, lambda: nc.vector.reciprocal(out=hr[i2][:, 0:N], in_=hr[i2][:, 0:N]),
                             reads=[hr[i2].b], writes=[hr[i2].b])
                        gcol = qg8[:, 0:1] if n < 8 else qg8[:, 1:2]
                        k.op(k.dve, lambda: nc.vector.scalar_tensor_tensor(
                            out=qg[i2][:, 0:N], in0=pq[:, 0:N], scalar=gcol, in1=hr[i2][:, 0:N], op0=ALU.mult,
                            op1=ALU.mult), reads=[pq.b, qg8.b, hr[i2].b], writes=[qg[i2].b])
                        if s == 1:
                            k.op(k.act, lambda: nc.scalar.copy(out=qk_out[:, n, 0:N], in_=qg[i2][:, 0:N]),
                                 reads=[qg[i2].b], writes=[qk_out.b])
                        else:
                            k.op(k.act, lambda: nc.scalar.copy(out=qgb[i2][:, 0:N], in_=qg[i2][:, 0:N]),
                                 reads=[qg[i2].b], writes=[qgb[i2].b])
                            pp = nxt("C")
                            k.op(k.pe, lambda: nc.tensor.matmul(pp[:, 0:N], lhsT=perm_bf, rhs=qgb[i2][:, 0:N],
                                                                start=True, stop=True), reads=[qgb[i2].b, cbf.b],
                                 writes=[pp.b])
                            k.op(k.dve, lambda: nc.vector.tensor_tensor(out=t1[i2][:, 0:N], in0=qg[i2][:, 0:N],
                                                                         in1=ct[:, 0:N], op=ALU.mult),
                                 reads=[qg[i2].b, ct.b], writes=[t1[i2].b])
                            k.op(k.dve, lambda: nc.vector.tensor_tensor(out=t2[i2][:, 0:N], in0=pp[:, 0:N],
                                                                        in1=stb[:, 0:N], op=ALU.mult),
                                 reads=[pp.b, stb.b], writes=[t2[i2].b])
                            k.op(k.dve, lambda: nc.vector.tensor_tensor(out=qk_out[:, n, 0:N], in0=t1[i2][:, 0:N],
                                                                        in1=t2[i2][:, 0:N], op=ALU.add),
                                 reads=[t1[i2].b, t2[i2].b], writes=[qk_out.b])
                    for tt in range(N // 128):
                        pv = nxt("A")
                        for c in range(8):
                            k.op(k.pe, lambda: nc.tensor.matmul(pv[:, 0:256], lhsT=hT[:, c, tt * 128:(tt + 1) * 128],
                                                                rhs=wq[:, c, 1280:1536], start=(c == 0), stop=(c == 7)),
                                 reads=[wq.b, hT.b], writes=[pv.b], inc=(c == 7))
                        k.op(k.act, lambda: nc.scalar.copy(out=v_out[:, tt, :], in_=pv[:, 0:256]), reads=[pv.b],
                             writes=[v_out.b])
                    k.dma(k.pool, qT_d[:, :, t0:t0 + N].rearrange("c p t -> p c t"), qk_out[:, 0:8, 0:N], qk_out.b,
                          reads=[qk_out.b])
                    k.dma(k.pool, kT_d[:, :, t0:t0 + N].rearrange("c p t -> p c t"), qk_out[:, 8:10, 0:N], qk_out.b,
                          reads=[qk_out.b])
                    k.dma(k.pool, v_d[t0:t0 + N, :].rearrange("(j p) d -> p j d", p=128), v_out[:, 0:N // 128, :],
                          v_out.b, reads=[v_out.b])

        def phase_barrier():
            k.barrier()

        phase_barrier()

        if stage >= 2:
            with contextlib.ExitStack() as s1:
                KT = tl("KT", [128, 2, TT], BF16, s1)
                VA = tl("VA", [128, NKT, 4, 128], BF16, s1)
                wo = tl("wo0", [128, 8, D], BF16, s1)
                with contextlib.ExitStack() as s2:
                    stg = [tl(f"p2_stg{i}", [128, D], F32, s2) for i in range(2)]
                    load_weight_bf16(wo, lambda i: wo[:, i, :], lambda i: wo0_d[i * 128:(i + 1) * 128, :], 8, D, stg,
                                     [k.dve, k.act])
                k.barrier()
                qT = [tl(f"p2_qT{i}", [128, 8, 512], BF16, s1) for i in range(2)]
                PT = [tl(f"p2_PT{i}", [128, 512], BF16, s1) for i in range(3)]
                oT = tl("p2_oT", [128, 8, 512], BF16, s1)
                xT = tl("p2_xT", [128, 8, 512], F32, s1)
                rec = [tl(f"p2_rec{i}", [128, 512], F32, s1) for i in range(2)]
                k.dma(k.sp, KT[:, :, :], kT_d[:, :, :].rearrange("c p t -> p c t"), KT.b, writes=[KT.b])
                k.op(k.pool, lambda: nc.gpsimd.memset(VA[:, :, :, :], 1.0), writes=[VA.b])
                for g in range(4):
                    off = 0 if g % 2 == 0 else 64
                    for j0 in range(0, NKT, 8):
                        j1 = min(NKT, j0 + 8)
                        k.dma(k.sp, VA[:, j0:j1, g, off:off + 64],
                              v_d[j0 * 128:j1 * 128, g * 64:(g + 1) * 64].rearrange("(j p) d -> p j d", p=128), VA.b,
                              writes=[VA.b])
                for bi, (t0, N, s) in enumerate(blocks0):
                    q = qT[bi % 2]
                    k.dma(k.sp, q[:, :, 0:N], qT_d[:, :, t0:t0 + N].rearrange("c p t -> p c t"), q.b, writes=[q.b])
                    k.dma(k.sp, xT[:, :, 0:N], xT_d[:, :, t0:t0 + N].rearrange("c p t -> p c t"), xT.b, writes=[xT.b])
                    nkt = 2 if s == 1 else NKT
                    hidx = 0
                    for c in range(8):
                        for sl in range(2):
                            g = (c // 4) * 2 + sl
                            po = psB[hidx % 2]
                            lo, hi = sl * 64, (sl + 1) * 64
                            pss = [None] * nkt
                            for j in range(nkt + 2):
                                if j < nkt:
                                    ps_ = nxt("A")
                                    pss[j] = ps_
                                    k.op(k.pe, lambda: nc.tensor.matmul(ps_[:, 0:N],
                                                                        lhsT=KT[lo:hi, g // 2, j * 128:(j + 1) * 128],
                                                                        rhs=q[lo:hi, c, 0:N], start=True, stop=True),
                                         reads=[KT.b, q.b], writes=[ps_.b])
                                    pt_ = PT[j % 3]
                                    k.op(k.act, lambda: nc.scalar.activation(out=pt_[:, 0:N], in_=ps_[:, 0:N],
                                                                             func=AF.Exp), reads=[ps_.b],
                                         writes=[pt_.b])
                                if j >= 2:
                                    jj = j - 2
                                    pt2 = PT[jj % 3]
                                    k.op(k.pe, lambda: nc.tensor.matmul(po[:, 0:N], lhsT=VA[:, jj, g, :],
                                                                        rhs=pt2[:, 0:N], start=(jj == 0),
                                                                        stop=(jj == nkt - 1)),
                                         reads=[VA.b, pt2.b], writes=[po.b], inc=(jj == nkt - 1))
                            rc = rec[hidx % 2]
                            olo, ohi = (64, 128) if sl == 0 else (0, 64)
                            k.op(k.dve, lambda: nc.vector.reciprocal(out=rc[lo:hi, 0:N], in_=po[olo:ohi, 0:N]),
                                 reads=[po.b], writes=[rc.b])
                            k.op(k.dve, lambda: nc.vector.tensor_tensor(out=oT[lo:hi, c, 0:N], in0=po[lo:hi, 0:N],
                                                                        in1=rc[lo:hi, 0:N], op=ALU.mult),
                                 reads=[po.b, rc.b], writes=[oT.b])
                            hidx += 1
                    for n in range(8):
                        py = nxt("C")
                        for c in range(8):
                            k.op(k.pe, lambda: nc.tensor.matmul(py[:, 0:N], lhsT=wo[:, c, n * 128:(n + 1) * 128],
                                                                rhs=oT[:, c, 0:N], start=(c == 0), stop=(c == 7)),
                                 reads=[wo.b, oT.b], writes=[py.b], inc=(c == 7))
                        k.op(k.dve, lambda: nc.vector.scalar_tensor_tensor(
                            out=xT[:, n, 0:N], in0=py[:, 0:N], scalar=modv[:, 0, s, 2, n:n + 1], in1=xT[:, n, 0:N],
                            op0=ALU.mult, op1=ALU.add), reads=[py.b, modv.b, xT.b], writes=[xT.b])
                    k.dma(k.pool, x1T_d[:, :, t0:t0 + N].rearrange("c p t -> p c t"), xT[:, :, 0:N], xT.b, reads=[xT.b])
            phase_barrier()

        if stage >= 3:
            zt = tl("zeros", [128, 8, 1], F32)
            k.op(k.pool, lambda: nc.gpsimd.memset(zt[:, :, :], 0.0), writes=[zt.b])
            for (dd, n_) in ((h1c_d, CL), (h1l_d, L)):
                for col in (0, n_ + 1):
                    k.dma(k.pool, dd[:, :, col:col + 1].rearrange("c p t -> p c t"), zt[:, :, :], zt.b, reads=[zt.b],
                          allow_slow_non_contiguous=True)

            def post_h1(t0, N, s_, xT, hT, sq, rstd, tmp, sg):
                rms_stats(xT, N, sq, rstd)
                dst = h1c_d if s_ == 1 else h1l_d
                r0 = (t0 if s_ == 1 else t0 - CL) + 1
                for c in range(8):
                    tb = tmp[c % 2]
                    ob = sg[c % 2]
                    k.op(k.dve, lambda: nc.vector.scalar_tensor_tensor(
                        out=tb[:, 0:N], in0=xT[:, c, 0:N], scalar=modv[:, 1, s_, 1, c:c + 1], in1=rstd[:, 0:N],
                        op0=ALU.mult, op1=ALU.mult), reads=[xT.b, modv.b, rstd.b], writes=[tb.b])
                    k.op(k.act, lambda: nc.scalar.activation(out=ob[:, 0:N], in_=tb[:, 0:N], func=AF.Identity,
                                                             bias=modv[:, 1, s_, 0, c:c + 1]),
                         reads=[tb.b, modv.b], writes=[ob.b])
                    k.dma(k.pool, dst[c, :, r0:r0 + N], ob[:, 0:N], ob.b, reads=[ob.b])

            ffn_phase(0, x1T_d, x2T_d, blocks0, post_fn=post_h1)
            phase_barrier()

        if stage >= 4:
            rwkv_phase()
            phase_barrier()

        if stage >= 6:
            def post_final(t0, N, s_, xT, hT, sq, rstd, tmp, sg):
                rms_stats(xT, N, sq, rstd)
                fg = V("final_g", 0, 8)
                for c in range(8):
                    k.op(k.dve, lambda: nc.vector.scalar_tensor_tensor(
                        out=xT[:, c, 0:N], in0=xT[:, c, 0:N], scalar=fg[:, c:c + 1], in1=rstd[:, 0:N],
                        op0=ALU.mult, op1=ALU.mult), reads=[xT.b, vecs.b, rstd.b], writes=[xT.b])
                for tt in range(N // 128):
                    for half in range(2):
                        pt = nxt("C")
                        for cc in range(4):
                            c = half * 4 + cc
                            k.op(k.pe, lambda: nc.tensor.transpose(pt[:, cc * 128:(cc + 1) * 128],
                                                                   xT[:, c, tt * 128:(tt + 1) * 128], ident),
                                 reads=[xT.b, cst.b], writes=[pt.b], inc=(cc == 3))
                        o = sg[half]
                        k.op(k.act, lambda: nc.scalar.copy(out=o[:, :], in_=pt[:, :]), reads=[pt.b], writes=[o.b])
                        r = (t0 - CL) + tt * 128
                        k.dma(k.pool, out_d[r:r + 128, half * 512:(half + 1) * 512], o[:, :], o.b, reads=[o.b])

            ffn_phase(1, x3T_d, None, blocks0[1:], post_fn=post_final)
            phase_barrier()

        dbg_src = {1: xT_d, 2: x1T_d, 3: x2T_d, 5: x3T_d}.get(stage)
        if dbg_src is not None:
            with contextlib.ExitStack() as s1:
                xT = tl("o_xT", [128, 8, 512], F32, s1)
                ot = [tl(f"o_t{i}", [128, D], F32, s1) for i in range(2)]
                for bi in range(L // 512):
                    t0 = CL + bi * 512
                    k.dma(k.sp, xT[:, :, :], dbg_src[:, :, t0:t0 + 512].rearrange("c p t -> p c t"), xT.b,
                          writes=[xT.b])
                    for tt in range(4):
                        o = ot[tt % 2]
                        for half in range(2):
                            pt = nxt("C")
                            for cc in range(4):
                                c = half * 4 + cc
                                k.op(k.pe, lambda: nc.tensor.transpose(pt[:, cc * 128:(cc + 1) * 128],
                                                                       xT[:, c, tt * 128:(tt + 1) * 128], ident),
                                     reads=[xT.b, cst.b], writes=[pt.b], inc=(cc == 3))
                            k.op(k.act, lambda: nc.scalar.copy(out=o[:, half * 512:(half + 1) * 512], in_=pt[:, :]),
                                 reads=[pt.b], writes=[o.b])
                        r = bi * 512 + tt * 128
                        k.dma(k.pool, out_d[r:r + 128, :], o[:, :], o.b, reads=[o.b])
        k.finish()
        print(f"[build] L={L} stage={stage} inst={k.n_inst} waits={k.n_wait}")
    return nc


_CACHE = {}


def prep_inputs(inp, b, L):
    wqkv_p, wo_p = _CACHE.get("w") or host_weights(inp)
    _CACHE["w"] = (wqkv_p, wo_p)
    cm, ct, stb = _CACHE.get(("c", L)) or make_consts(L)
    _CACHE[("c", L)] = (cm, ct, stb)
    f = lambda a: np.ascontiguousarray(np.asarray(a, np.float32))
    return {
        "x": f(inp["x"][b, :L]), "ctx": f(inp["ctx"][b]), "vecs": pack_vecs(inp, b), "consts": cm, "ctab": ct,
        "stab": stb, "mod_w": f(inp["mod_w"]), "wqkv": wqkv_p, "wo0": wo_p, "ffn_wg": f(inp["ffn_wg"]),
        "ffn_wu": f(inp["ffn_wu"]), "ffn_wd": f(inp["ffn_wd"]),
        "wrkv": f(inp["rwkv_wrkv"][0]), "wo1": f(inp["rwkv_wo"][0]),
        "w1c": f(np.concatenate([inp["rwkv_w1"][0][0], inp["rwkv_w1"][0][1]], axis=1)),
        "a1c": f(np.concatenate([inp["rwkv_a1"][0][0], inp["rwkv_a1"][0][1]], axis=1)),
        "g1": f(inp["rwkv_g1"][0]),
        "w2c": f(np.concatenate([inp["rwkv_w2"][0][0], inp["rwkv_w2"][0][1]], axis=0)),
        "a2c": f(np.concatenate([inp["rwkv_a2"][0][0], inp["rwkv_a2"][0][1]], axis=0)),
        "g2": f(inp["rwkv_g2"][0]), "masks": make_masks(),
    }


def kernel(**inputs):
    B, L, _ = inputs["x"].shape
    inp = {kk: np.asarray(v) for kk, v in inputs.items()}
    nc = build(L)
    in_maps = [prep_inputs(inp, b, L) for b in range(B)]
    res = run_bass_kernel_spmd(nc, in_maps, core_ids=list(range(B)))
    return np.stack([np.asarray(r["out"], np.float32) for r in res.results], axis=0)
```

```python
import contextlib
import numpy as np
import concourse.bass as bass
import concourse.mybir as mybir
from concourse.bass_utils import run_bass_kernel_spmd

F32 = mybir.dt.float32
BF16 = mybir.dt.bfloat16
AF = mybir.ActivationFunctionType
ALU = mybir.AluOpType

D = 1024
NCH = 8
CL = 256
DFF = 2816
NF = 22
EPS = 1e-6
GN_EPS = 64e-5


class Buf:
    __slots__ = ("name", "w", "r", "dsem", "dcnt")

    def __init__(self, name):
        self.name = name
        self.w = None
        self.r = {}
        self.dsem = None
        self.dcnt = 0


class Eng:
    def __init__(self, key, eng, sem):
        self.key = key
        self.eng = eng
        self.sem = sem
        self.cnt = 0
        self.waited = {}


class K:
    def __init__(self, nc, stack):
        self.nc = nc
        self.stack = stack
        self.pe = Eng("pe", nc.tensor, self._sem("c_pe"))
        self.act = Eng("act", nc.scalar, self._sem("c_act"))
        self.dve = Eng("dve", nc.vector, self._sem("c_dve"))
        self.pool = Eng("pool", nc.gpsimd, self._sem("c_pool"))
        self.sp = Eng("sp", nc.sync, self._sem("c_sp"))
        self.n_inst = 0
        self.n_wait = 0
        self.dma_bufs = []

    def _sem(self, name):
        return self.stack.enter_context(self.nc.semaphore(name))

    def _need(self, e, sem, val):
        if val > e.waited.get(id(sem), 0):
            e.eng.wait_ge(sem, val)
            e.waited[id(sem)] = val
            self.n_wait += 1

    def _pre(self, e, reads, writes, is_dma=False):
        for b in reads:
            if b.w is not None:
                sem, val, key = b.w
                self._need(e, sem, val)
        for b in writes:
            if b.w is not None:
                sem, val, key = b.w
                if is_dma or key != e.key:
                    self._need(e, sem, val)
            for key, (sem, val) in b.r.items():
                if is_dma or key != e.key:
                    self._need(e, sem, val)

    def op(self, e, fn, reads=(), writes=(), inc=True):
        self._pre(e, reads, writes)
        ins = fn()
        self.n_inst += 1
        if inc:
            ins.then_inc(e.sem, 1)
            e.cnt += 1
            val = e.cnt
        else:
            val = e.cnt + 1
        for b in reads:
            b.r[e.key] = (e.sem, val)
        for b in writes:
            b.w = (e.sem, val, e.key)
            b.r = {}
        return ins

    def dma(self, q, out_ap, in_ap, sb, reads=(), writes=(), **kw):
        if sb.dsem is None:
            sb.dsem = self._sem("d_" + sb.name)
            self.dma_bufs.append(sb)
        self._pre(q, reads, writes, is_dma=True)
        ins = q.eng.dma_start(out=out_ap, in_=in_ap, **kw)
        ins.then_inc(sb.dsem, 16)
        sb.dcnt += 16
        self.n_inst += 1
        key = "dma_" + sb.name
        for b in reads:
            b.r[key] = (sb.dsem, sb.dcnt)
        for b in writes:
            b.w = (sb.dsem, sb.dcnt, key)
            b.r = {}
        return ins

    def finish(self):
        for sb in self.dma_bufs:
            self._need(self.sp, sb.dsem, sb.dcnt)

    def barrier(self):
        engs = [self.pe, self.act, self.dve, self.pool, self.sp]
        for e in engs:
            for o in engs:
                if o is not e and o.cnt > 0:
                    self._need(e, o.sem, o.cnt)
            for sb in self.dma_bufs:
                if sb.dcnt > 0:
                    self._need(e, sb.dsem, sb.dcnt)


class T:
    def __init__(self, k, stack, name, shape, dt, psum=False):
        nc = k.nc
        if psum:
            self.t = stack.enter_context(nc.psum_tensor("t_" + name, list(shape), dt))
        else:
            self.t = stack.enter_context(nc.sbuf_tensor("t_" + name, list(shape), dt))
        self.b = Buf(name)

    def __getitem__(self, key):
        return self.t[key]


def _perm_d():
    i = np.arange(64)
    return ((i % 32) // 16) * 32 + (i // 32) * 16 + (i % 16)


HEAD_OF = [[0, 1, 2, 3, 8, 9, 10, 11], [4, 5, 6, 7, 12, 13, 14, 15]]


def _fm(v):
    v = np.asarray(v, np.float32).reshape(-1, 128)
    return np.ascontiguousarray(v.T)


class VecPack:
    def __init__(self):
        self.cols = []
        self.off = {}
        self.n = 0

    def add(self, name, arr):
        arr = np.asarray(arr, np.float32)
        assert arr.shape[0] == 128
        self.off[name] = (self.n, arr.shape[1])
        self.cols.append(arr)
        self.n += arr.shape[1]

    def build(self):
        return np.ascontiguousarray(np.concatenate(self.cols, axis=1))


def vec_layout():
    names = [("c", 8), ("c_ctx", 8), ("mod_b0", 48), ("mod_b1", 48), ("n1g0", 8), ("n1g1", 8), ("n2g0", 8),
             ("n2g1", 8), ("final_g", 8), ("qg", 1), ("kg", 1), ("mix", 48), ("w0", 16), ("a0", 16), ("k_k", 8),
             ("k_a", 8), ("r_k", 8), ("ln_g", 8), ("ln_b", 8)]
    off = {}
    n = 0
    for nm, w in names:
        off[nm] = (n, w)
        n += w
    return off, n


def pack_vecs(inp, b):
    pd = _perm_d()
    vp = VecPack()
    vp.add("c", _fm(inp["c"][b]))
    vp.add("c_ctx", _fm(inp["c_ctx"]))
    vp.add("mod_b0", _fm(inp["mod_b"][0]))
    vp.add("mod_b1", _fm(inp["mod_b"][1]))
    vp.add("n1g0", _fm(inp["norm1_g"][0]))
    vp.add("n1g1", _fm(inp["norm1_g"][1]))
    vp.add("n2g0", _fm(inp["norm2_g"][0]))
    vp.add("n2g1", _fm(inp["norm2_g"][1]))
    vp.add("final_g", _fm(inp["final_g"]))
    vp.add("qg", np.tile(inp["attn_q_gain"][0][pd], 2).reshape(128, 1))
    vp.add("kg", np.tile(inp["attn_k_gain"][0][pd], 2).reshape(128, 1))
    vp.add("mix", np.concatenate([_fm(inp["rwkv_mix"][0][j]) for j in range(6)], axis=1))
    vp.add("w0", np.concatenate([_fm(inp["rwkv_w0"][0][d]) for d in range(2)], axis=1))
    vp.add("a0", np.concatenate([_fm(inp["rwkv_a0"][0][d]) for d in range(2)], axis=1))
    vp.add("k_k", _fm(inp["rwkv_k_k"][0]))
    vp.add("k_a", _fm(inp["rwkv_k_a"][0]))
    vp.add("r_k", _fm(inp["rwkv_r_k"][0].reshape(-1)))
    vp.add("ln_g", _fm(inp["rwkv_ln_g"][0]))
    vp.add("ln_b", _fm(inp["rwkv_ln_b"][0]))
    off, n = vec_layout()
    assert vp.off == off
    return vp.build()


def make_consts(L):
    c = {}
    c["ident"] = np.eye(128, dtype=np.float32)
    p = np.zeros((128, 128), np.float32)
    for m in range(128):
        p[m ^ 32, m] = 1.0
    c["perm32"] = p
    bo = np.zeros((128, 128), np.float32)
    bo[:64, :64] = 1.0
    bo[64:, 64:] = 1.0
    c["blockones"] = bo
    c["allones"] = np.ones((128, 128), np.float32)
    cm = np.concatenate([c["ident"], c["perm32"], c["blockones"], c["allones"]], axis=1)
    t = np.arange(L)
    row = (t // 64).astype(np.float32)
    col = (t % 64).astype(np.float32)
    inv_freq = (np.float32(10000.0) ** (-np.arange(16, dtype=np.float32) / np.float32(16))).astype(np.float32)
    ang = np.stack([row[:, None] * inv_freq, col[:, None] * inv_freq], axis=1)
    cos = np.cos(ang).astype(np.float32)
    sin = np.sin(ang).astype(np.float32)
    ct = np.zeros((128, L), np.float32)
    stb = np.zeros((128, L), np.float32)
    for r in range(128):
        i = r % 64
        a = (i % 32) // 16
        pp = i % 16
        ct[r] = cos[:, a, pp]
        stb[r] = -sin[:, a, pp] if i < 32 else sin[:, a, pp]
    return np.ascontiguousarray(cm), ct, stb


def make_masks():
    i = np.arange(64)
    us = (i[:, None] < i[None, :]).astype(np.float32)
    ui = (i[:, None] <= i[None, :]).astype(np.float32)
    m = np.concatenate([us, ui, us.T, ui.T], axis=1)
    return np.ascontiguousarray(np.concatenate([m, m], axis=0))


def host_weights(inp):
    pd = _perm_d()
    wqkv = np.asarray(inp["attn_wqkv"][0], np.float32)
    qcols = []
    for c in range(8):
        for s in range(2):
            h = HEAD_OF[s][c]
            qcols.append(h * 64 + pd)
    qcols = np.concatenate(qcols)
    kcols = np.concatenate([1024 + g * 64 + pd for g in range(4)])
    vcols = 1280 + np.arange(256)
    wqkv_p = np.ascontiguousarray(wqkv[:, np.concatenate([qcols, kcols, vcols])])
    orows = np.concatenate([HEAD_OF[s][c] * 64 + np.arange(64) for c in range(8) for s in range(2)])
    wo_p = np.ascontiguousarray(np.asarray(inp["attn_wo"][0], np.float32)[orows, :])
    return wqkv_p, wo_p


def build(L, stage=99):
    TT = CL + L
    NKT = TT // 128
    nc = bass.Bass("TRN2", target_bir_lowering=False)
    voff, NV = vec_layout()

    def din(name, shape, dt=F32):
        return nc.dram_tensor(name, list(shape), dt, kind="ExternalInput").ap()

    def dscr(name, shape, dt=F32):
        return nc.dram_tensor(name, list(shape), dt).ap()

    x_d = din("x", [L, D])
    ctx_d = din("ctx", [CL, D])
    vecs_d = din("vecs", [128, NV])
    consts_d = din("consts", [128, 512])
    ctab_d = din("ctab", [128, L])
    stab_d = din("stab", [128, L])
    modw_d = din("mod_w", [2, D, 6 * D])
    wqkv_d = din("wqkv", [D, 1536])
    wo0_d = din("wo0", [D, D])
    wg_d = din("ffn_wg", [2, D, DFF])
    wu_d = din("ffn_wu", [2, D, DFF])
    wd_d = din("ffn_wd", [2, DFF, D])
    wrkv_d = din("wrkv", [3, D, D])
    wo1_d = din("wo1", [D, D])
    w1c_d = din("w1c", [D, 128])
    a1c_d = din("a1c", [D, 128])
    g1_d = din("g1", [D, 128])
    w2c_d = din("w2c", [128, D])
    a2c_d = din("a2c", [128, D])
    g2_d = din("g2", [128, D])
    masks_d = din("masks", [128, 256])
    out_d = nc.dram_tensor("out", [L, D], F32, kind="ExternalOutput").ap()

    xT_d = dscr("xT_s", [NCH, 128, TT])
    qT_d = dscr("qT_s", [NCH, 128, TT], BF16)
    kT_d = dscr("kT_s", [2, 128, TT], BF16)
    v_d = dscr("v_s", [TT, 256], BF16)
    x1T_d = dscr("x1T_s", [NCH, 128, TT])
    x2T_d = dscr("x2T_s", [NCH, 128, TT])
    h1c_d = dscr("h1c_s", [NCH, 128, CL + 2])
    h1l_d = dscr("h1l_s", [NCH, 128, L + 2])
    yf_d = dscr("yf_s", [NCH, 128, L])
    x3T_d = dscr("x3T_s", [NCH, 128, TT])

    st = contextlib.ExitStack()
    with st:
        k = K(nc, st)

        uniq = {"n": 0}

        def tl(name, shape, dt, stack=None, psum=False):
            uniq["n"] += 1
            return T(k, stack or st, f"{name}_{uniq['n']}", shape, dt, psum=psum)

        vecs = tl("vecs", [128, NV], F32)
        k.dma(k.sp, vecs[:, :], vecs_d[:, :], vecs.b, writes=[vecs.b])
        cst = tl("cst", [128, 512], F32)
        k.dma(k.sp, cst[:, :], consts_d[:, :], cst.b, writes=[cst.b])
        cbf = tl("cbf", [128, 512], BF16)
        k.op(k.dve, lambda: nc.vector.tensor_copy(out=cbf[:, :], in_=cst[:, :]), reads=[cst.b], writes=[cbf.b])
        ident = cst[:, 0:128]
        perm_bf = cbf[:, 128:256]
        bones_bf = cbf[:, 256:384]
        ones_bf = cbf[:, 384:512]
        modv = tl("modv", [128, 2, 2, 6, 8], F32)
        qg8 = tl("qg8", [128, 2], F32)

        def V(name, i=0, n=1):
            o, w = voff[name]
            return vecs[:, o + i:o + i + n]

        k.op(k.dve, lambda: nc.vector.tensor_scalar(out=qg8[:, 0:1], in0=V("qg"), scalar1=0.125, scalar2=None,
                                                    op0=ALU.mult), reads=[vecs.b], writes=[qg8.b])
        k.op(k.dve, lambda: nc.vector.tensor_copy(out=qg8[:, 1:2], in_=V("kg")), reads=[vecs.b], writes=[qg8.b])

        psA = [tl(f"psA{i}", [128, 512], F32, psum=True) for i in range(3)]
        psB = [tl(f"psB{i}", [128, 512], F32, psum=True) for i in range(2)]
        psC = [tl(f"psC{i}", [128, 512], F32, psum=True) for i in range(3)]
        rr = {"A": 0, "B": 0, "C": 0}

        def nxt(which):
            lst = {"A": psA, "B": psB, "C": psC}[which]
            i = rr[which]
            rr[which] = (i + 1) % len(lst)
            return lst[i]

        with contextlib.ExitStack() as s0:
            sc = tl("sc", [128, 8, 2], F32, s0)
            k.op(k.act, lambda: nc.scalar.activation(out=sc[:, :, 0], in_=V("c", 0, 8), func=AF.Silu),
                 reads=[vecs.b], writes=[sc.b])
            k.op(k.act, lambda: nc.scalar.activation(out=sc[:, :, 1], in_=V("c_ctx", 0, 8), func=AF.Silu),
                 reads=[vecs.b], writes=[sc.b])
            mst = [tl(f"mst{i}", [128, 6 * D], F32, s0) for i in range(2)]
            macc = tl("macc", [128, 96], F32, s0)
            for l in range(2):
                for kc in range(8):
                    ms = mst[kc % 2]
                    k.dma(k.sp, ms[:, :], modw_d[l, kc * 128:(kc + 1) * 128, :], ms.b, writes=[ms.b])
                    pm = nxt("C")
                    for n in range(48):
                        k.op(k.pe, lambda: nc.tensor.matmul(pm[:, 2 * n:2 * n + 2], lhsT=ms[:, n * 128:(n + 1) * 128],
                                                            rhs=sc[:, kc, :], start=True, stop=True),
                             reads=[ms.b, sc.b], writes=[pm.b], inc=(n == 47))
                    if kc == 0:
                        k.op(k.dve, lambda: nc.vector.tensor_copy(out=macc[:, :], in_=pm[:, 0:96]),
                             reads=[pm.b], writes=[macc.b])
                    else:
                        k.op(k.dve, lambda: nc.vector.tensor_tensor(out=macc[:, :], in0=macc[:, :], in1=pm[:, 0:96],
                                                                    op=ALU.add),
                             reads=[pm.b, macc.b], writes=[macc.b])
                mb = V(f"mod_b{l}", 0, 48)
                for s in range(2):
                    mv = macc[:, :].rearrange("p (n s) -> p n s", s=2)[:, :, s]
                    tmp = tl(f"mtmp{l}{s}", [128, 48], F32, s0)
                    k.op(k.dve, lambda: nc.vector.tensor_tensor(out=tmp[:, :], in0=mv, in1=mb, op=ALU.add),
                         reads=[macc.b, vecs.b], writes=[tmp.b])
                    for (dst, src) in ((0, 0), (2, 2), (3, 3), (5, 5)):
                        k.op(k.dve, lambda: nc.vector.tensor_copy(out=modv[:, l, s, dst, :],
                                                                  in_=tmp[:, src * 8:(src + 1) * 8]),
                             reads=[tmp.b], writes=[modv.b])
                    for (dst, src, gname) in ((1, 1, f"n1g{l}"), (4, 4, f"n2g{l}")):
                        k.op(k.dve, lambda: nc.vector.scalar_tensor_tensor(
                            out=modv[:, l, s, dst, :], in0=tmp[:, src * 8:(src + 1) * 8], scalar=1.0,
                            in1=V(gname, 0, 8), op0=ALU.add, op1=ALU.mult),
                            reads=[tmp.b, vecs.b], writes=[modv.b])

        k.barrier()
        def load_weight_bf16(dst, dst_view_fn, src_rows_fn, nrow_tiles, ncols, stg, engs):
            for i in range(nrow_tiles):
                sg = stg[i % len(stg)]
                k.dma(k.sp, sg[:, 0:ncols], src_rows_fn(i), sg.b, writes=[sg.b])
                e = engs[i % len(engs)]
                if e is k.act:
                    k.op(e, lambda: nc.scalar.copy(out=dst_view_fn(i), in_=sg[:, 0:ncols]), reads=[sg.b], writes=[dst.b])
                elif e is k.dve:
                    k.op(e, lambda: nc.vector.tensor_copy(out=dst_view_fn(i), in_=sg[:, 0:ncols]), reads=[sg.b],
                         writes=[dst.b])
                else:
                    k.op(e, lambda: nc.gpsimd.tensor_copy(out=dst_view_fn(i), in_=sg[:, 0:ncols]), reads=[sg.b],
                         writes=[dst.b])

        def rms_stats(xT, N, sq, rstd):
            for c in range(8):
                k.op(k.act, lambda: nc.scalar.activation(out=sq[:, c, 0:N], in_=xT[:, c, 0:N], func=AF.Square),
                     reads=[xT.b], writes=[sq.b])
            ps = nxt("C")
            for c in range(8):
                k.op(k.pe, lambda: nc.tensor.matmul(ps[:, 0:N], lhsT=ones_bf, rhs=sq[:, c, 0:N], start=(c == 0),
                                                    stop=(c == 7)),
                     reads=[sq.b, cbf.b], writes=[ps.b], inc=(c == 7))
            k.op(k.act, lambda: nc.scalar.activation(out=rstd[:, 0:N], in_=ps[:, 0:N], func=AF.Sqrt, scale=1.0 / D,
                                                     bias=EPS),
                 reads=[ps.b], writes=[rstd.b])
            k.op(k.dve, lambda: nc.vector.reciprocal(out=rstd[:, 0:N], in_=rstd[:, 0:N]), reads=[rstd.b],
                 writes=[rstd.b])

        def norm_mod(xT, N, l, s, which, hT, sq, rstd, tmp):
            rms_stats(xT, N, sq, rstd)
            jsh, jgm = (0, 1) if which == 0 else (3, 4)
            for c in range(8):
                tb = tmp[c % len(tmp)]
                k.op(k.dve, lambda: nc.vector.scalar_tensor_tensor(
                    out=tb[:, 0:N], in0=xT[:, c, 0:N], scalar=modv[:, l, s, jgm, c:c + 1], in1=rstd[:, 0:N],
                    op0=ALU.mult, op1=ALU.mult), reads=[xT.b, modv.b, rstd.b], writes=[tb.b])
                k.op(k.act, lambda: nc.scalar.activation(out=hT[:, c, 0:N], in_=tb[:, 0:N], func=AF.Identity,
                                                         bias=modv[:, l, s, jsh, c:c + 1]),
                     reads=[tb.b, modv.b], writes=[hT.b])

        def ffn_phase(l, src_d, dst_d, blocks, post_fn=None):
            with contextlib.ExitStack() as sp_:
                wg = tl("wg", [128, 8, DFF], BF16, sp_)
                wu = tl("wu", [128, 8, DFF], BF16, sp_)
                wd = tl("wd", [128, NF, D], BF16, sp_)
                with contextlib.ExitStack() as s2:
                    stg = [tl(f"f_stg{i}", [128, DFF], F32, s2) for i in range(2)]
                    engs = [k.dve, k.act]
                    load_weight_bf16(wg, lambda i: wg[:, i, :], lambda i: wg_d[l, i * 128:(i + 1) * 128, :], 8, DFF,
                                     stg, engs)
                    load_weight_bf16(wu, lambda i: wu[:, i, :], lambda i: wu_d[l, i * 128:(i + 1) * 128, :], 8, DFF,
                                     stg, engs)
                    load_weight_bf16(wd, lambda i: wd[:, i, :], lambda i: wd_d[l, i * 128:(i + 1) * 128, :], NF, D,
                                     stg, engs)
                k.barrier()
                xT = tl("f_xT", [128, 8, 512], F32, sp_)
                hT = tl("f_hT", [128, 8, 512], BF16, sp_)
                aT = tl("f_aT", [128, NF, 512], BF16, sp_)
                rstd = tl("f_rstd", [128, 512], F32, sp_)
                tmp = [tl(f"f_tmp{i}", [128, 512], F32, sp_) for i in range(2)]
                sg = [tl(f"f_sg{i}", [128, 512], F32, sp_) for i in range(2)]
                for (t0, N, s) in blocks:
                    k.dma(k.sp, xT[:, :, 0:N], src_d[:, :, t0:t0 + N].rearrange("c p t -> p c t"), xT.b,
                          writes=[xT.b])
                    sq = aT
                    norm_mod(xT, N, l, s, 1, hT, T_alias(aT, "sq"), rstd, tmp)
                    for f in range(NF):
                        pg = nxt("A")
                        for c in range(8):
                            k.op(k.pe, lambda: nc.tensor.matmul(pg[:, 0:N], lhsT=wg[:, c, f * 128:(f + 1) * 128],
                                                                rhs=hT[:, c, 0:N], start=(c == 0), stop=(c == 7)),
                                 reads=[wg.b, hT.b], writes=[pg.b], inc=(c == 7))
                        pu = nxt("C")
                        for c in range(8):
                            k.op(k.pe, lambda: nc.tensor.matmul(pu[:, 0:N], lhsT=wu[:, c, f * 128:(f + 1) * 128],
                                                                rhs=hT[:, c, 0:N], start=(c == 0), stop=(c == 7)),
                                 reads=[wu.b, hT.b], writes=[pu.b], inc=(c == 7))
                        sgi = sg[f % 2]
                        k.op(k.act, lambda: nc.scalar.activation(out=sgi[:, 0:N], in_=pg[:, 0:N], func=AF.Silu),
                             reads=[pg.b], writes=[sgi.b])
                        k.op(k.dve, lambda: nc.vector.tensor_tensor(out=aT[:, f, 0:N], in0=sgi[:, 0:N],
                                                                    in1=pu[:, 0:N], op=ALU.mult),
                             reads=[sgi.b, pu.b], writes=[aT.b])
                    for n in range(8):
                        pd_ = nxt("B")
                        for f in range(NF):
                            k.op(k.pe, lambda: nc.tensor.matmul(pd_[:, 0:N], lhsT=wd[:, f, n * 128:(n + 1) * 128],
                                                                rhs=aT[:, f, 0:N], start=(f == 0), stop=(f == NF - 1)),
                                 reads=[wd.b, aT.b], writes=[pd_.b], inc=(f == NF - 1))
                        k.op(k.dve, lambda: nc.vector.scalar_tensor_tensor(
                            out=xT[:, n, 0:N], in0=pd_[:, 0:N], scalar=modv[:, l, s, 5, n:n + 1], in1=xT[:, n, 0:N],
                            op0=ALU.mult, op1=ALU.add), reads=[pd_.b, modv.b, xT.b], writes=[xT.b])
                    if dst_d is not None:
                        k.dma(k.pool, dst_d[:, :, t0:t0 + N].rearrange("c p t -> p c t"), xT[:, :, 0:N], xT.b,
                              reads=[xT.b])
                    if post_fn is not None:
                        post_fn(t0, N, s, xT, hT, T_alias(aT, "sq"), rstd, tmp, sg)

        def T_alias(t, name):
            return t


        def rwkv_phase():
            NB = 256
            C0 = float(np.exp(-0.5))
            with contextlib.ExitStack() as s1:
                Wr = tl("Wr", [128, 8, D], BF16, s1)
                Wk = tl("Wk", [128, 8, D], BF16, s1)
                Wv = tl("Wv", [128, 8, D], BF16, s1)
                wo1 = tl("wo1", [128, 8, D], BF16, s1)
                w1c = tl("w1c", [128, 8, 128], BF16, s1)
                a1c = tl("a1c", [128, 8, 128], BF16, s1)
                g1 = tl("g1", [128, 8, 128], BF16, s1)
                w2c = tl("w2c", [128, D], BF16, s1)
                a2c = tl("a2c", [128, D], BF16, s1)
                g2 = tl("g2", [128, D], BF16, s1)
                with contextlib.ExitStack() as s2:
                    stg = [tl(f"r_stg{i}", [128, D], F32, s2) for i in range(2)]
                    engs = [k.dve, k.act]
                    for (wt, j) in ((Wr, 0), (Wk, 1), (Wv, 2)):
                        load_weight_bf16(wt, lambda i: wt[:, i, :], lambda i: wrkv_d[j, i * 128:(i + 1) * 128, :], 8, D,
                                         stg, engs)
                    load_weight_bf16(wo1, lambda i: wo1[:, i, :], lambda i: wo1_d[i * 128:(i + 1) * 128, :], 8, D, stg,
                                     engs)
                    for (wt, src) in ((w1c, w1c_d), (a1c, a1c_d), (g1, g1_d)):
                        load_weight_bf16(wt, lambda i: wt[:, i, :], lambda i: src[i * 128:(i + 1) * 128, :], 8, 128, stg,
                                         engs)
                    for (wt, src) in ((w2c, w2c_d), (a2c, a2c_d), (g2, g2_d)):
                        load_weight_bf16(wt, lambda i: wt[:, :], lambda i: src[:, :], 1, D, stg, engs)
                k.barrier()
                msk = tl("msk", [128, 256], F32, s1)
                k.dma(k.sp, msk[:, :], masks_d[:, :], msk.b, writes=[msk.b])
                rst = tl("rst", [128, NB], F32, s1)
                k.op(k.pool, lambda: nc.gpsimd.memset(rst[:, :], 1.0), writes=[rst.b])
                k.op(k.pool, lambda: nc.gpsimd.memset(rst[:, :].rearrange("p (c t) -> p c t", t=64)[:, :, 0:1], 0.0),
                     writes=[rst.b])
                omka = tl("omka", [128, 8], F32, s1)
                k.op(k.dve, lambda: nc.vector.tensor_scalar(out=omka[:, :], in0=V("k_a", 0, 8), scalar1=-1.0,
                                                            scalar2=1.0, op0=ALU.mult, op1=ALU.add),
                     reads=[vecs.b], writes=[omka.b])
                ident_bf = cbf[:, 0:128]

                U = tl("r_U", [128, 8, NB + 2], F32, s1)
                XX = tl("r_XX", [128, 8, NB], F32, s1)
                LB = tl("r_LB", [128, 8, NB], BF16, s1)
                Kt = tl("r_Kt", [128, 8, NB], F32, s1)
                Vt = tl("r_Vt", [128, 8, NB], BF16, s1)
                Rt = tl("r_Rt", [128, 8, NB], BF16, s1)
                OB = tl("r_OB", [128, 8, NB], BF16, s1)
                tw = tl("r_tw", [128, NB], BF16, s1)
                ta = tl("r_ta", [128, NB], BF16, s1)
                tg = tl("r_tg", [128, NB], BF16, s1)

                def f32t(nm, n=1):
                    return [tl(f"r_{nm}{i}", [128, NB], F32, s1) for i in range(n)]

                def dbl(lst):
                    return lst if len(lst) == 2 else [lst[0], lst[0]]

                sig = dbl(f32t("sig", 1)); aa = dbl(f32t("aa", 1)); ao = dbl(f32t("ao", 1)); kkr = dbl(f32t("kkr", 1))
                rn = dbl(f32t("rn", 1)); kkn = dbl(f32t("kkn", 1)); kd = f32t("kd", 2); bet = dbl(f32t("bet", 1))
                cum = dbl(f32t("cum", 1)); xe = dbl(f32t("xe", 1)); xi = dbl(f32t("xi", 1)); Ee = dbl(f32t("Ee", 1))
                ai = dbl(f32t("ai", 1)); ri = dbl(f32t("ri", 1)); t3 = dbl(f32t("t3", 1)); yf = f32t("yf", 2)
                gn = dbl(f32t("gn", 1)); x2b = f32t("x2b", 2)
                sqb = [tl(f"r_sqb{i}", [128, NB], BF16, s1) for i in range(2)]
                wtot = [tl(f"r_wtot{i}", [128, 4], F32, s1) for i in range(2)]
                wt0 = [tl(f"r_wt0{i}", [64, 4], F32, s1) for i in range(4)]
                AR = [tl(f"r_AR{i}", [128, 4, 128], BF16, s1) for i in range(2)]
                BK = [tl(f"r_BK{i}", [128, 4, 128], BF16, s1) for i in range(2)]
                TMa = [tl(f"r_TMa{i}", [64, 4, 128], BF16, s1) for i in range(2)]
                Vtm = [tl(f"r_Vtm{i}", [64, 4, 128], BF16, s1) for i in range(2)]
                NSET = 4
                Xs = [[tl(f"r_X{g}_{i}", [64, 4, 128], BF16, s1) for i in range(2)] for g in range(NSET)]
                Ws = [[tl(f"r_W{g}_{i}", [64, 4, 128], BF16, s1) for i in range(2)] for g in range(NSET)]
                AKT = [tl(f"r_AKT{i}", [64, 4, 64], BF16, s1) for i in range(NSET)]
                KA = [tl(f"r_KA{i}", [64, 4, 128], F32, s1) for i in range(NSET)]
                M1 = [tl(f"r_M1{i}", [64, 4, 64], F32, s1) for i in range(NSET)]
                QT = [tl(f"r_QT{i}", [64, 4, 64], F32, s1) for i in range(NSET)]
                ZN = [tl(f"r_ZN{i}", [64, 4, 128], BF16, s1) for i in range(NSET)]
                STP = [[tl(f"r_STP{g}_{i}", [64, 64], F32, s1) for i in range(2)] for g in range(NSET)]
                RC0 = [tl(f"r_RC0{i}", [64, 4, 64], BF16, s1) for i in range(NSET)]
                STh = [tl(f"r_STh{i}", [64, 64], F32, s1) for i in range(16)]
                cnt = {"u": 0}

                def bc4(ap2d):
                    return ap2d.unsqueeze(1).to_broadcast([ap2d.shape[0], 4, ap2d.shape[1]])

                def lerp(j, N):
                    for c in range(8):
                        e = k.dve
                        eng = nc.vector
                        k.op(e, lambda: eng.scalar_tensor_tensor(
                            out=LB[:, c, 0:N], in0=XX[:, c, 0:N], scalar=V("mix", j * 8 + c, 1), in1=U[:, c, 1:N + 1],
                            op0=ALU.mult, op1=ALU.add), reads=[XX.b, U.b, vecs.b], writes=[LB.b])

                def proj_full(Wt, dst, N):
                    for n in range(8):
                        ps = nxt("A")
                        for c in range(8):
                            k.op(k.pe, lambda: nc.tensor.matmul(ps[:, 0:N], lhsT=Wt[:, c, n * 128:(n + 1) * 128],
                                                                rhs=LB[:, c, 0:N], start=(c == 0), stop=(c == 7)),
                                 reads=[Wt.b, LB.b], writes=[ps.b], inc=(c == 7))
                        k.op(k.act, lambda: nc.scalar.copy(out=dst[:, n, 0:N], in_=ps[:, 0:N]), reads=[ps.b],
                             writes=[dst.b])

                def proj_lora(Wt, dst, N, func):
                    ps = nxt("A")
                    for c in range(8):
                        k.op(k.pe, lambda: nc.tensor.matmul(ps[:, 0:N], lhsT=Wt[:, c, :], rhs=LB[:, c, 0:N],
                                                            start=(c == 0), stop=(c == 7)),
                             reads=[Wt.b, LB.b], writes=[ps.b], inc=(c == 7))
                    k.op(k.act, lambda: nc.scalar.activation(out=dst[:, 0:N], in_=ps[:, 0:N], func=func),
                         reads=[ps.b], writes=[dst.b])


                def block(t0, N, s_, d, sweepB):
                    readout = (s_ == 0)
                    src = h1c_d if s_ == 1 else h1l_d
                    r0 = t0 if s_ == 1 else t0 - CL
                    nch = N // 64
                    k.dma(k.sp, U[:, :, 0:N + 2], src[:, :, r0:r0 + N + 2].rearrange("c p t -> p c t"), U.b,
                          writes=[U.b])
                    k.op(k.dve, lambda: nc.vector.tensor_tensor(out=XX[:, :, 0:N], in0=U[:, :, 0:N], in1=U[:, :, 2:N + 2],
                                                                op=ALU.add), reads=[U.b], writes=[XX.b])
                    k.op(k.dve, lambda: nc.vector.scalar_tensor_tensor(out=XX[:, :, 0:N], in0=XX[:, :, 0:N], scalar=0.5,
                                                                       in1=U[:, :, 1:N + 1], op0=ALU.mult,
                                                                       op1=ALU.subtract), reads=[XX.b, U.b],
                         writes=[XX.b])
                    lerp(2, N); proj_full(Wk, Kt, N)
                    lerp(3, N); proj_full(Wv, Vt, N)
                    if readout:
                        lerp(0, N); proj_full(Wr, Rt, N)
                    else:
                        k.op(k.pool, lambda: nc.gpsimd.memset(Rt[:, :, :], 0.0), writes=[Rt.b])
                    lerp(1, N); proj_lora(w1c, tw, N, AF.Tanh)
                    lerp(4, N); proj_lora(a1c, ta, N, AF.Copy)
                    if sweepB and readout:
                        lerp(5, N); proj_lora(g1, tg, N, AF.Sigmoid)
                    if d == 0:
                        m_lt, m_le, m_ltT = msk[:, 0:64], msk[:, 64:128], msk[:, 128:192]
                    else:
                        m_lt, m_le, m_ltT = msk[:, 128:192], msk[:, 192:256], msk[:, 0:64]
                    db = slice(d * 64, (d + 1) * 64)
                    ob_ = slice((1 - d) * 64, (2 - d) * 64)
                    def hp_prep(hp):
                        i2 = hp % 2
                        hs = slice(hp * 128, (hp + 1) * 128)
                        pz = nxt("A")
                        k.op(k.pe, lambda: nc.tensor.matmul(pz[:, 0:N], lhsT=w2c[db, hs], rhs=tw[db, 0:N], start=True,
                                                            stop=True), reads=[w2c.b, tw.b], writes=[pz.b])
                        k.op(k.act, lambda: nc.scalar.activation(out=sig[i2][:, 0:N], in_=pz[:, 0:N], func=AF.Sigmoid,
                                                                 bias=V("w0", d * 8 + hp, 1)), reads=[pz.b, vecs.b],
                             writes=[sig[i2].b])
                        pa = nxt("A")
                        k.op(k.pe, lambda: nc.tensor.matmul(pa[:, 0:N], lhsT=a2c[db, hs], rhs=ta[db, 0:N], start=True,
                                                            stop=True), reads=[a2c.b, ta.b], writes=[pa.b])
                        k.op(k.act, lambda: nc.scalar.activation(out=aa[i2][:, 0:N], in_=pa[:, 0:N], func=AF.Sigmoid,
                                                                 bias=V("a0", d * 8 + hp, 1)), reads=[pa.b, vecs.b],
                             writes=[aa[i2].b])
                        k.op(k.dve, lambda: nc.vector.tensor_scalar(out=kkr[i2][:, 0:N], in0=Kt[:, hp, 0:N],
                                                                    scalar1=V("k_k", hp, 1), scalar2=None,
                                                                    op0=ALU.mult), reads=[Kt.b, vecs.b],
                             writes=[kkr[i2].b])
                        k.op(k.act, lambda: nc.scalar.activation(out=sqb[i2][:, 0:N], in_=kkr[i2][:, 0:N],
                                                                 func=AF.Square), reads=[kkr[i2].b], writes=[sqb[i2].b])
                        pn = nxt("C")
                        k.op(k.pe, lambda: nc.tensor.matmul(pn[:, 0:N], lhsT=bones_bf, rhs=sqb[i2][:, 0:N], start=True,
                                                            stop=True), reads=[sqb[i2].b, cbf.b], writes=[pn.b])
                        k.op(k.dve, lambda: nc.vector.tensor_scalar(out=rn[i2][:, 0:N], in0=pn[:, 0:N], scalar1=1e-24,
                                                                    scalar2=None, op0=ALU.max), reads=[pn.b],
                             writes=[rn[i2].b])
                        k.op(k.act, lambda: nc.scalar.activation(out=rn[i2][:, 0:N], in_=rn[i2][:, 0:N], func=AF.Sqrt),
                             reads=[rn[i2].b], writes=[rn[i2].b])
                        k.op(k.dve, lambda: nc.vector.reciprocal(out=rn[i2][:, 0:N], in_=rn[i2][:, 0:N]),
                             reads=[rn[i2].b], writes=[rn[i2].b])
                        k.op(k.dve, lambda: nc.vector.tensor_tensor(out=kkn[i2][:, 0:N], in0=kkr[i2][:, 0:N],
                                                                    in1=rn[i2][:, 0:N], op=ALU.mult),
                             reads=[kkr[i2].b, rn[i2].b], writes=[kkn[i2].b])
                        k.op(k.dve, lambda: nc.vector.tensor_scalar(out=t3[i2][:, 0:N], in0=aa[i2][:, 0:N],
                                                                     scalar1=V("k_a", hp, 1), scalar2=omka[:, hp:hp + 1],
                                                                     op0=ALU.mult, op1=ALU.add),
                             reads=[aa[i2].b, vecs.b, omka.b], writes=[t3[i2].b])
                        k.op(k.dve, lambda: nc.vector.tensor_tensor(out=kd[i2][:, 0:N], in0=t3[i2][:, 0:N],
                                                                     in1=Kt[:, hp, 0:N], op=ALU.mult),
                             reads=[t3[i2].b, Kt.b], writes=[kd[i2].b])
                        k.op(k.dve, lambda: nc.vector.tensor_tensor(out=bet[i2][:, 0:N], in0=kkn[i2][:, 0:N],
                                                                     in1=aa[i2][:, 0:N], op=ALU.mult),
                             reads=[kkn[i2].b, aa[i2].b], writes=[bet[i2].b])
                        k.op(k.dve, lambda: nc.vector.tensor_tensor_scan(out=cum[i2][:, 0:N], data0=rst[:, 0:N],
                                                                         data1=sig[i2][:, 0:N], initial=0.0,
                                                                         op0=ALU.mult, op1=ALU.add),
                             reads=[rst.b, sig[i2].b], writes=[cum[i2].b])
                        c3 = cum[i2][:, 0:N].rearrange("p (c t) -> p c t", t=64)
                        if d == 0:
                            k.op(k.dve, lambda: nc.vector.tensor_tensor(
                                out=xe[i2][:, 0:N].rearrange("p (c t) -> p c t", t=64),
                                in0=c3[:, :, 63:64].to_broadcast([128, nch, 64]), in1=c3, op=ALU.subtract),
                                reads=[cum[i2].b], writes=[xe[i2].b])
                            k.op(k.dve, lambda: nc.vector.tensor_tensor(out=xi[i2][:, 0:N], in0=xe[i2][:, 0:N],
                                                                        in1=sig[i2][:, 0:N], op=ALU.add),
                                 reads=[xe[i2].b, sig[i2].b], writes=[xi[i2].b])
                            xe_, xi_ = xe[i2], xi[i2]
                        else:
                            k.op(k.dve, lambda: nc.vector.tensor_tensor(out=xe[i2][:, 0:N], in0=cum[i2][:, 0:N],
                                                                        in1=sig[i2][:, 0:N], op=ALU.subtract),
                                 reads=[cum[i2].b, sig[i2].b], writes=[xe[i2].b])
                            xe_, xi_ = xe[i2], cum[i2]
                        k.op(k.act, lambda: nc.scalar.activation(out=Ee[i2][:, 0:N], in_=xe_[:, 0:N], func=AF.Exp,
                                                                 scale=-C0), reads=[xe_.b], writes=[Ee[i2].b])
                        k.op(k.act, lambda: nc.scalar.activation(out=ai[i2][:, 0:N], in_=xi_[:, 0:N], func=AF.Exp,
                                                                 scale=C0), reads=[xi_.b], writes=[ai[i2].b])
                        k.op(k.act, lambda: nc.scalar.activation(out=ri[i2][:, 0:N], in_=xe_[:, 0:N], func=AF.Exp,
                                                                 scale=C0), reads=[xe_.b], writes=[ri[i2].b])
                        k.op(k.act, lambda: nc.scalar.activation(out=wtot[i2][:, 0:nch], in_=c3[:, :, 63], func=AF.Exp,
                                                                 scale=-C0), reads=[cum[i2].b], writes=[wtot[i2].b])
                        ARv, BKv = AR[i2], BK[i2]

                        def v3(t_):
                            return t_[:, 0:N].rearrange("p (c t) -> p c t", t=64)

                        k.op(k.dve, lambda: nc.vector.scalar_tensor_tensor(out=ARv[:, 0:nch, 0:64], in0=v3(kkn[i2]),
                                                                           scalar=-1.0, in1=v3(ai[i2]), op0=ALU.mult,
                                                                           op1=ALU.mult), reads=[kkn[i2].b, ai[i2].b],
                             writes=[ARv.b])
                        k.op(k.dve, lambda: nc.vector.tensor_tensor(
                            out=ARv[:, 0:nch, 64:128], in0=Rt[:, hp, 0:N].rearrange("p (c t) -> p c t", t=64),
                            in1=v3(ri[i2]), op=ALU.mult), reads=[Rt.b, ri[i2].b], writes=[ARv.b])
                        k.op(k.dve, lambda: nc.vector.tensor_tensor(out=BKv[:, 0:nch, 0:64], in0=v3(bet[i2]),
                                                                    in1=v3(Ee[i2]), op=ALU.mult),
                             reads=[bet[i2].b, Ee[i2].b], writes=[BKv.b])
                        k.op(k.dve, lambda: nc.vector.tensor_tensor(out=BKv[:, 0:nch, 64:128], in0=v3(kd[i2]),
                                                                     in1=v3(Ee[i2]), op=ALU.mult),
                             reads=[kd[i2].b, Ee[i2].b], writes=[BKv.b])
                        pta = nxt("C"); ptb = nxt("C"); ptk = nxt("C"); ptv = nxt("A")
                        ptab = pta[0:64, :].bitcast(BF16).rearrange("p (c t) -> p c t", t=256)
                        ptbb = ptb[0:64, :].bitcast(BF16).rearrange("p (c t) -> p c t", t=256)
                        ptkb = ptk[0:64, :].bitcast(BF16).rearrange("p (c t) -> p c t", t=256)
                        ptvb = ptv[0:64, :].bitcast(BF16).rearrange("p (c t) -> p c t", t=256)
                        for ci in range(nch):
                            last = (ci == nch - 1)
                            k.op(k.pe, lambda: nc.tensor.transpose(ptab[:, ci, 0:128], ARv[:, ci, 0:64], ident_bf),
                                 reads=[ARv.b, cbf.b], writes=[pta.b], inc=last)
                            k.op(k.pe, lambda: nc.tensor.transpose(ptbb[:, ci, 0:128], BKv[:, ci, 0:64], ident_bf),
                                 reads=[BKv.b, cbf.b], writes=[ptb.b], inc=last)
                            k.op(k.pe, lambda: nc.tensor.transpose(ptkb[:, ci, 0:128], BKv[:, ci, 64:128], ident_bf),
                                 reads=[BKv.b, cbf.b], writes=[ptk.b], inc=last)
                            k.op(k.pe, lambda: nc.tensor.transpose(ptvb[:, ci, 0:128], Vt[:, hp, ci * 64:(ci + 1) * 64],
                                                                   ident_bf),
                                 reads=[Vt.b, cbf.b], writes=[ptv.b], inc=last)
                        k.op(k.act, lambda: nc.scalar.copy(out=TMa[i2][:, 0:nch, :], in_=ptab[:, 0:nch, 0:128]),
                             reads=[pta.b], writes=[TMa[i2].b])
                        for h_ in range(2):
                            us_ = (i2 * 2 + h_) % NSET
                            hb_ = slice(h_ * 64, (h_ + 1) * 64)
                            k.op(k.act, lambda: nc.scalar.copy(out=Ws[us_][0][:, 0:nch, 0:64], in_=ptbb[:, 0:nch, hb_]),
                                 reads=[ptb.b], writes=[Ws[us_][0].b])
                            k.op(k.dve, lambda: nc.vector.tensor_copy(out=KA[us_][:, 0:nch, 0:64],
                                                                      in_=ptkb[:, 0:nch, hb_]),
                                 reads=[ptk.b], writes=[KA[us_].b])
                        k.op(k.act, lambda: nc.scalar.copy(out=Vtm[i2][:, 0:nch, :], in_=ptvb[:, 0:nch, 0:128]),
                             reads=[ptv.b], writes=[Vtm[i2].b])

                    def head_gen(hp, h, py):
                        i2 = hp % 2
                        hs = slice(hp * 128, (hp + 1) * 128)
                        ARv, BKv = AR[i2], BK[i2]
                        nb_ = [64, nch, 64]
                        hb = slice(h * 64, (h + 1) * 64)
                        u2 = (i2 * 2 + h) % NSET
                        X, W0, W1 = Xs[u2][0], Ws[u2][0], Ws[u2][1]
                        p2 = nxt("A"); p1 = nxt("A")
                        p2v = p2[:, :].rearrange("p (c t) -> p c t", t=128)
                        p1v = p1[0:64, :].rearrange("p (c t) -> p c t", t=128)
                        for ci in range(nch):
                            k.op(k.pe, lambda: nc.tensor.matmul(p2v[:, ci, :], lhsT=BKv[hb, ci, :], rhs=ARv[hb, ci, :],
                                                                start=True, stop=True), reads=[BKv.b, ARv.b],
                                 writes=[p2.b], inc=(ci == nch - 1))
                        for ci in range(nch):
                            k.op(k.pe, lambda: nc.tensor.matmul(p1v[:, ci, :], lhsT=ARv[hb, ci, 0:64],
                                                                rhs=BKv[hb, ci, :], start=True, stop=True),
                                 reads=[BKv.b, ARv.b], writes=[p1.b], inc=(ci == nch - 1))
                        nb_ = [64, nch, 64]
                        k.op(k.dve, lambda: nc.vector.tensor_tensor(
                            out=X[:, 0:nch, 64:128], in0=p2v[0:64, 0:nch, 0:64],
                            in1=m_lt[0:64, :].unsqueeze(1).to_broadcast(nb_), op=ALU.mult),
                            reads=[p2.b, msk.b], writes=[X.b])
                        k.op(k.dve, lambda: nc.vector.tensor_tensor(
                            out=W0[:, 0:nch, 64:128], in0=p2v[0:64, 0:nch, 64:128],
                            in1=m_le[0:64, :].unsqueeze(1).to_broadcast(nb_), op=ALU.mult),
                            reads=[p2.b, msk.b], writes=[W0.b])
                        k.op(k.dve, lambda: nc.vector.tensor_tensor(
                            out=KA[u2][:, 0:nch, 64:128], in0=p2v[64:128, 0:nch, 64:128],
                            in1=m_le[64:128, :].unsqueeze(1).to_broadcast(nb_), op=ALU.mult),
                            reads=[p2.b, msk.b], writes=[KA[u2].b])
                        k.op(k.dve, lambda: nc.vector.tensor_tensor(
                            out=X[:, 0:nch, 0:64], in0=p1v[:, 0:nch, 0:64],
                            in1=m_ltT[0:64, :].unsqueeze(1).to_broadcast(nb_), op=ALU.mult),
                            reads=[p1.b, msk.b], writes=[X.b])
                        k.op(k.dve, lambda: nc.vector.tensor_tensor(
                            out=AKT[u2][:, 0:nch, :], in0=p1v[:, 0:nch, 64:128],
                            in1=m_ltT[0:64, :].unsqueeze(1).to_broadcast(nb_), op=ALU.mult),
                            reads=[p1.b, msk.b], writes=[AKT[u2].b])
                        yield
                        Wc, Wn = W0, W1
                        Xc, Xn = X, Xs[u2][1]
                        for j in range(6):
                            pw = nxt("A")
                            pwv = pw[0:64, :].rearrange("p (c t) -> p c t", t=128)
                            for ci in range(nch):
                                k.op(k.pe, lambda: nc.tensor.matmul(pwv[:, ci, :], lhsT=ident_bf[0:64, 0:64],
                                                                    rhs=Wc[:, ci, :], start=True, stop=False),
                                     reads=[Wc.b, cbf.b], writes=[pw.b], inc=False)
                                k.op(k.pe, lambda: nc.tensor.matmul(pwv[:, ci, :], lhsT=Xc[:, ci, 0:64],
                                                                    rhs=Wc[:, ci, :], start=False, stop=True),
                                     reads=[Wc.b, Xc.b], writes=[pw.b], inc=(ci == nch - 1))
                            k.op(k.act, lambda: nc.scalar.copy(out=Wn[:, 0:nch, :], in_=pwv[:, 0:nch, :]),
                                 reads=[pw.b], writes=[Wn.b])
                            Wc, Wn = Wn, Wc
                            if j < 5:
                                pq_ = nxt("C")
                                pqv = pq_[0:64, :].rearrange("p (c t) -> p c t", t=128)
                                for ci in range(nch):
                                    k.op(k.pe, lambda: nc.tensor.matmul(pqv[:, ci, 0:64], lhsT=Xc[:, ci, 64:128],
                                                                        rhs=Xc[:, ci, 0:64], start=True, stop=True),
                                         reads=[Xc.b], writes=[pq_.b], inc=False)
                                    k.op(k.pe, lambda: nc.tensor.matmul(pqv[:, ci, 64:128], lhsT=Xc[:, ci, 0:64],
                                                                        rhs=Xc[:, ci, 64:128], start=True, stop=True),
                                         reads=[Xc.b], writes=[pq_.b], inc=(ci == nch - 1))
                                k.op(k.dve, lambda: nc.vector.tensor_copy(out=Xn[:, 0:nch, :], in_=pqv[:, 0:nch, :]),
                                     reads=[pq_.b], writes=[Xn.b])
                                Xc, Xn = Xn, Xc
                            yield
                        pfa = nxt("A"); pfb = nxt("A")
                        pfav = pfa[0:64, :].rearrange("p (c t) -> p c t", t=128)
                        pfbv = pfb[0:64, :].rearrange("p (c t) -> p c t", t=128)
                        for ci in range(nch):
                            k.op(k.pe, lambda: nc.tensor.matmul(pfav[:, ci, :], lhsT=TMa[i2][:, ci, hb],
                                                                rhs=Wc[:, ci, :], start=True, stop=True),
                                 reads=[TMa[i2].b, Wc.b], writes=[pfa.b], inc=(ci == nch - 1))
                        for ci in range(nch):
                            k.op(k.pe, lambda: nc.tensor.matmul(pfbv[:, ci, :], lhsT=AKT[u2][:, ci, :],
                                                                rhs=Wc[:, ci, :], start=True, stop=True),
                                 reads=[AKT[u2].b, Wc.b], writes=[pfb.b], inc=(ci == nch - 1))
                        k.op(k.dve, lambda: nc.vector.tensor_tensor(
                            out=M1[u2][:, 0:nch, :], in0=pfav[:, 0:nch, 0:64],
                            in1=ident[0:64, 0:64].unsqueeze(1).to_broadcast(nb_), op=ALU.add),
                            reads=[pfa.b, cst.b], writes=[M1[u2].b])
                        if h == 0:
                            rcs, rcb = ARv[0:64, 0:nch, 64:128], ARv.b
                        else:
                            k.op(k.dve, lambda: nc.vector.tensor_copy(out=RC0[u2][:, 0:nch, :],
                                                                      in_=ARv[64:128, 0:nch, 64:128]),
                                 reads=[ARv.b], writes=[RC0[u2].b])
                            rcs, rcb = RC0[u2][:, 0:nch, :], RC0[u2].b
                        k.op(k.dve, lambda: nc.vector.tensor_tensor(out=QT[u2][:, 0:nch, :], in0=pfav[:, 0:nch, 64:128],
                                                                    in1=rcs, op=ALU.add),
                             reads=[pfa.b, rcb], writes=[QT[u2].b])
                        k.op(k.dve, lambda: nc.vector.tensor_tensor(out=ZN[u2][:, 0:nch, :], in0=pfbv[:, 0:nch, :],
                                                                    in1=KA[u2][:, 0:nch, :], op=ALU.add),
                             reads=[pfb.b, KA[u2].b], writes=[ZN[u2].b])
                        wti = wt0[u2]
                        k.op(k.dve, lambda: nc.vector.tensor_copy(out=wti[:, 0:nch], in_=wtot[i2][hb, 0:nch]),
                             reads=[wtot[i2].b], writes=[wti.b])
                        yield
                        hh = hp * 2 + h
                        order = range(nch) if d == 0 else range(nch - 1, -1, -1)
                        for oi, ci in enumerate(order):
                            stp = STP[u2][oi % 2]
                            k.op(k.dve, lambda: nc.vector.tensor_scalar(out=stp[:, :], in0=STh[hh][:, :],
                                                                        scalar1=wti[:, ci:ci + 1], scalar2=None,
                                                                        op0=ALU.mult), reads=[STh[hh].b, wti.b],
                                 writes=[stp.b])
                            pst = nxt("C")
                            k.op(k.pe, lambda: nc.tensor.matmul(pst[0:64, 0:64], lhsT=M1[u2][:, ci, :], rhs=stp[:, :],
                                                                start=True, stop=False), reads=[M1[u2].b, stp.b],
                                 writes=[pst.b], inc=False)
                            k.op(k.pe, lambda: nc.tensor.matmul(pst[0:64, 0:64], lhsT=ZN[u2][:, ci, 0:64],
                                                                rhs=Vtm[i2][:, ci, hb], start=False, stop=True),
                                 reads=[ZN[u2].b, Vtm[i2].b], writes=[pst.b])
                            if readout:
                                k.op(k.pe, lambda: nc.tensor.matmul(py[hb, ci * 64:(ci + 1) * 64], lhsT=stp[:, :],
                                                                    rhs=QT[u2][:, ci, :], start=True, stop=False),
                                     reads=[stp.b, QT[u2].b], writes=[py.b], inc=False)
                                k.op(k.pe, lambda: nc.tensor.matmul(py[hb, ci * 64:(ci + 1) * 64],
                                                                    lhsT=Vtm[i2][:, ci, hb],
                                                                    rhs=ZN[u2][:, ci, 64:128], start=False, stop=True),
                                     reads=[Vtm[i2].b, ZN[u2].b], writes=[py.b])
                            k.op(k.act, lambda: nc.scalar.copy(out=STh[hh][:, :], in_=pst[0:64, 0:64]), reads=[pst.b],
                                 writes=[STh[hh].b])
                            yield

                    def hp_readout(hp, py):
                        i2 = hp % 2
                        hs = slice(hp * 128, (hp + 1) * 128)
                        if not readout:
                            return
                        tr0 = t0 - CL
                        if not sweepB:
                            k.op(k.act, lambda: nc.scalar.copy(out=yf[i2][:, 0:N], in_=py[:, 0:N]), reads=[py.b],
                                 writes=[yf[i2].b])
                            k.dma(k.pool, yf_d[hp, :, tr0:tr0 + N], yf[i2][:, 0:N], yf[i2].b, reads=[yf[i2].b])
                            return
                        k.dma(k.sp, yf[i2][:, 0:N], yf_d[hp, :, tr0:tr0 + N], yf[i2].b, writes=[yf[i2].b])
                        wk_ = gn[i2]
                        k.op(k.dve, lambda: nc.vector.tensor_tensor(out=wk_[:, 0:N], in0=py[:, 0:N], in1=yf[i2][:, 0:N],
                                                                    op=ALU.add), reads=[py.b, yf[i2].b], writes=[wk_.b])
                        k.op(k.act, lambda: nc.scalar.copy(out=sqb[i2][:, 0:N], in_=wk_[:, 0:N]), reads=[wk_.b],
                             writes=[sqb[i2].b])
                        pm_ = nxt("C")
                        k.op(k.pe, lambda: nc.tensor.matmul(pm_[:, 0:N], lhsT=bones_bf, rhs=sqb[i2][:, 0:N], start=True,
                                                            stop=True), reads=[sqb[i2].b, cbf.b], writes=[pm_.b])
                        k.op(k.dve, lambda: nc.vector.scalar_tensor_tensor(out=wk_[:, 0:N], in0=pm_[:, 0:N],
                                                                           scalar=-1.0 / 64, in1=wk_[:, 0:N],
                                                                           op0=ALU.mult, op1=ALU.add),
                             reads=[pm_.b, wk_.b], writes=[wk_.b])
                        k.op(k.act, lambda: nc.scalar.activation(out=sqb[i2][:, 0:N], in_=wk_[:, 0:N], func=AF.Square),
                             reads=[wk_.b], writes=[sqb[i2].b])
                        pv_ = nxt("C")
                        k.op(k.pe, lambda: nc.tensor.matmul(pv_[:, 0:N], lhsT=bones_bf, rhs=sqb[i2][:, 0:N], start=True,
                                                            stop=True), reads=[sqb[i2].b, cbf.b], writes=[pv_.b])
                        k.op(k.act, lambda: nc.scalar.activation(out=rn[i2][:, 0:N], in_=pv_[:, 0:N], func=AF.Sqrt,
                                                                 scale=1.0 / 64, bias=GN_EPS), reads=[pv_.b],
                             writes=[rn[i2].b])
                        k.op(k.dve, lambda: nc.vector.reciprocal(out=rn[i2][:, 0:N], in_=rn[i2][:, 0:N]),
                             reads=[rn[i2].b], writes=[rn[i2].b])
                        k.op(k.dve, lambda: nc.vector.scalar_tensor_tensor(out=wk_[:, 0:N], in0=wk_[:, 0:N],
                                                                           scalar=V("ln_g", hp, 1), in1=rn[i2][:, 0:N],
                                                                           op0=ALU.mult, op1=ALU.mult),
                             reads=[wk_.b, rn[i2].b, vecs.b], writes=[wk_.b])
                        pa2 = nxt("A")
                        k.op(k.pe, lambda: nc.tensor.matmul(pa2[:, 0:N], lhsT=a2c[ob_, hs], rhs=ta[ob_, 0:N], start=True,
                                                            stop=True), reads=[a2c.b, ta.b], writes=[pa2.b])
                        k.op(k.act, lambda: nc.scalar.activation(out=ao[i2][:, 0:N], in_=pa2[:, 0:N], func=AF.Sigmoid,
                                                                 bias=V("a0", (1 - d) * 8 + hp, 1)),
                             reads=[pa2.b, vecs.b], writes=[ao[i2].b])
                        k.op(k.dve, lambda: nc.vector.tensor_scalar(out=t3[i2][:, 0:N], in0=ao[i2][:, 0:N],
                                                                     scalar1=V("k_a", hp, 1), scalar2=omka[:, hp:hp + 1],
                                                                     op0=ALU.mult, op1=ALU.add),
                             reads=[ao[i2].b, vecs.b, omka.b], writes=[t3[i2].b])
                        k.op(k.dve, lambda: nc.vector.tensor_tensor(out=t3[i2][:, 0:N], in0=t3[i2][:, 0:N],
                                                                     in1=Kt[:, hp, 0:N], op=ALU.mult),
                             reads=[t3[i2].b, Kt.b], writes=[t3[i2].b])
                        k.op(k.dve, lambda: nc.vector.tensor_tensor(out=t3[i2][:, 0:N], in0=t3[i2][:, 0:N],
                                                                     in1=kd[i2][:, 0:N], op=ALU.add),
                             reads=[t3[i2].b, kd[i2].b], writes=[t3[i2].b])
                        k.op(k.dve, lambda: nc.vector.scalar_tensor_tensor(out=sqb[i2][:, 0:N], in0=t3[i2][:, 0:N],
                                                                           scalar=V("r_k", hp, 1), in1=Rt[:, hp, 0:N],
                                                                           op0=ALU.mult, op1=ALU.mult),
                             reads=[t3[i2].b, Rt.b, vecs.b], writes=[sqb[i2].b])
                        pb_ = nxt("C")
                        k.op(k.pe, lambda: nc.tensor.matmul(pb_[:, 0:N], lhsT=bones_bf, rhs=sqb[i2][:, 0:N], start=True,
                                                            stop=True), reads=[sqb[i2].b, cbf.b], writes=[pb_.b])
                        k.op(k.dve, lambda: nc.vector.tensor_tensor(out=t3[i2][:, 0:N], in0=pb_[:, 0:N],
                                                                    in1=Vt[:, hp, 0:N], op=ALU.mult),
                             reads=[pb_.b, Vt.b], writes=[t3[i2].b])
                        k.op(k.dve, lambda: nc.vector.scalar_tensor_tensor(out=wk_[:, 0:N], in0=wk_[:, 0:N],
                                                                           scalar=V("ln_b", hp, 1), in1=t3[i2][:, 0:N],
                                                                           op0=ALU.add, op1=ALU.add),
                             reads=[wk_.b, t3[i2].b, vecs.b], writes=[wk_.b])
                        pg_ = nxt("A")
                        k.op(k.pe, lambda: nc.tensor.matmul(pg_[:, 0:N], lhsT=g2[:, hs], rhs=tg[:, 0:N], start=True,
                                                            stop=True), reads=[g2.b, tg.b], writes=[pg_.b])
                        k.op(k.dve, lambda: nc.vector.tensor_tensor(out=OB[:, hp, 0:N], in0=wk_[:, 0:N], in1=pg_[:, 0:N],
                                                                    op=ALU.mult), reads=[wk_.b, pg_.b], writes=[OB.b])

                    def drive(gens):
                        while gens:
                            for g_ in list(gens):
                                try:
                                    next(g_)
                                except StopIteration:
                                    gens.remove(g_)

                    def step(gens):
                        for g_ in list(gens):
                            try:
                                next(g_)
                            except StopIteration:
                                gens.remove(g_)

                    active = []
                    nxt_hp = 0
                    while nxt_hp < 8 or active:
                        while nxt_hp < 8 and len(active) < 2:
                            hp_prep(nxt_hp)
                            py_ = psB[nxt_hp % 2] if readout else None
                            active.append([nxt_hp, [head_gen(nxt_hp, 0, py_), head_gen(nxt_hp, 1, py_)], py_])
                            nxt_hp += 1
                        for a_ in active:
                            step(a_[1])
                        while active and not active[0][1]:
                            hp_readout(active[0][0], active[0][2])
                            active.pop(0)
                    if sweepB and readout:
                        for n in range(8):
                            i2 = n % 2
                            k.dma(k.sp, x2b[i2][:, 0:N], x2T_d[n, :, t0:t0 + N], x2b[i2].b, writes=[x2b[i2].b])
                            po_ = nxt("A")
                            for c in range(8):
                                k.op(k.pe, lambda: nc.tensor.matmul(po_[:, 0:N], lhsT=wo1[:, c, n * 128:(n + 1) * 128],
                                                                    rhs=OB[:, c, 0:N], start=(c == 0), stop=(c == 7)),
                                     reads=[wo1.b, OB.b], writes=[po_.b], inc=(c == 7))
                            k.op(k.dve, lambda: nc.vector.scalar_tensor_tensor(
                                out=x2b[i2][:, 0:N], in0=po_[:, 0:N], scalar=modv[:, 1, 0, 2, n:n + 1],
                                in1=x2b[i2][:, 0:N], op0=ALU.mult, op1=ALU.add), reads=[po_.b, modv.b, x2b[i2].b],
                                writes=[x2b[i2].b])
                            k.dma(k.pool, x3T_d[n, :, t0:t0 + N], x2b[i2][:, 0:N], x2b[i2].b, reads=[x2b[i2].b])

                lat = [(CL + i * NB, NB, 0) for i in range(L // NB)]
                for st_ in STh:
                    k.op(k.pool, lambda: nc.gpsimd.memset(st_[:, :], 0.0), writes=[st_.b])
                block(0, CL, 1, 0, False)
                for (t0, N, s_) in lat:
                    block(t0, N, s_, 0, False)
                k.barrier()
                for st_ in STh:
                    k.op(k.pool, lambda: nc.gpsimd.memset(st_[:, :], 0.0), writes=[st_.b])
                block(0, CL, 1, 1, True)
                for (t0, N, s_) in reversed(lat):
                    block(t0, N, s_, 1, True)

        blocks0 = [(0, CL, 1)] + [(CL + i * 512, 512, 0) for i in range(L // 512)]

        if stage >= 1:
            with contextlib.ExitStack() as s1:
                wq = tl("wqkv", [128, 8, 1536], BF16, s1)
                with contextlib.ExitStack() as s2:
                    stg = [tl(f"p1_stg{i}", [128, 1536], F32, s2) for i in range(2)]
                    load_weight_bf16(wq, lambda i: wq[:, i, :], lambda i: wqkv_d[i * 128:(i + 1) * 128, :], 8, 1536, stg,
                                     [k.dve, k.act])
                k.barrier()
                xs = [tl(f"xs{i}", [128, D], F32, s1) for i in range(2)]
                xT = tl("p1_xT", [128, 8, 512], F32, s1)
                hT = tl("p1_hT", [128, 8, 512], BF16, s1)
                sq = tl("p1_sq", [128, 8, 512], BF16, s1)
                rstd = tl("p1_rstd", [128, 512], F32, s1)
                tmp = [tl(f"p1_tmp{i}", [128, 512], F32, s1) for i in range(2)]
                qk_out = tl("p1_qk", [128, 10, 512], BF16, s1)
                v_out = tl("p1_v", [128, 4, 256], BF16, s1)
                ct = tl("p1_ct", [128, 512], F32, s1)
                stb = tl("p1_st", [128, 512], F32, s1)
                hsq = [tl(f"p1_hsq{i}", [128, 512], BF16, s1) for i in range(2)]
                hr = [tl(f"p1_hr{i}", [128, 512], F32, s1) for i in range(2)]
                qg = [tl(f"p1_qg{i}", [128, 512], F32, s1) for i in range(2)]
                qgb = [tl(f"p1_qgb{i}", [128, 512], BF16, s1) for i in range(2)]
                t1 = [tl(f"p1_t1{i}", [128, 512], F32, s1) for i in range(2)]
                t2 = [tl(f"p1_t2{i}", [128, 512], F32, s1) for i in range(2)]
                for bi, (t0, N, s) in enumerate(blocks0):
                    src = ctx_d if s == 1 else x_d
                    r0 = t0 if s == 1 else t0 - CL
                    for tt in range(N // 128):
                        xi = xs[tt % 2]
                        k.dma(k.sp, xi[:, :], src[r0 + tt * 128:r0 + (tt + 1) * 128, :], xi.b, writes=[xi.b])
                        for half in range(2):
                            pt = nxt("C")
                            for cc in range(4):
                                c = half * 4 + cc
                                k.op(k.pe, lambda: nc.tensor.transpose(pt[:, cc * 128:(cc + 1) * 128],
                                                                       xi[:, c * 128:(c + 1) * 128], ident),
                                     reads=[xi.b, cst.b], writes=[pt.b], inc=(cc == 3))
                            k.op(k.act, lambda: nc.scalar.copy(
                                out=xT[:, half * 4:(half + 1) * 4, tt * 128:(tt + 1) * 128],
                                in_=pt[:, :].rearrange("p (c t) -> p c t", t=128)), reads=[pt.b], writes=[xT.b])
                    k.dma(k.pool, xT_d[:, :, t0:t0 + N].rearrange("c p t -> p c t"), xT[:, :, 0:N], xT.b, reads=[xT.b])
                    if s == 0:
                        k.dma(k.sp, ct[:, :], ctab_d[:, r0:r0 + 512], ct.b, writes=[ct.b])
                        k.dma(k.sp, stb[:, :], stab_d[:, r0:r0 + 512], stb.b, writes=[stb.b])
                    norm_mod(xT, N, 0, s, 0, hT, sq, rstd, tmp)
                    for n in range(10):
                        pq = nxt("A")
                        for c in range(8):
                            k.op(k.pe, lambda: nc.tensor.matmul(pq[:, 0:N], lhsT=wq[:, c, n * 128:(n + 1) * 128],
                                                                rhs=hT[:, c, 0:N], start=(c == 0), stop=(c == 7)),
                                 reads=[wq.b, hT.b], writes=[pq.b], inc=(c == 7))
                        i2 = n % 2
                        k.op(k.act, lambda: nc.scalar.activation(out=hsq[i2][:, 0:N], in_=pq[:, 0:N], func=AF.Square),
                             reads=[pq.b], writes=[hsq[i2].b])
                        ph = nxt("C")
                        k.op(k.pe, lambda: nc.tensor.matmul(ph[:, 0:N], lhsT=bones_bf, rhs=hsq[i2][:, 0:N], start=True,
                                                            stop=True), reads=[hsq[i2].b, cbf.b], writes=[ph.b])
                        k.op(k.act, lambda: nc.scalar.activation(out=hr[i2][:, 0:N], in_=ph[:, 0:N], func=AF.Sqrt,
                                                                 scale=1.0 / 64, bias=EPS), reads=[ph.b],
                             writes=[hr[i2].b])
                        k.op(k.dve, lambda: nc.vector.reciprocal(out=hr[i2][:, 0:N], in_=hr[i2][:, 0:N]),
                             reads=[hr[i2].b], writes=[hr[i2].b])
                        gcol = qg8[:, 0:1] if n < 8 else qg8[:, 1:2]
                        k.op(k.dve, lambda: nc.vector.scalar_tensor_tensor(
                            out=qg[i2][:, 0:N], in0=pq[:, 0:N], scalar=gcol, in1=hr[i2][:, 0:N], op0=ALU.mult,
                            op1=ALU.mult), reads=[pq.b, qg8.b, hr[i2].b], writes=[qg[i2].b])
                        if s == 1:
                            k.op(k.act, lambda: nc.scalar.copy(out=qk_out[:, n, 0:N], in_=qg[i2][:, 0:N]),
                                 reads=[qg[i2].b], writes=[qk_out.b])
                        else:
                            k.op(k.act, lambda: nc.scalar.copy(out=qgb[i2][:, 0:N], in_=qg[i2][:, 0:N]),
                                 reads=[qg[i2].b], writes=[qgb[i2].b])
                            pp = nxt("C")
                            k.op(k.pe, lambda: nc.tensor.matmul(pp[:, 0:N], lhsT=perm_bf, rhs=qgb[i2][:, 0:N],
                                                                start=True, stop=True), reads=[qgb[i2].b, cbf.b],
                                 writes=[pp.b])
                            k.op(k.dve, lambda: nc.vector.tensor_tensor(out=t1[i2][:, 0:N], in0=qg[i2][:, 0:N],
                                                                         in1=ct[:, 0:N], op=ALU.mult),
                                 reads=[qg[i2].b, ct.b], writes=[t1[i2].b])
                            k.op(k.dve, lambda: nc.vector.tensor_tensor(out=t2[i2][:, 0:N], in0=pp[:, 0:N],
                                                                        in1=stb[:, 0:N], op=ALU.mult),
                                 reads=[pp.b, stb.b], writes=[t2[i2].b])
                            k.op(k.dve, lambda: nc.vector.tensor_tensor(out=qk_out[:, n, 0:N], in0=t1[i2][:, 0:N],
                                                                        in1=t2[i2][:, 0:N], op=ALU.add),
                                 reads=[t1[i2].b, t2[i2].b], writes=[qk_out.b])
                    for tt in range(N // 128):
                        pv = nxt("A")
                        for c in range(8):
                            k.op(k.pe, lambda: nc.tensor.matmul(pv[:, 0:256], lhsT=hT[:, c, tt * 128:(tt + 1) * 128],
                                                                rhs=wq[:, c, 1280:1536], start=(c == 0), stop=(c == 7)),
                                 reads=[wq.b, hT.b], writes=[pv.b], inc=(c == 7))
                        k.op(k.act, lambda: nc.scalar.copy(out=v_out[:, tt, :], in_=pv[:, 0:256]), reads=[pv.b],
                             writes=[v_out.b])
                    k.dma(k.pool, qT_d[:, :, t0:t0 + N].rearrange("c p t -> p c t"), qk_out[:, 0:8, 0:N], qk_out.b,
                          reads=[qk_out.b])
                    k.dma(k.pool, kT_d[:, :, t0:t0 + N].rearrange("c p t -> p c t"), qk_out[:, 8:10, 0:N], qk_out.b,
                          reads=[qk_out.b])
                    k.dma(k.pool, v_d[t0:t0 + N, :].rearrange("(j p) d -> p j d", p=128), v_out[:, 0:N // 128, :],
                          v_out.b, reads=[v_out.b])

        def phase_barrier():
            k.barrier()

        phase_barrier()

        if stage >= 2:
            with contextlib.ExitStack() as s1:
                KT = tl("KT", [128, 2, TT], BF16, s1)
                VA = tl("VA", [128, NKT, 4, 128], BF16, s1)
                wo = tl("wo0", [128, 8, D], BF16, s1)
                with contextlib.ExitStack() as s2:
                    stg = [tl(f"p2_stg{i}", [128, D], F32, s2) for i in range(2)]
                    load_weight_bf16(wo, lambda i: wo[:, i, :], lambda i: wo0_d[i * 128:(i + 1) * 128, :], 8, D, stg,
                                     [k.dve, k.act])
                k.barrier()
                QZ = [[tl(f"p2_qz{sl}_{i}", [128, 8, 512], BF16, s1) for i in range(2)] for sl in range(2)]
                for sl in range(2):
                    for i in range(2):
                        k.op(k.pool, lambda: nc.gpsimd.memset(QZ[sl][i][:, :, :], 0.0), writes=[QZ[sl][i].b])
                PT = [tl(f"p2_PT{i}", [128, 512], BF16, s1) for i in range(3)]
                oT = tl("p2_oT", [128, 8, 512], BF16, s1)
                xT = tl("p2_xT", [128, 8, 512], F32, s1)
                rec = [tl(f"p2_rec{i}", [128, 512], F32, s1) for i in range(2)]
                k.dma(k.sp, KT[:, :, :], kT_d[:, :, :].rearrange("c p t -> p c t"), KT.b, writes=[KT.b])
                k.op(k.pool, lambda: nc.gpsimd.memset(VA[:, :, :, :], 1.0), writes=[VA.b])
                for g in range(4):
                    off = 0 if g % 2 == 0 else 64
                    for j0 in range(0, NKT, 8):
                        j1 = min(NKT, j0 + 8)
                        k.dma(k.sp, VA[:, j0:j1, g, off:off + 64],
                              v_d[j0 * 128:j1 * 128, g * 64:(g + 1) * 64].rearrange("(j p) d -> p j d", p=128), VA.b,
                              writes=[VA.b])
                for bi, (t0, N, s) in enumerate(blocks0):
                    qz = [QZ[0][bi % 2], QZ[1][bi % 2]]
                    for sl in range(2):
                        k.dma(k.sp, qz[sl][sl * 64:(sl + 1) * 64, :, 0:N],
                              qT_d[:, sl * 64:(sl + 1) * 64, t0:t0 + N].rearrange("c p t -> p c t"), qz[sl].b,
                              writes=[qz[sl].b])
                    k.dma(k.sp, xT[:, :, 0:N], xT_d[:, :, t0:t0 + N].rearrange("c p t -> p c t"), xT.b, writes=[xT.b])
                    nkt = 2 if s == 1 else NKT
                    hidx = 0
                    for c in range(8):
                        for sl in range(2):
                            g = (c // 4) * 2 + sl
                            po = psB[hidx % 2]
                            lo, hi = sl * 64, (sl + 1) * 64
                            pss = [None] * nkt
                            for j in range(nkt + 2):
                                if j < nkt:
                                    ps_ = nxt("A")
                                    pss[j] = ps_
                                    k.op(k.pe, lambda: nc.tensor.matmul(ps_[:, 0:N],
                                                                        lhsT=KT[:, g // 2, j * 128:(j + 1) * 128],
                                                                        rhs=qz[sl][:, c, 0:N], start=True, stop=True),
                                         reads=[KT.b, qz[sl].b], writes=[ps_.b])
                                    pt_ = PT[j % 3]
                                    k.op(k.act, lambda: nc.scalar.activation(out=pt_[:, 0:N], in_=ps_[:, 0:N],
                                                                             func=AF.Exp), reads=[ps_.b],
                                         writes=[pt_.b])
                                if j >= 2:
                                    jj = j - 2
                                    pt2 = PT[jj % 3]
                                    k.op(k.pe, lambda: nc.tensor.matmul(po[:, 0:N], lhsT=VA[:, jj, g, :],
                                                                        rhs=pt2[:, 0:N], start=(jj == 0),
                                                                        stop=(jj == nkt - 1)),
                                         reads=[VA.b, pt2.b], writes=[po.b], inc=(jj == nkt - 1))
                            rc = rec[hidx % 2]
                            olo, ohi = (64, 128) if sl == 0 else (0, 64)
                            k.op(k.dve, lambda: nc.vector.reciprocal(out=rc[lo:hi, 0:N], in_=po[olo:ohi, 0:N]),
                                 reads=[po.b], writes=[rc.b])
                            k.op(k.dve, lambda: nc.vector.tensor_tensor(out=oT[lo:hi, c, 0:N], in0=po[lo:hi, 0:N],
                                                                        in1=rc[lo:hi, 0:N], op=ALU.mult),
                                 reads=[po.b, rc.b], writes=[oT.b])
                            hidx += 1
                    for n in range(8):
                        py = nxt("C")
                        for c in range(8):
                            k.op(k.pe, lambda: nc.tensor.matmul(py[:, 0:N], lhsT=wo[:, c, n * 128:(n + 1) * 128],
                                                                rhs=oT[:, c, 0:N], start=(c == 0), stop=(c == 7)),
                                 reads=[wo.b, oT.b], writes=[py.b], inc=(c == 7))
                        k.op(k.dve, lambda: nc.vector.scalar_tensor_tensor(
                            out=xT[:, n, 0:N], in0=py[:, 0:N], scalar=modv[:, 0, s, 2, n:n + 1], in1=xT[:, n, 0:N],
                            op0=ALU.mult, op1=ALU.add), reads=[py.b, modv.b, xT.b], writes=[xT.b])
                    k.dma(k.pool, x1T_d[:, :, t0:t0 + N].rearrange("c p t -> p c t"), xT[:, :, 0:N], xT.b, reads=[xT.b])
            phase_barrier()

        if stage >= 3:
            zt = tl("zeros", [128, 8, 1], F32)
            k.op(k.pool, lambda: nc.gpsimd.memset(zt[:, :, :], 0.0), writes=[zt.b])
            for (dd, n_) in ((h1c_d, CL), (h1l_d, L)):
                for col in (0, n_ + 1):
                    k.dma(k.pool, dd[:, :, col:col + 1].rearrange("c p t -> p c t"), zt[:, :, :], zt.b, reads=[zt.b],
                          allow_slow_non_contiguous=True)

            def post_h1(t0, N, s_, xT, hT, sq, rstd, tmp, sg):
                rms_stats(xT, N, sq, rstd)
                dst = h1c_d if s_ == 1 else h1l_d
                r0 = (t0 if s_ == 1 else t0 - CL) + 1
                for c in range(8):
                    tb = tmp[c % 2]
                    ob = sg[c % 2]
                    k.op(k.dve, lambda: nc.vector.scalar_tensor_tensor(
                        out=tb[:, 0:N], in0=xT[:, c, 0:N], scalar=modv[:, 1, s_, 1, c:c + 1], in1=rstd[:, 0:N],
                        op0=ALU.mult, op1=ALU.mult), reads=[xT.b, modv.b, rstd.b], writes=[tb.b])
                    k.op(k.act, lambda: nc.scalar.activation(out=ob[:, 0:N], in_=tb[:, 0:N], func=AF.Identity,
                                                             bias=modv[:, 1, s_, 0, c:c + 1]),
                         reads=[tb.b, modv.b], writes=[ob.b])
                    k.dma(k.pool, dst[c, :, r0:r0 + N], ob[:, 0:N], ob.b, reads=[ob.b])

            ffn_phase(0, x1T_d, x2T_d, blocks0, post_fn=post_h1)
            phase_barrier()

        if stage >= 4:
            rwkv_phase()
            phase_barrier()

        if stage >= 6:
            def post_final(t0, N, s_, xT, hT, sq, rstd, tmp, sg):
                rms_stats(xT, N, sq, rstd)
                fg = V("final_g", 0, 8)
                for c in range(8):
                    k.op(k.dve, lambda: nc.vector.scalar_tensor_tensor(
                        out=xT[:, c, 0:N], in0=xT[:, c, 0:N], scalar=fg[:, c:c + 1], in1=rstd[:, 0:N],
                        op0=ALU.mult, op1=ALU.mult), reads=[xT.b, vecs.b, rstd.b], writes=[xT.b])
                for tt in range(N // 128):
                    for half in range(2):
                        pt = nxt("C")
                        for cc in range(4):
                            c = half * 4 + cc
                            k.op(k.pe, lambda: nc.tensor.transpose(pt[:, cc * 128:(cc + 1) * 128],
                                                                   xT[:, c, tt * 128:(tt + 1) * 128], ident),
                                 reads=[xT.b, cst.b], writes=[pt.b], inc=(cc == 3))
                        o = sg[half]
                        k.op(k.act, lambda: nc.scalar.copy(out=o[:, :], in_=pt[:, :]), reads=[pt.b], writes=[o.b])
                        r = (t0 - CL) + tt * 128
                        k.dma(k.pool, out_d[r:r + 128, half * 512:(half + 1) * 512], o[:, :], o.b, reads=[o.b])

            ffn_phase(1, x3T_d, None, blocks0[1:], post_fn=post_final)
            phase_barrier()

        dbg_src = {1: xT_d, 2: x1T_d, 3: x2T_d, 5: x3T_d}.get(stage)
        if dbg_src is not None:
            with contextlib.ExitStack() as s1:
                xT = tl("o_xT", [128, 8, 512], F32, s1)
                ot = [tl(f"o_t{i}", [128, D], F32, s1) for i in range(2)]
                for bi in range(L // 512):
                    t0 = CL + bi * 512
                    k.dma(k.sp, xT[:, :, :], dbg_src[:, :, t0:t0 + 512].rearrange("c p t -> p c t"), xT.b,
                          writes=[xT.b])
                    for tt in range(4):
                        o = ot[tt % 2]
                        for half in range(2):
                            pt = nxt("C")
                            for cc in range(4):
                                c = half * 4 + cc
                                k.op(k.pe, lambda: nc.tensor.transpose(pt[:, cc * 128:(cc + 1) * 128],
                                                                       xT[:, c, tt * 128:(tt + 1) * 128], ident),
                                     reads=[xT.b, cst.b], writes=[pt.b], inc=(cc == 3))
                            k.op(k.act, lambda: nc.scalar.copy(out=o[:, half * 512:(half + 1) * 512], in_=pt[:, :]),
                                 reads=[pt.b], writes=[o.b])
                        r = bi * 512 + tt * 128
                        k.dma(k.pool, out_d[r:r + 128, :], o[:, :], o.b, reads=[o.b])
        k.finish()
        print(f"[build] L={L} stage={stage} inst={k.n_inst} waits={k.n_wait}")
    return nc


_CACHE = {}


def prep_inputs(inp, b, L):
    wqkv_p, wo_p = _CACHE.get("w") or host_weights(inp)
    _CACHE["w"] = (wqkv_p, wo_p)
    cm, ct, stb = _CACHE.get(("c", L)) or make_consts(L)
    _CACHE[("c", L)] = (cm, ct, stb)
    f = lambda a: np.ascontiguousarray(np.asarray(a, np.float32))
    return {
        "x": f(inp["x"][b, :L]), "ctx": f(inp["ctx"][b]), "vecs": pack_vecs(inp, b), "consts": cm, "ctab": ct,
        "stab": stb, "mod_w": f(inp["mod_w"]), "wqkv": wqkv_p, "wo0": wo_p, "ffn_wg": f(inp["ffn_wg"]),
        "ffn_wu": f(inp["ffn_wu"]), "ffn_wd": f(inp["ffn_wd"]),
        "wrkv": f(inp["rwkv_wrkv"][0]), "wo1": f(inp["rwkv_wo"][0]),
        "w1c": f(np.concatenate([inp["rwkv_w1"][0][0], inp["rwkv_w1"][0][1]], axis=1)),
        "a1c": f(np.concatenate([inp["rwkv_a1"][0][0], inp["rwkv_a1"][0][1]], axis=1)),
        "g1": f(inp["rwkv_g1"][0]),
        "w2c": f(np.concatenate([inp["rwkv_w2"][0][0], inp["rwkv_w2"][0][1]], axis=0)),
        "a2c": f(np.concatenate([inp["rwkv_a2"][0][0], inp["rwkv_a2"][0][1]], axis=0)),
        "g2": f(inp["rwkv_g2"][0]), "masks": make_masks(),
    }


def kernel(**inputs):
    B, L, _ = inputs["x"].shape
    inp = {kk: np.asarray(v) for kk, v in inputs.items()}
    nc = build(L)
    in_maps = [prep_inputs(inp, b, L) for b in range(B)]
    res = run_bass_kernel_spmd(nc, in_maps, core_ids=list(range(B)))
    return np.stack([np.asarray(r["out"], np.float32) for r in res.results], axis=0)
```
